# Optimizing a Trainium2 kernel written in Bass

```python
import jax, jax.numpy as jnp
from jax import lax
import numpy as np

D_MODEL = 2048
BATCH = 16
SEQ = 2048
DEPTH = 1
DEC_BATCH = 128
DEC_SEQ = 8
PAST_LEN = 16384
PAGE_SIZE = 128

HEAD_DIM = 64
N_HEADS_ATTN = D_MODEL // HEAD_DIM
N_KV_HEADS = max(N_HEADS_ATTN // 8, 1)
GQA = N_HEADS_ATTN // N_KV_HEADS
D_ATTN = N_HEADS_ATTN * HEAD_DIM
D_KV = N_KV_HEADS * HEAD_DIM
WINDOW = 128
SSM_HEAD_DIM = 64
D_SSM = D_MODEL
N_HEADS_SSM = D_SSM // SSM_HEAD_DIM
N_GROUPS = 4
D_STATE = 128
CONV_K = 4
SSM_CHUNK = 128
D_MIX = D_ATTN + D_SSM
CONV_DIM = D_SSM + 2 * N_GROUPS * D_STATE
SPLIT_SIZES = (D_ATTN, D_KV, D_KV, D_ATTN, D_SSM, CONV_DIM, N_HEADS_SSM)
SPLIT_POINTS = tuple(int(s) for s in np.cumsum(SPLIT_SIZES)[:-1])
IN_DIM = int(sum(SPLIT_SIZES))
EPS = 1e-6
DT_MIN = 1e-3
DT_MAX = 1e-1

kernel_name = 'hymba_swa_sink_ssd_step'


def rmsnorm(x, g):
    xf = x.astype(jnp.float32)
    y = xf * lax.rsqrt(jnp.mean(xf * xf, axis=-1, keepdims=True) + EPS)
    return (y * g.astype(jnp.float32)).astype(x.dtype)


def sink_attend(q, k, v, mask, sink):
    s = jnp.einsum('...qkgd,...skd->...kgqs', q, k).astype(jnp.float32) * (HEAD_DIM ** -0.5)
    s = jnp.where(mask, s, -jnp.inf)
    snk = sink.astype(jnp.float32).reshape(N_KV_HEADS, GQA, 1, 1)
    m = jnp.maximum(jnp.max(s, axis=-1, keepdims=True), snk)
    p = jnp.exp(s - m)
    p = p / (jnp.sum(p, axis=-1, keepdims=True) + jnp.exp(snk - m))
    return jnp.einsum('...kgqs,...skd->...qkgd', p.astype(v.dtype), v)


def swa_prompt(q, k, v, sink):
    n, S = q.shape[:2]
    L = WINDOW
    nb = S // L
    qb = q.reshape(n, nb, L, N_KV_HEADS, GQA, HEAD_DIM)
    kb = k.reshape(n, nb, L, N_KV_HEADS, HEAD_DIM)
    vb = v.reshape(n, nb, L, N_KV_HEADS, HEAD_DIM)
    prev = lambda t: jnp.concatenate([jnp.zeros_like(t[:, :1]), t[:, :-1]], axis=1)
    kk = jnp.concatenate([prev(kb), kb], axis=2)
    vv = jnp.concatenate([prev(vb), vb], axis=2)
    rel = jnp.arange(L)[:, None] + L - jnp.arange(2 * L)[None, :]
    has_key = (jnp.arange(nb)[:, None, None] > 0) | (jnp.arange(2 * L)[None, None, :] >= L)
    mask = ((rel >= 0) & (rel < WINDOW))[None] & has_key
    o = sink_attend(qb, kk, vv, mask[:, None, None], sink)
    return o.reshape(n, S, D_ATTN)


def swa_step(q, k, v, kbuf, vbuf, sink):
    n, T = q.shape[:2]
    W = kbuf.shape[1]
    kk = jnp.concatenate([kbuf, k], axis=1)
    vv = jnp.concatenate([vbuf, v], axis=1)
    rel = jnp.arange(T)[:, None] + W - jnp.arange(W + T)[None, :]
    mask = (rel >= 0) & (rel < WINDOW)
    o = sink_attend(q.reshape(n, T, N_KV_HEADS, GQA, HEAD_DIM), kk, vv, mask, sink)
    return o.reshape(n, T, D_ATTN), kk[:, -W:], vv[:, -W:]


def causal_conv(xpad, w, b):
    C = xpad.shape[-1]
    y = lax.conv_general_dilated(xpad, w[:, None, :].astype(xpad.dtype), (1,), 'VALID',
                                 dimension_numbers=('NWC', 'WIO', 'NWC'), feature_group_count=C)
    return y + b.astype(xpad.dtype)


def ssd(x, dt, A, Bm, Cm, h0, chunk):
    n, T, H, P = x.shape
    G, N = Bm.shape[-2:]
    R = H // G
    L = chunk
    nc = T // L
    f32 = jnp.float32
    x = x.astype(f32).reshape(n, nc, L, G, R, P)
    dt = dt.reshape(n, nc, L, G, R)
    Bm = Bm.astype(f32).reshape(n, nc, L, G, N)
    Cm = Cm.astype(f32).reshape(n, nc, L, G, N)
    a = lax.cumsum(dt * A.reshape(G, R), axis=2)
    xdt = x * dt[..., None]
    causal = jnp.tril(jnp.ones((L, L), bool))[:, :, None, None]
    decay = jnp.exp(jnp.where(causal, a[:, :, :, None] - a[:, :, None, :], -jnp.inf))
    cb = jnp.einsum('bclge,bcsge->bclsg', Cm, Bm)
    y_intra = jnp.einsum('bclsg,bclsgr,bcsgrp->bclgrp', cb, decay, xdt)
    decay_end = jnp.exp(a[:, :, -1:] - a)
    s_chunk = jnp.einsum('bclge,bclgr,bclgrp->bcgrpe', Bm, decay_end, xdt)
    chunk_decay = jnp.exp(a[:, :, -1])

    def step(h, inp):
        d, s = inp
        return d[..., None, None] * h + s, h

    h_last, h_prev = lax.scan(step, h0.astype(f32).reshape(n, G, R, P, N),
                              (jnp.moveaxis(chunk_decay, 1, 0), jnp.moveaxis(s_chunk, 1, 0)))
    h_prev = jnp.moveaxis(h_prev, 0, 1)
    y_inter = jnp.einsum('bclge,bcgrpe,bclgr->bclgrp', Cm, h_prev, jnp.exp(a))
    return (y_intra + y_inter).reshape(n, T, H, P), h_last.reshape(n, H, P, N)


def layer(x, kbuf, vbuf, conv_buf, h0, norm_pre, w_in, attn_sink, attn_norm, conv_w, conv_b,
          dt_bias, a_log, d_skip, ssm_norm, w_out, norm_post):
    n, T, _ = x.shape
    f32 = jnp.float32
    h = rmsnorm(x, norm_pre)
    u = jnp.einsum('btd,de->bte', h, w_in)
    q, k, v, g_a, z, xbc, dt = jnp.split(u, SPLIT_POINTS, axis=-1)
    k = k.reshape(n, T, N_KV_HEADS, HEAD_DIM)
    v = v.reshape(n, T, N_KV_HEADS, HEAD_DIM)
    if kbuf is None:
        o_a = swa_prompt(q, k, v, attn_sink)
        new_k, new_v = k[:, -WINDOW:], v[:, -WINDOW:]
    else:
        o_a, new_k, new_v = swa_step(q, k, v, kbuf, vbuf, attn_sink)
    attn_out = rmsnorm(o_a * jax.nn.silu(g_a), attn_norm)
    xpad = jnp.concatenate([conv_buf.astype(xbc.dtype), xbc], axis=1)
    new_conv = xpad[:, -(CONV_K - 1):]
    xbc_c = jax.nn.silu(causal_conv(xpad, conv_w, conv_b))
    xs, Bm, Cm = jnp.split(xbc_c, (D_SSM, D_SSM + N_GROUPS * D_STATE), axis=-1)
    xs = xs.reshape(n, T, N_HEADS_SSM, SSM_HEAD_DIM)
    dt = jax.nn.softplus(dt.astype(f32) + dt_bias.astype(f32))
    A = -jnp.exp(a_log.astype(f32))
    chunk = SSM_CHUNK if T % SSM_CHUNK == 0 else T
    y, h_new = ssd(xs, dt, A, Bm.reshape(n, T, N_GROUPS, D_STATE), Cm.reshape(n, T, N_GROUPS, D_STATE), h0, chunk)
    y = y + d_skip.astype(f32)[:, None] * xs.astype(f32)
    yz = (y.reshape(n, T, D_SSM) * jax.nn.silu(z.astype(f32))).reshape(n, T, N_GROUPS, D_SSM // N_GROUPS)
    ssm_out = rmsnorm(yz, ssm_norm.reshape(N_GROUPS, D_SSM // N_GROUPS)).reshape(n, T, D_SSM).astype(x.dtype)
    mix = jnp.concatenate([attn_out, ssm_out], axis=-1)
    out = jnp.einsum('bte,ed->btd', mix, w_out)
    return x + rmsnorm(out, norm_post), new_k, new_v, new_conv, h_new.astype(h0.dtype)


def setup_inputs(seed: int = 0) -> dict:
    key = jax.random.key(seed)
    ks = jax.random.split(key, 20)
    f32 = jnp.float32
    W_CACHE = min(WINDOW, PAST_LEN)
    nrm = lambda k, s: jax.random.normal(k, s, f32)
    dt0 = jnp.exp(jax.random.uniform(ks[14], (DEPTH, N_HEADS_SSM), f32, np.log(DT_MIN), np.log(DT_MAX)))
    return {
        'x_prompt': nrm(ks[0], (BATCH, SEQ, D_MODEL)),
        'x_sample': nrm(ks[1], (DEC_BATCH, DEC_SEQ, D_MODEL)),
        'cache_k': nrm(ks[2], (DEPTH, DEC_BATCH, W_CACHE, N_KV_HEADS, HEAD_DIM)),
        'cache_v': nrm(ks[3], (DEPTH, DEC_BATCH, W_CACHE, N_KV_HEADS, HEAD_DIM)),
        'state_conv': nrm(ks[4], (DEPTH, DEC_BATCH, CONV_K - 1, CONV_DIM)),
        'state_ssm': 0.5 * nrm(ks[5], (DEPTH, DEC_BATCH, N_HEADS_SSM, SSM_HEAD_DIM, D_STATE)),
        'norm_pre': 1.0 + 0.05 * nrm(ks[6], (DEPTH, D_MODEL)),
        'w_in': nrm(ks[7], (DEPTH, D_MODEL, IN_DIM)) * D_MODEL ** -0.5,
        'attn_sink': nrm(ks[8], (DEPTH, N_HEADS_ATTN)),
        'attn_norm': 1.0 + 0.05 * nrm(ks[9], (DEPTH, D_ATTN)),
        'conv_w': nrm(ks[10], (DEPTH, CONV_K, CONV_DIM)) * CONV_K ** -0.5,
        'conv_b': 0.02 * nrm(ks[11], (DEPTH, CONV_DIM)),
        'dt_bias': dt0 + jnp.log(-jnp.expm1(-dt0)),
        'a_log': jnp.log(jax.random.uniform(ks[12], (DEPTH, N_HEADS_SSM), f32, 1.0, 16.0)),
        'd_skip': 1.0 + 0.1 * nrm(ks[13], (DEPTH, N_HEADS_SSM)),
        'ssm_norm': 1.0 + 0.05 * nrm(ks[15], (DEPTH, D_SSM)),
        'w_out': nrm(ks[16], (DEPTH, D_MIX, D_MODEL)) * D_MIX ** -0.5,
        'norm_post': 1.0 + 0.05 * nrm(ks[17], (DEPTH, D_MODEL)),
    }


def reference(x_prompt, x_sample, cache_k, cache_v, state_conv, state_ssm, norm_pre, w_in, attn_sink,
              attn_norm, conv_w, conv_b, dt_bias, a_log, d_skip, ssm_norm, w_out, norm_post):
    yp, ys = x_prompt, x_sample
    kp_l, vp_l, cp_l, hp_l, ks_l, vs_l, cs_l, hs_l = [], [], [], [], [], [], [], []
    for l in range(DEPTH):
        params = (norm_pre[l], w_in[l], attn_sink[l], attn_norm[l], conv_w[l], conv_b[l],
                  dt_bias[l], a_log[l], d_skip[l], ssm_norm[l], w_out[l], norm_post[l])
        nb = yp.shape[0]
        conv0 = jnp.zeros((nb, CONV_K - 1, CONV_DIM), yp.dtype)
        h00 = jnp.zeros((nb, N_HEADS_SSM, SSM_HEAD_DIM, D_STATE), state_ssm.dtype)
        yp, kp, vp, cp, hp = layer(yp, None, None, conv0, h00, *params)
        ys, ksm, vsm, csm, hsm = layer(ys, cache_k[l], cache_v[l], state_conv[l], state_ssm[l], *params)
        kp_l.append(kp); vp_l.append(vp); cp_l.append(cp); hp_l.append(hp)
        ks_l.append(ksm); vs_l.append(vsm); cs_l.append(csm); hs_l.append(hsm)
    return (yp, ys, jnp.stack(kp_l), jnp.stack(vp_l), jnp.stack(cp_l), jnp.stack(hp_l),
            jnp.stack(ks_l), jnp.stack(vs_l), jnp.stack(cs_l), jnp.stack(hs_l))
```

```python
import numpy as np
from contextlib import ExitStack
import concourse.bass as bass
import concourse.mybir as mybir
from concourse.bass_utils import run_bass_kernel_spmd

F32 = mybir.dt.float32
BF16 = mybir.dt.bfloat16
ALU = mybir.AluOpType
AF = mybir.ActivationFunctionType
AX = mybir.AxisListType

D = 2048
IN_DIM = 9760
C_Q, C_K, C_V, C_GA, C_Z, C_X, C_B, C_C, C_DT = 0, 2048, 2304, 2560, 4608, 6656, 8704, 9216, 9728
EPS = 1e-6
NCORES = 8
NSLOT = 3
DRAIN_N = 1


class Prog:
    ENG = ('pe', 'act', 'dve', 'pool', 'sp')

    def __init__(self):
        self.ops = []
        self.last_w = {}
        self.readers = {}

    def begin_capture(self):
        self._cap = []

    def end_capture(self):
        lst, self._cap = self._cap, None
        return lst

    def op(self, eng, fn, reads=(), writes=(), dma=None):
        if getattr(self, '_cap', None) is not None:
            self._cap.append((eng, fn, tuple(reads), tuple(writes), dma))
            return None
        i = len(self.ops)
        deps = set()
        for b in reads:
            w = self.last_w.get(b)
            if w is not None:
                deps.add(w)
        for b in writes:
            w = self.last_w.get(b)
            if w is not None:
                deps.add(w)
            for r in self.readers.get(b, ()):
                deps.add(r)
        for b in reads:
            lst = self.readers.setdefault(b, [])
            if dma is None:
                lst[:] = [r for r in lst if not (self.ops[r]['dma'] is None and self.ops[r]['eng'] == eng)]
            lst.append(i)
        for b in writes:
            self.last_w[b] = i
            self.readers[b] = []
        self.ops.append(dict(eng=eng, fn=fn, deps=deps, dma=dma, sig=False, sem=None, val=0))
        return i

    @staticmethod
    def _skip(dep, op):
        return dep['dma'] is None and dep['eng'] == op['eng'] and op['eng'] == 'pe'

    def schedule(self):
        ops = self.ops
        for op in ops:
            for d in op['deps']:
                dep = ops[d]
                if self._skip(dep, op):
                    continue
                if dep['dma'] is None:
                    dep['sig'] = True
        cnt = {e: 0 for e in self.ENG}
        dmacnt = {}
        for op in ops:
            if op['dma'] is not None:
                dmacnt[op['dma']] = dmacnt.get(op['dma'], 0) + 16
                op['sem'] = ('dma', op['dma'])
                op['val'] = dmacnt[op['dma']]
            elif op['sig']:
                cnt[op['eng']] += 1
                op['sem'] = ('eng', op['eng'])
                op['val'] = cnt[op['eng']]
        known = {e: {} for e in self.ENG}
        streams = {e: [] for e in self.ENG}
        snaps = {}
        nwait = 0
        for op in ops:
            e = op['eng']
            need = {}
            for d in op['deps']:
                dep = ops[d]
                if self._skip(dep, op):
                    continue
                key, val = dep['sem'], dep['val']
                if known[e].get(key, 0) >= val:
                    continue
                need[key] = max(need.get(key, 0), val)
            for key, val in sorted(need.items(), key=lambda kv: str(kv[0])):
                if known[e].get(key, 0) >= val:
                    continue
                streams[e].append(('wait', key, val))
                nwait += 1
                known[e][key] = val
                for k2, v2 in snaps[(key, val)].items():
                    if known[e].get(k2, 0) < v2:
                        known[e][k2] = v2
            streams[e].append(('op', op))
            if op['sem'] is not None:
                s = dict(known[e])
                if op['dma'] is None:
                    s[op['sem']] = op['val']
                snaps[(op['sem'], op['val'])] = s
        self.streams = streams
        self.sem_keys = sorted({op['sem'] for op in ops if op['sem'] is not None}, key=str)
        self.stats = dict(nops=len(ops), nwait=nwait, cnt=cnt, dmacnt=dmacnt)
        return streams

    def emit(self, nc, stack):
        streams = self.schedule()
        sems = {}
        for k in self.sem_keys:
            sems[k] = stack.enter_context(nc.semaphore("s_%s_%s" % (k[0], k[1])))
        block = stack.enter_context(nc.Block())

        def replay(name, eng):
            for rec in streams[name]:
                if rec[0] == 'wait':
                    eng.wait_ge(sems[rec[1]], rec[2])
                else:
                    op = rec[1]
                    ins = op['fn'](eng)
                    if op['dma'] is not None:
                        ins.then_inc(sems[op['sem']], 16)
                    elif op['sig']:
                        ins.then_inc(sems[op['sem']], 1)

        @block.tensor
        def _(eng):
            replay('pe', eng)

        @block.scalar
        def _(eng):
            replay('act', eng)

        @block.vector
        def _(eng):
            replay('dve', eng)

        @block.gpsimd
        def _(eng):
            replay('pool', eng)

        @block.sync
        def _(eng):
            replay('sp', eng)


def host_consts():
    c = np.zeros((128, 5 * 128 + 16), np.float32)
    i = np.arange(128)
    c[:, 0:128] = np.eye(128)
    c[:, 128:256] = (i[:, None] <= i[None, :])
    c[:, 256:384] = 1.0
    same = (i[:, None] // 8) == (i[None, :] // 8)
    c[:, 384:512] = same & (i[:, None] <= i[None, :])
    c[:, 512:640] = same
    c[:, 640:656] = (i[:, None] // 8) == np.arange(16)[None, :]
    return c


def build(NP, SEQ, SAMPLE=True):
    NG = SEQ // 512
    nc = bass.Bass("TRN2", target_bir_lowering=False)
    din = lambda name, shape: nc.dram_tensor(name, list(shape), F32, kind="ExternalInput").ap()
    dout = lambda name, shape: nc.dram_tensor(name, list(shape), F32, kind="ExternalOutput").ap()
    xp = din("xp", [NP * SEQ, D])
    w_in = din("w_in", [D, IN_DIM])
    w_out = din("w_out", [2 * D, D])
    norm_pre = din("norm_pre", [D])
    attn_sink = din("attn_sink", [32])
    attn_norm = din("attn_norm", [D])
    conv_w = din("conv_w", [4, 3072])
    conv_b = din("conv_b", [3072])
    dt_bias = din("dt_bias", [32])
    a_log = din("a_log", [32])
    d_skip = din("d_skip", [32])
    ssm_norm = din("ssm_norm", [D])
    norm_post = din("norm_post", [D])
    consts = din("consts", [128, 656])
    yp = dout("yp", [NP * SEQ, D])
    kp = dout("kp", [NP, 128, 256])
    vp = dout("vp", [NP, 128, 256])
    cp = dout("cp", [NP, 3, 3072])
    hp = dout("hp", [NP, 32, 64, 128])
    if SAMPLE:
        xsm = din("xsm", [128, D])
        ck = din("ck", [16, 128, 256])
        cv = din("cv", [16, 128, 256])
        sconv = din("sconv", [16, 3, 3072])
        sssm = din("sssm", [16, 32, 64, 128])
        ysm = dout("ysm", [128, D])
        ksm = dout("ksm", [16, 128, 256])
        vsm = dout("vsm", [16, 128, 256])
        csm = dout("csm", [16, 3, 3072])
        hsm = dout("hsm", [16, 32, 64, 128])
        vscr = nc.dram_tensor("vscr", [128, 256], F32, kind="Internal").ap()

    P = Prog()
    st = ExitStack()
    sb = lambda name, shape, dt: st.enter_context(nc.sbuf_tensor(name, list(shape), dt))
    cf = sb("cf", [128, 656], F32)
    ident_f, triu_f, ones_f = cf[:, 0:128], cf[:, 128:256], cf[:, 256:384]
    identb = sb("identb", [128, 128], BF16)
    maskb = sb("maskb", [128, 2, 128], BF16)
    onespad = sb("onespad", [128, 2, 128], BF16)
    onesb = sb("onesb", [128, 128], BF16)
    triub_blk = sb("triub_blk", [128, 128], BF16)
    onespad_f = sb("onespad_f", [32, 2, 128], F32)
    gpre = sb("gpre", [128, 16], F32)
    anorm = sb("anorm", [128, 16], F32)
    snorm = sb("snorm", [128, 16], F32)
    convw = sb("convw", [128, 4, 24], F32)
    convb = sb("convb", [128, 24], F32)
    Dcol = sb("Dcol", [128, 16], F32)
    Ecol = sb("Ecol", [128, 16], F32)
    dtb_b = sb("dtb_b", [128, 32], F32)
    A_b = sb("A_b", [128, 32], F32)
    epsc = sb("epsc", [128, 1], F32)
    onec = sb("onec", [128, 1], F32)
    hT = sb("hT", [128, 16, 512], BF16)
    mixT = sb("mixT", [128, 32, 512], BF16)
    wsl = [sb("w%d" % i, [128, 16, 512], BF16) for i in range(NSLOT)]
    xt = sb("xt", [128, D], F32)
    stt = sb("stt", [128, 8], F32)
    fscr = sb("fscr", [128, 1], F32)
    KT = sb("KT", [128, 4, 640], BF16)
    Vpad = sb("Vpad", [128, 5, 4, 2, 128], BF16)
    hstate = sb("hstate", [128, D], F32)
    hpad = sb("hpad", [128, 32, 128], BF16)
    hist = sb("hist", [128, 24, 3], F32)
    dt_all = sb("dt_all", [128, 4, 32], F32)
    dtA_all = sb("dtA_all", [128, 4, 32], F32)
    a_all = sb("a_all", [128, 4, 32], F32)
    w_all = sb("w_all", [128, 4, 32], F32)
    cd_all = sb("cd_all", [128, 4, 32], F32)
    aT_all = sb("aT_all", [32, 4, 128], F32)
    ssq = sb("ssq", [128, 4, 4], F32)
    RTW = 15360
    RT = sb("RT", [128, RTW], F32)

    def rt32(off, n):
        return RT[:, off:off + n]

    def rt16(off, n):
        return RT[:, off:off + n].bitcast(BF16)

    QTu = rt16(0, 1024).rearrange("p (t c q) -> p t c q", t=4, c=4)
    gs = rt16(1024, 1024).rearrange("p (c t) -> p c t", c=4)
    Eb = rt16(2048, 1024).rearrange("p (e t) -> p e t", e=4)
    EbB = rt16(4992, 1024).rearrange("p (e t) -> p e t", e=4)
    Ebs = [Eb, EbB]
    rstd_a = rt32(11136, 512)
    QTuB = rt16(6016, 1024).rearrange("p (t c q) -> p t c q", t=4, c=4)
    gsB = rt16(7040, 1024).rearrange("p (c t) -> p c t", c=4)
    rd = rt32(3072, 512)
    o32 = rt32(3584, 512)
    sqa = rt16(4096, 256)
    vstage = rt32(4352, 256)
    kstage = rt32(4608, 256)
    dtmp = rt32(4864, 128)
    xpre = rt32(0, 3090).rearrange("p (c t) -> p c t", c=6)
    ctmp = rt32(3090, 1024).rearrange("p (e t) -> p e t", e=2)
    xc = rt16(4114, 1024).rearrange("p (c t) -> p c t", c=4)
    BCt = rt16(5138, 512).rearrange("p (c t) -> p c t", c=2)
    zs = rt16(5650, 1024).rearrange("p (c t) -> p c t", c=4)
    dm = rt32(6674, 1024).rearrange("p (h l) -> p h l", h=8)
    Eb2 = rt16(7698, 512).rearrange("p (h l) -> p h l", h=8)
    MT = rt16(8210, 512).rearrange("p (h l) -> p h l", h=8)
    EA = rt16(8722, 512).rearrange("p (h l) -> p h l", h=8)
    CTs = rt16(9234, 512).rearrange("p (h l) -> p h l", h=8)
    xdtp = rt16(9746, 512).rearrange("p (c e q) -> p c e q", c=4, e=2)
    xw = rt16(10258, 256)
    Btok = rt16(10514, 64)
    cbm = rt16(10578, 64)
    yd = rt32(10642, 512)
    rg = rt32(11154, 128)
    sqs = rt16(11282, 256)
    Rm = rt32(11538, 128 * 1)
    hstage = rt32(6674, 512).rearrange("p (c n) -> p c n", c=4)
    cstage = rt32(3090, 768).rearrange("p (c n) -> p c n", c=6)
    Kc = rt16(4992, 1024).rearrange("p (b e d) -> p b e d", b=16, e=2)
    KTc = rt16(6016, 1024).rearrange("p (b s) -> p b s", b=16)
    Vc_pad = rt16(7040, 2048).rearrange("p (b e q) -> p b e q", b=16, e=2)
    Vn_pad = rt16(9088, 2048).rearrange("p (b e q) -> p b e q", b=16, e=2)
    xpre_s = rt32(0, 1056).rearrange("p (c b t) -> p c b t", c=6, b=16)
    cst_in = rt32(1056, 768).rearrange("p (c n) -> p c n", c=6)
    cst_o = rt32(1824, 288).rearrange("p (c n) -> p c n", c=6)
    cstage_s = rt32(3090, 768).rearrange("p (c n) -> p c n", c=6)
    h0f = rt32(11776, 1024).rearrange("p (b n) -> p b n", b=8)
    hps = rt16(12800, 1024).rearrange("p (b e q) -> p b e q", b=8, e=2)
    h0b = rt16(13824, 512).rearrange("p (b n) -> p b n", b=8)
    Bm = rt16(14336, 1024).rearrange("p (b n) -> p b n", b=16)
    zsB = rt16(11776, 1024).rearrange("p (c t) -> p c t", c=4)
    xcB = rt16(12800, 1024).rearrange("p (c t) -> p c t", c=4)
    BCtB = rt16(13824, 512).rearrange("p (c t) -> p c t", c=2)
    cur = dict(QTu=QTu, gs=gs, zs=zs, xc=xc, BCt=BCt, sfx='')
    outacc = rt32(0, 8192).rearrange("p (t c) -> p t c", t=4)
    ojunk = rt16(8192, 256)
    xsb = rt16(8448, 1024)
    gpost_b = rt32(9472, 2048)

    psb = [st.enter_context(nc.psum_tensor("ps%d" % i, [128, 512], F32)) for i in range(8)]

    def ps32(i):
        return psb[i][:, :]

    def ps16(i):
        return psb[i][:, :].bitcast(BF16)

    op = P.op
    PSN = ['ps%d' % i for i in range(8)]

    pend = []

    def defer(tag, fn):
        P.begin_capture()
        fn()
        pend.append((tag, P.end_capture()))

    def drain(k=None, older_than=None):
        n = 0
        while pend and (k is None or n < k):
            if older_than is not None and pend[0][0] >= older_than:
                break
            tag, lst = pend.pop(0)
            for rec in lst:
                P.op(*rec)
            n += 1

    def fence(tag):
        drain()
        op('dve', lambda e: e.memset(fscr[:, :], 0.0), reads=[], writes=['RT'])

    def bcast_dram(vec, n, parts=128):
        return bass.AP(tensor=vec.tensor, offset=0, ap=[[0, parts], [1, n]])

    dma_ctr = [0]

    def dma(q, out, in_, r, w, sem, **kw):
        op(q, lambda e: e.dma_start(out=out, in_=in_, **kw), reads=r, writes=w, dma=sem)

    dma('sp', cf[:, :], consts, [], ['cf'], 'c0')
    dma('sp', gpre[:, :], norm_pre.rearrange("(k p) -> p k", p=128), [], ['gpre'], 'c1', allow_slow_non_contiguous=True)
    dma('sp', anorm[:, :], attn_norm.rearrange("(k p) -> p k", p=128), [], ['anorm'], 'c2', allow_slow_non_contiguous=True)
    dma('sp', snorm[:, :], ssm_norm.rearrange("(k p) -> p k", p=128), [], ['snorm'], 'c3', allow_slow_non_contiguous=True)
    for k in range(4):
        dma('sp', convw[:, k, :], conv_w[k].rearrange("(c p) -> p c", p=128), [], ['convw'], 'c4', allow_slow_non_contiguous=True)
    dma('sp', convb[:, :], conv_b.rearrange("(c p) -> p c", p=128), [], ['convb'], 'c5', allow_slow_non_contiguous=True)
    for h2 in range(2):
        dma('sp', Dcol[64 * h2:64 * h2 + 64, :], bass.AP(tensor=d_skip.tensor, offset=h2, ap=[[0, 64], [2, 16]]),
            [], ['Dcol'], 'c6', allow_slow_non_contiguous=True)
        dma('sp', Ecol[64 * h2:64 * h2 + 64, :], bass.AP(tensor=attn_sink.tensor, offset=h2, ap=[[0, 64], [2, 16]]),
            [], ['Ecol'], 'c7', allow_slow_non_contiguous=True)
    dma('sp', dtb_b[:, :], bcast_dram(dt_bias, 32), [], ['dtb_b'], 'c8')
    dma('sp', A_b[:, :], bcast_dram(a_log, 32), [], ['A_b'], 'c9')
    op('dve', lambda e: e.tensor_copy(out=identb[:, :], in_=ident_f), reads=['cf'], writes=['identb'])
    op('dve', lambda e: e.tensor_copy(out=maskb[:, 1, :], in_=triu_f), reads=['cf'], writes=['maskb'])
    op('dve', lambda e: e.tensor_scalar(out=maskb[:, 0, :], in0=triu_f, scalar1=-1.0, scalar2=1.0, op0=ALU.mult, op1=ALU.add),
       reads=['cf'], writes=['maskb'])
    op('dve', lambda e: e.tensor_copy(out=onesb[:, :], in_=ones_f), reads=['cf'], writes=['onesb'])
    op('pool', lambda e: e.memset(onespad[:, :, :], 0.0), writes=['onespad'])
    op('dve', lambda e: e.tensor_copy(out=triub_blk[:, :], in_=cf[:, 384:512]), reads=['cf'], writes=['maskb'])
    op('pool', lambda e: e.memset(onespad_f[:, :, :], 0.0), writes=['onespad_f'])
    op('pool', lambda e: e.memset(onespad_f[:, 0, 0:64], 1.0), writes=['onespad_f'])
    op('pool', lambda e: e.memset(onespad_f[:, 1, 64:128], 1.0), writes=['onespad_f'])
    op('pool', lambda e: e.memset(onespad[:, 0, 0:64], 1.0), reads=[], writes=['onespad'])
    op('pool', lambda e: e.memset(onespad[:, 1, 64:128], 1.0), reads=[], writes=['onespad'])
    op('pool', lambda e: e.memset(epsc[:, :], EPS), writes=['epsc'])
    op('pool', lambda e: e.memset(onec[:, :], 1.0), writes=['onec'])
    op('pool', lambda e: e.memset(Vpad[:, :, :, :, :], 0.0), writes=['Vpad%d' % i for i in range(5)])
    op('pool', lambda e: e.memset(hpad[:, :, :], 0.0), writes=['hpad%d' % i for i in range(4)])
    op('act', lambda e: e.activation(out=Ecol[:, :], in_=Ecol[:, :], func=AF.Exp), reads=['Ecol'], writes=['Ecol'])
    op('act', lambda e: e.activation(out=A_b[:, :], in_=A_b[:, :], func=AF.Exp), reads=['A_b'], writes=['A_b'])
    op('dve', lambda e: e.tensor_scalar(out=A_b[:, :], in0=A_b[:, :], scalar1=-1.0, scalar2=None, op0=ALU.mult),
       reads=['A_b'], writes=['A_b'])

    tasks = []
    grp_x = []

    def wload(cols, slot, base=0, src=None):
        src = w_in if src is None else src
        w = wsl[slot]
        c0, n = cols
        dma('pool', w[:, :, base:base + n], src[:, c0:c0 + n].rearrange("(k p) c -> p k c", p=128),
            [], ['w%d' % slot], 'wl%d' % slot)

    def wload_rows(r0, c0, slot):
        w = wsl[slot]
        dma('pool', w[:, :, :], w_out[r0:r0 + 2048, c0:c0 + 512].rearrange("(k p) c -> p k c", p=128),
            [], ['w%d' % slot], 'wl%d' % slot)

    pbank = [0]

    def inproj_chunk(slot, wc, ntok, evac):
        bi = pbank[0] % 2
        pbank[0] += 1
        w = wsl[slot]
        for k in range(16):
            op('pe', lambda e, k=k: e.matmul(ps32(bi)[:, 0:ntok], lhsT=w[:, k, wc * 128:(wc + 1) * 128], rhs=hT[:, k, 0:ntok],
                                              start=(k == 0), stop=(k == 15)),
               reads=['w%d' % slot, 'hT'], writes=[PSN[bi]])
        evac(ps32(bi)[:, 0:ntok], PSN[bi])
        drain(DRAIN_N)

    def phase0(xsrc, NT, tiles=None):
        for tt in (range(NT) if tiles is None else tiles):
            dma('sp', xt[:, :], xsrc[tt * 128:(tt + 1) * 128, :], [], ['xt'], 'xld')
            op('act', lambda e: e.activation(out=xsb[:, :], in_=xt[:, :], func=AF.Square, accum_out=stt[:, 0:1]),
               reads=['xt', 'RT'], writes=['xsb', 'stt'])
            op('act', lambda e: e.activation(out=stt[:, 1:2], in_=stt[:, 0:1], func=AF.Ln, scale=1.0 / D, bias=epsc[:, :]),
               reads=['stt', 'epsc'], writes=['stt'])
            op('act', lambda e: e.activation(out=stt[:, 2:3], in_=stt[:, 1:2], func=AF.Exp, scale=-0.5),
               reads=['stt'], writes=['stt'])
            op('dve', lambda e: e.tensor_scalar(out=xsb[:, :], in0=xt[:, :], scalar1=stt[:, 2:3], scalar2=None, op0=ALU.mult),
               reads=['xt', 'stt', 'RT'], writes=['xsb'])
            for half in range(2):
                bi = 6 + half
                pv = ps16(bi).rearrange("p (k t) -> p k t", k=8)
                for k8 in range(8):
                    k = half * 8 + k8
                    op('pe', lambda e, k=k, k8=k8, pv=pv: e.transpose(out=pv[:, k8, :], in_=xsb[:, k * 128:(k + 1) * 128], identity=identb[:, :]),
                       reads=['xsb', 'identb', 'RT'], writes=[PSN[bi]])
                op('dve', lambda e, half=half, tt=tt, pv=pv: e.tensor_tensor(
                    out=hT[:, half * 8:half * 8 + 8, tt * 128:(tt + 1) * 128], in0=pv,
                    in1=gpre[:, half * 8:half * 8 + 8].unsqueeze(2).broadcast_to([128, 8, 128]), op=ALU.mult),
                   reads=[PSN[bi], 'gpre'], writes=['hT'])

    def phase1_tile(slot, tt, NT, first_tile, last_tile_of_seq, seq_b, tri=None, blk=None, sample=False):
        tri = triu_f if tri is None else tri
        blk = ones_f if blk is None else blk
        w = wsl[slot]
        bi = pbank[0] % 2
        pbank[0] += 1
        for k in range(16):
            op('pe', lambda e, k=k: e.matmul(ps32(bi)[:, 0:288], lhsT=hT[:, k, tt * 128:(tt + 1) * 128], rhs=w[:, k, 0:288],
                                              start=(k == 0), stop=(k == 15)),
               reads=['w%d' % slot, 'hT'], writes=[PSN[bi]])
        import os as _os
        lvl = int(_os.environ.get("PH1", "99"))
        if lvl < 1:
            return
        pv = ps32(bi)
        vsrc = pv[:, 0:256].rearrange("p (j d) -> p j d", j=4)
        if not sample:
            op('act', lambda e: e.activation(out=Vpad[:, tt + 1, :, 0, 0:64], in_=vsrc, func=AF.Identity),
               reads=[PSN[bi]], writes=['Vpad%d' % (tt + 1)])
            op('dve', lambda e: e.tensor_copy(out=Vpad[:, tt + 1, :, 1, 64:128], in_=vsrc),
               reads=[PSN[bi]], writes=['Vpad%d' % (tt + 1)])
        else:
            op('dve', lambda e: e.tensor_copy(out=vstage, in_=pv[:, 0:256]), reads=[PSN[bi], 'RT'], writes=['vstage'])
            if not _os.environ.get('NOVSCR'):
                dma('sp', vscr, vstage, ['vstage', 'RT'], ['vscr'], 'vscr')
            for b in range(16):
                if _os.environ.get('NOVROWS'):
                    break
                dma('sp', vsm[b, 120:128, :], vstage[8 * b:8 * b + 8, :], ['vstage', 'RT'], [], 'so0')
        if last_tile_of_seq and not _os.environ.get('NOVST'):
            op('dve', lambda e: e.tensor_copy(out=vstage, in_=pv[:, 0:256]), reads=[PSN[bi], 'RT'], writes=['vstage'])
            if not _os.environ.get('NOVDMA'):
                dma('sp', vp[seq_b], vstage, ['vstage', 'RT'], [], 'vst')
        if lvl < 2:
            return
        op('dve', lambda e: e.tensor_tensor(out=dt_all[:, tt, :], in0=pv[:, 256:288], in1=dtb_b[:, :], op=ALU.add),
           reads=[PSN[bi], 'dtb_b'], writes=['dt%d' % tt])
        op('act', lambda e: e.activation(out=dt_all[:, tt, :], in_=dt_all[:, tt, :], func=AF.Exp), reads=['dt%d' % tt], writes=['dt%d' % tt])
        op('act', lambda e: e.activation(out=dt_all[:, tt, :], in_=dt_all[:, tt, :], func=AF.Ln, bias=onec[:, :]),
           reads=['dt%d' % tt, 'onec'], writes=['dt%d' % tt])
        op('dve', lambda e: e.tensor_tensor(out=dtA_all[:, tt, :], in0=dt_all[:, tt, :], in1=A_b[:, :], op=ALU.mult),
           reads=['dt%d' % tt, 'A_b'], writes=['dtA%d' % tt])
        if lvl < 3:
            return
        p2 = ps32(2)
        op('pe', lambda e: e.matmul(p2[:, 0:32], lhsT=tri, rhs=dtA_all[:, tt, :], start=True, stop=True),
           reads=['cf', 'dtA%d' % tt], writes=['ps2'])
        op('pe', lambda e: e.matmul(p2[:, 32:64], lhsT=blk, rhs=dtA_all[:, tt, :], start=True, stop=True),
           reads=['cf', 'dtA%d' % tt], writes=['ps2'])
        op('pe', lambda e: e.matmul(p2[0:32, 64:192], lhsT=dtA_all[:, tt, :], rhs=tri, start=True, stop=True),
           reads=['cf', 'dtA%d' % tt], writes=['ps2'])
        if lvl < 4:
            return
        op('act', lambda e: e.activation(out=a_all[:, tt, :], in_=p2[:, 0:32], func=AF.Identity), reads=['ps2'], writes=['a%d' % tt])
        op('act', lambda e: e.activation(out=aT_all[:, tt, :], in_=p2[0:32, 64:192], func=AF.Identity), reads=['ps2'], writes=['aT%d' % tt])
        op('act', lambda e: e.activation(out=cd_all[:, tt, :], in_=p2[:, 32:64], func=AF.Exp), reads=['ps2'], writes=['cd%d' % tt])
        op('dve', lambda e: e.tensor_tensor(out=w_all[:, tt, :], in0=p2[:, 32:64], in1=a_all[:, tt, :], op=ALU.subtract),
           reads=['ps2', 'a%d' % tt], writes=['w%d_' % tt])
        op('act', lambda e: e.activation(out=w_all[:, tt, :], in_=w_all[:, tt, :], func=AF.Exp), reads=['w%d_' % tt], writes=['w%d_' % tt])
        op('dve', lambda e: e.tensor_tensor(out=w_all[:, tt, :], in0=w_all[:, tt, :], in1=dt_all[:, tt, :], op=ALU.mult),
           reads=['w%d_' % tt, 'dt%d' % tt], writes=['w%d_' % tt])

    def attn_front(j, tt, has_prev):
        EB = Ebs[tt % 2]
        en = 'Eb%d_' % (tt % 2)
        Q_, sfx = cur['QTu'], cur['sfx']
        kbs = [0, 1] if has_prev else [1]
        for h2 in range(2):
            for kb in kbs:
                e_i = h2 * 2 + kb
                bi = (2, 3, 7, 2)[e_i]
                op('pe', lambda e, h2=h2, kb=kb, bi=bi: e.matmul(
                    ps32(bi), lhsT=KT[64 * h2:64 * h2 + 64, j, (tt + kb) * 128:(tt + kb + 1) * 128],
                    rhs=Q_[64 * h2:64 * h2 + 64, tt, :, :], start=True, stop=True),
                   reads=['KT%d' % j, 'QTu' + sfx, 'RT'], writes=[PSN[bi]])
                op('act', lambda e, e_i=e_i, bi=bi: e.activation(out=EB[:, e_i, :], in_=ps32(bi), func=AF.Exp, scale=0.125),
                   reads=[PSN[bi], 'RT'], writes=[en + str(e_i)])
                op('dve', lambda e, e_i=e_i, kb=kb: e.tensor_tensor(
                    out=EB[:, e_i, :].rearrange("p (c q) -> p c q", c=4), in0=EB[:, e_i, :].rearrange("p (c q) -> p c q", c=4),
                    in1=maskb[:, kb, :].unsqueeze(1).broadcast_to([128, 4, 128]), op=ALU.mult),
                   reads=[en + str(e_i), 'maskb', 'RT'], writes=[en + str(e_i)])

    def attn_back(j, tt, has_prev):
        EB = Ebs[tt % 2]
        en = 'Eb%d_' % (tt % 2)
        G_, sfx = cur['gs'], cur['sfx']
        kbs = [0, 1] if has_prev else [1]
        lst = [(h2, kb) for h2 in range(2) for kb in kbs]
        for idx, (h2, kb) in enumerate(lst):
            op('pe', lambda e, h2=h2, kb=kb, idx=idx: e.matmul(
                ps32(4), lhsT=Vpad[:, tt + kb, j, h2, :], rhs=EB[:, h2 * 2 + kb, :], start=(idx == 0), stop=(idx == len(lst) - 1)),
               reads=['Vpad%d' % (tt + kb), en + str(h2 * 2 + kb), 'RT'], writes=['ps4'])
        for idx, (h2, kb) in enumerate(lst):
            op('pe', lambda e, h2=h2, kb=kb, idx=idx: e.matmul(
                ps32(5), lhsT=onespad[:, h2, :], rhs=EB[:, h2 * 2 + kb, :], start=(idx == 0), stop=(idx == len(lst) - 1)),
               reads=['onespad', en + str(h2 * 2 + kb), 'RT'], writes=['ps5'])
        for c in range(4):
            op('dve', lambda e, c=c: e.tensor_scalar(out=rd[:, c * 128:(c + 1) * 128], in0=ps32(5)[:, c * 128:(c + 1) * 128],
                                                      scalar1=Ecol[:, 4 * j + c:4 * j + c + 1], scalar2=None, op0=ALU.add),
               reads=['ps5', 'Ecol', 'RT'], writes=['rd'])
        op('act', lambda e: e.activation(out=rd, in_=rd, func=AF.Ln), reads=['rd', 'RT'], writes=['rd'])
        op('act', lambda e: e.activation(out=rd, in_=rd, func=AF.Exp, scale=-1.0), reads=['rd', 'RT'], writes=['rd'])
        op('dve', lambda e: e.tensor_tensor(out=o32, in0=ps32(4), in1=rd, op=ALU.mult), reads=['ps4', 'rd', 'RT'], writes=['o32'])
        mv = mixT[:, 4 * j:4 * j + 4, tt * 128:(tt + 1) * 128]
        op('dve', lambda e: e.tensor_tensor(out=mv, in0=o32.rearrange("p (c q) -> p c q", c=4),
                                             in1=G_[:, :, tt * 128:(tt + 1) * 128], op=ALU.mult),
           reads=['o32', 'gs' + sfx, 'RT'], writes=['mixA%d' % tt])
        op('act', lambda e: e.activation(out=sqa.rearrange("p (c q) -> p c q", c=4), in_=mv, func=AF.Square),
           reads=['mixA%d' % tt, 'RT'], writes=['sqa'])

    def attn_stats(j, tt, first):
        for c in range(4):
            op('pe', lambda e, c=c: e.matmul(ps32(6)[:, tt * 128:(tt + 1) * 128], lhsT=onesb[:, :], rhs=sqa[:, c * 128:(c + 1) * 128],
                                              start=(first and c == 0), stop=(j == 3 and c == 3), skip_group_check=True),
               reads=['onesb', 'sqa', 'RT'], writes=['ps6'])

    def attn_unit(j, NT, first_group):
        hp_ = lambda t: not (first_group and t == 0)
        defer(j, lambda: attn_front(j, 0, hp_(0)))
        for tt in range(NT):
            def piece(tt=tt):
                if tt + 1 < NT:
                    attn_front(j, tt + 1, hp_(tt + 1))
                if tt > 0:
                    attn_stats(j, tt - 1, j == 0 and tt - 1 == 0)
                attn_back(j, tt, hp_(tt))
            defer(j, piece)
        defer(j, lambda: attn_stats(j, NT - 1, j == 0 and NT - 1 == 0))

    def attn_finish(NT):
        n = NT * 128
        op('act', lambda e: e.activation(out=rstd_a[:, 0:n], in_=ps32(6)[:, 0:n], func=AF.Ln, scale=1.0 / D, bias=epsc[:, :]),
           reads=['ps6', 'epsc', 'RT'], writes=['rstd_a'])
        op('act', lambda e: e.activation(out=rstd_a[:, 0:n], in_=rstd_a[:, 0:n], func=AF.Exp, scale=-0.5),
           reads=['rstd_a', 'RT'], writes=['rstd_a'])
        for c16 in range(16):
            eng = 'dve'
            op(eng, lambda e, c16=c16: e.scalar_tensor_tensor(out=mixT[:, c16, 0:n], in0=mixT[:, c16, 0:n], scalar=anorm[:, c16:c16 + 1],
                                                              in1=rstd_a[:, 0:n], op0=ALU.mult, op1=ALU.mult),
               reads=['rstd_a', 'anorm', 'RT'] + ['mixA%d' % t for t in range(NT)], writes=['mixA%d' % t for t in range(NT)])

    def conv_group(g, NT, seq_b, last_group):
        n = NT * 128
        for ci in range(6):
            ch = (4 * g + ci) if ci < 4 else (16 + g if ci == 4 else 20 + g)
            op('act', lambda e, ci=ci, ch=ch: e.activation(out=xpre[:, ci, 0:3], in_=hist[:, ch, :], func=AF.Identity),
               reads=['hist%d' % ch, 'RT'], writes=['xpre%d' % ci])
        chof = lambda ci: (4 * g + ci) if ci < 4 else (16 + g if ci == 4 else 20 + g)

        def tap0(ci):
            ch = chof(ci)
            acc = ctmp[:, ci % 2, 0:n]
            op('act', lambda e: e.activation(out=acc, in_=xpre[:, ci, 0:n], func=AF.Identity, scale=convw[:, 0, ch:ch + 1]),
               reads=['xpre%d' % ci, 'convw', 'RT'], writes=['ctmp%d' % (ci % 2)])
        tap0(0)
        tap0(1)
        for ci in range(6):
            ch = (4 * g + ci) if ci < 4 else (16 + g if ci == 4 else 20 + g)
            eng = 'dve'
            acc = ctmp[:, ci % 2, 0:n]
            an = 'ctmp%d' % (ci % 2)
            for k in range(1, 4):
                op(eng, lambda e, ci=ci, ch=ch, k=k, acc=acc: e.scalar_tensor_tensor(
                    out=acc, in0=xpre[:, ci, k:k + n], scalar=convw[:, k, ch:ch + 1], in1=acc, op0=ALU.mult, op1=ALU.add),
                   reads=['xpre%d' % ci, 'convw', an, 'RT'], writes=[an])
            dst = cur['xc'][:, ci, 0:n] if ci < 4 else cur['BCt'][:, ci - 4, 0:n]
            dn = (('xc%d' % ci) if ci < 4 else ('BCt%d' % (ci - 4))) + cur['sfx']
            op('act', lambda e, ch=ch, acc=acc, dst=dst: e.activation(out=dst, in_=acc, func=AF.Silu, bias=convb[:, ch:ch + 1]),
               reads=[an, 'convb', 'RT'], writes=[dn])
            op('act', lambda e, ci=ci, ch=ch: e.activation(out=hist[:, ch, :], in_=xpre[:, ci, n:n + 3], func=AF.Identity),
               reads=['xpre%d' % ci, 'RT'], writes=['hist%d' % ch])
            if ci + 2 < 6:
                tap0(ci + 2)
        if last_group:
            p5 = ps32(5)[0:3, :].rearrange("p (c n) -> p c n", c=4)
            p7 = ps32(7)[0:3, 0:256].rearrange("p (c n) -> p c n", c=2)
            for ci in range(6):
                tgt = p5[:, ci, :] if ci < 4 else p7[:, ci - 4, :]
                bn = 'ps5' if ci < 4 else 'ps7'
                op('pe', lambda e, ci=ci, tgt=tgt: e.transpose(out=tgt, in_=xpre[:, ci, n:n + 3], identity=ident_f),
                   reads=['xpre%d' % ci, 'cf', 'RT'], writes=[bn])
            op('act', lambda e: e.activation(out=cstage[0:3, 0:4, :], in_=p5, func=AF.Identity), reads=['ps5', 'ctmp0', 'ctmp1', 'RT'],
               writes=['ctmp0', 'ctmp1', 'cstage'])
            op('act', lambda e: e.activation(out=cstage[0:3, 4:6, :], in_=p7, func=AF.Identity), reads=['ps7', 'RT'], writes=['cstage2'])
            dma('sp', cp[seq_b, :, 512 * g:512 * g + 512].rearrange("r (c n) -> r c n", c=4), cstage[0:3, 0:4, :], ['cstage', 'ctmp0', 'ctmp1', 'RT'], [], 'cst')
            dma('sp', cp[seq_b, :, 2048 + 128 * g:2048 + 128 * g + 128], cstage[0:3, 4, :], ['cstage2', 'ctmp0', 'ctmp1', 'RT'], [], 'cst')
            dma('sp', cp[seq_b, :, 2560 + 128 * g:2560 + 128 * g + 128], cstage[0:3, 5, :], ['cstage2', 'ctmp0', 'ctmp1', 'RT'], [], 'cst')

    def ssd_tile(g, tt, first_tile, mask_ap=None, sample_hook=None):
        ssd_prep(g, tt, mask_ap)
        ssd_mid(g, tt, first_tile, sample_hook)
        ssd_back(g, tt)

    def ssd_prep(g, tt, mask_ap=None):
        tsl = slice(tt * 128, (tt + 1) * 128)
        mk = maskb[:, 1, :] if mask_ap is None else mask_ap
        xc, BCt, sfx = cur['xc'], cur['BCt'], cur['sfx']
        op('pe', lambda e: e.matmul(ps32(4)[:, 0:128], lhsT=BCt[:, 0, tsl], rhs=BCt[:, 1, tsl], start=True, stop=True),
           reads=['BCt0' + sfx, 'BCt1' + sfx, 'RT'], writes=['ps4'])
        op('dve', lambda e: e.tensor_tensor(out=cbm, in0=ps32(4)[:, 0:128], in1=mk, op=ALU.mult), reads=['ps4', 'maskb', 'RT'], writes=['cbm'])
        Rv = dm[0:32, :, :]
        op('dve', lambda e: e.tensor_tensor(out=Rv, in0=aT_all[:, tt, :].unsqueeze(1).broadcast_to([32, 8, 128]),
                                            in1=ident_f[0:32, 8 * g:8 * g + 8].unsqueeze(2).broadcast_to([32, 8, 128]), op=ALU.mult),
           reads=['aT%d' % tt, 'cf', 'RT'], writes=['dm'])
        for hf in range(2):
            op('pe', lambda e, hf=hf: e.matmul(ps32(2 + hf), lhsT=ones_f[0:32, :], rhs=dm[0:32, 4 * hf:4 * hf + 4, :], start=True, stop=True),
               reads=['cf', 'dm', 'RT'], writes=[PSN[2 + hf]])
        for hf in range(2):
            op('act', lambda e, hf=hf: e.activation(out=EA[:, 4 * hf:4 * hf + 4, :], in_=ps32(2 + hf).rearrange("p (h l) -> p h l", h=4), func=AF.Exp),
               reads=[PSN[2 + hf], 'RT'], writes=['EA'])
        op('dve', lambda e: e.tensor_tensor(out=CTs, in0=EA, in1=BCt[:, 1, tsl].unsqueeze(1).broadcast_to([128, 8, 128]), op=ALU.mult),
           reads=['EA', 'BCt1' + sfx, 'RT'], writes=['CTs'])
        for hf in range(2):
            op('dve', lambda e, hf=hf: e.tensor_tensor(
                out=dm[:, 4 * hf:4 * hf + 4, :], in0=ps32(2 + hf).rearrange("p (h l) -> p h l", h=4),
                in1=a_all[:, tt, 8 * g + 4 * hf:8 * g + 4 * hf + 4].unsqueeze(2).broadcast_to([128, 4, 128]), op=ALU.subtract),
               reads=[PSN[2 + hf], 'a%d' % tt, 'RT'], writes=['dm'])
        op('dve', lambda e: e.tensor_scalar(out=dm, in0=dm, scalar1=0.0, scalar2=None, op0=ALU.min), reads=['dm', 'RT'], writes=['dm'])
        op('act', lambda e: e.activation(out=Eb2, in_=dm, func=AF.Exp), reads=['dm', 'RT'], writes=['Eb2'])
        op('dve', lambda e: e.tensor_tensor(out=MT, in0=Eb2, in1=cbm.unsqueeze(1).broadcast_to([128, 8, 128]), op=ALU.mult),
           reads=['Eb2', 'cbm', 'RT'], writes=['MT'])
        pT = ps16(5).rearrange("p (c q) -> p c q", c=8)
        for c in range(4):
            op('pe', lambda e, c=c: e.transpose(out=pT[:, c, :], in_=xc[:, c, tsl], identity=identb[:, :]),
               reads=['xc%d' % c + sfx, 'identb', 'RT'], writes=['ps5'])
        op('pe', lambda e: e.transpose(out=pT[:, 4, :], in_=BCt[:, 0, tsl], identity=identb[:, :]), reads=['BCt0' + sfx, 'identb', 'RT'], writes=['ps5'])
        for h2 in range(2):
            hs = slice(64 * h2, 64 * h2 + 64)
            dsl = bass.AP(tensor=dt_all[:, :, :].tensor, offset=dt_all[:, tt, 8 * g + h2:8 * g + h2 + 1].offset, ap=[list(dt_all[:, :, :].ap[0]), [2, 4], [0, 64]])
            wsl_ = bass.AP(tensor=w_all[:, :, :].tensor, offset=w_all[:, tt, 8 * g + h2:8 * g + h2 + 1].offset, ap=[list(w_all[:, :, :].ap[0]), [2, 4], [0, 64]])
            op('dve', lambda e, h2=h2, hs=hs, dsl=dsl: e.tensor_tensor(out=xdtp[:, :, h2, hs], in0=pT[:, 0:4, hs], in1=dsl, op=ALU.mult),
               reads=['ps5', 'dt%d' % tt, 'RT'], writes=['xdtp'])
            op('dve', lambda e, h2=h2, hs=hs, wsl_=wsl_: e.tensor_tensor(out=xw.rearrange("p (c q) -> p c q", c=4)[:, :, hs], in0=pT[:, 0:4, hs], in1=wsl_, op=ALU.mult),
               reads=['ps5', 'w%d_' % tt, 'RT'], writes=['xw'])
        op('act', lambda e: e.activation(out=Btok, in_=pT[:, 4, :], func=AF.Identity), reads=['ps5', 'RT'], writes=['Btok'])

    def ssd_mid(g, tt, first_tile, sample_hook=None):
        tsl = slice(tt * 128, (tt + 1) * 128)
        Y = ps32(6).rearrange("p (c l) -> p c l", c=4)
        for c in range(4):
            seq = []
            for h2 in range(2):
                seq.append(('intra', h2))
                if not first_tile:
                    seq.append(('inter', h2))
            for idx, (kind, h2) in enumerate(seq):
                hl = 2 * c + h2
                if kind == 'intra':
                    op('pe', lambda e, c=c, h2=h2, hl=hl, idx=idx, ns=len(seq): e.matmul(
                        Y[:, c, :], lhsT=xdtp[:, c, h2, :], rhs=MT[:, hl, :], start=(idx == 0 and (sample_hook is None or c == 0)), stop=(idx == ns - 1),
                        skip_group_check=(sample_hook is not None)),
                       reads=['xdtp', 'MT', 'RT'], writes=['ps6'])
                else:
                    op('pe', lambda e, c=c, h2=h2, hl=hl, idx=idx, ns=len(seq): e.matmul(
                        Y[:, c, :], lhsT=hpad[:, 8 * g + hl, :], rhs=CTs[:, hl, :], start=(idx == 0), stop=(idx == ns - 1)),
                       reads=['hpad%d' % g, 'CTs', 'RT'], writes=['ps6'])
        if sample_hook is not None:
            sample_hook(Y)
        if sample_hook is None:
            op('pe', lambda e: e.matmul(ps32(7), lhsT=Btok, rhs=xw, start=True, stop=True), reads=['Btok', 'xw', 'RT'], writes=['ps7'])
        hsv = hstate[:, 512 * g:512 * g + 512]
        if sample_hook is not None:
            pass
        elif first_tile:
            op('act', lambda e: e.activation(out=hsv, in_=ps32(7), func=AF.Identity), reads=['ps7'], writes=['hst%d' % g])
        else:
            op('dve', lambda e: e.tensor_tensor(out=hsv.rearrange("p (h q) -> p h q", h=8), in0=hsv.rearrange("p (h q) -> p h q", h=8),
                                                 in1=cd_all[:, tt, 8 * g:8 * g + 8].unsqueeze(2).broadcast_to([128, 8, 64]), op=ALU.mult),
               reads=['hst%d' % g, 'cd%d' % tt], writes=['hst%d' % g])
            op('dve', lambda e: e.tensor_tensor(out=hsv, in0=hsv, in1=ps32(7), op=ALU.add), reads=['hst%d' % g, 'ps7'], writes=['hst%d' % g])
        hv = hstate[:, 512 * g:512 * g + 512].rearrange("p (c e q) -> p c e q", c=4, e=2)
        hpv = hpad[:, 8 * g:8 * g + 8, :].rearrange("p (c e) q -> p c e q", c=4)
        for h2 in range(2):
            if sample_hook is not None:
                break
            hs = slice(64 * h2, 64 * h2 + 64)
            op('act', lambda e, h2=h2, hs=hs: e.activation(out=hpv[:, :, h2, hs], in_=hv[:, :, h2, :], func=AF.Identity),
               reads=['hst%d' % g], writes=['hpad%d' % g])

    def ssd_back(g, tt):
        tsl = slice(tt * 128, (tt + 1) * 128)
        Y = ps32(6).rearrange("p (c l) -> p c l", c=4)
        ydv = yd.rearrange("p (c l) -> p c l", c=4)
        xc, zs, sfx = cur['xc'], cur['zs'], cur['sfx']
        for c in range(4):
            op('act', lambda e, c=c: e.activation(out=ydv[:, c, :], in_=xc[:, c, tsl], func=AF.Identity, scale=Dcol[:, 4 * g + c:4 * g + c + 1]),
               reads=['xc%d' % c + sfx, 'Dcol', 'RT'], writes=['yd'])
        op('dve', lambda e: e.tensor_tensor(out=ydv, in0=ydv, in1=Y, op=ALU.add), reads=['yd', 'ps6', 'RT'], writes=['yd'])
        op('dve', lambda e: e.tensor_tensor(out=ydv, in0=ydv, in1=zs[:, :, tsl], op=ALU.mult), reads=['yd', 'zs' + sfx, 'RT'], writes=['yd'])
        op('act', lambda e: e.activation(out=sqs, in_=yd, func=AF.Square), reads=['yd', 'RT'], writes=['sqs'])
        for c in range(4):
            op('pe', lambda e, c=c: e.matmul(ps32(4)[:, 128:256], lhsT=onesb[:, :], rhs=sqs[:, c * 128:(c + 1) * 128], start=(c == 0), stop=(c == 3)),
               reads=['onesb', 'sqs', 'RT'], writes=['ps4'])
        op('act', lambda e: e.activation(out=rg, in_=ps32(4)[:, 128:256], func=AF.Ln, scale=1.0 / 512, bias=epsc[:, :]), reads=['ps4', 'epsc', 'RT'], writes=['rg'])
        op('act', lambda e: e.activation(out=rg, in_=rg, func=AF.Exp, scale=-0.5), reads=['rg', 'RT'], writes=['rg'])
        op('dve', lambda e: e.tensor_tensor(out=ydv, in0=ydv, in1=rg.unsqueeze(1).broadcast_to([128, 4, 128]), op=ALU.mult), reads=['yd', 'rg', 'RT'], writes=['yd'])
        for c in range(4):
            op('act', lambda e, c=c: e.activation(out=mixT[:, 16 + 4 * g + c, tsl], in_=ydv[:, c, :], func=AF.Identity, scale=snorm[:, 4 * g + c:4 * g + c + 1]),
               reads=['yd', 'snorm', 'RT'], writes=['mixS%d' % tt])

    def ssm_out(g, dst):
        pv = ps32(5).rearrange("p (c n) -> p c n", c=4)
        for c in range(4):
            op('pe', lambda e, c=c: e.transpose(out=pv[:, c, :], in_=hstate[:, 512 * g + 128 * c:512 * g + 128 * c + 128], identity=ident_f),
               reads=['hst%d' % g, 'cf'], writes=['ps5'])
        op('act', lambda e: e.activation(out=hstage, in_=pv, func=AF.Identity), reads=['ps5', 'dm', 'RT'], writes=['dm', 'hstage'])
        dma('sp', dst[8 * g:8 * g + 8].rearrange("(c e) p n -> (e p) c n", e=2), hstage, ['hstage', 'dm', 'RT'], [], 'hst_o')

    def post_tile(tt, xsrc, ydst):
        dma('sp', xt[:, :], xsrc[tt * 128:(tt + 1) * 128, :], [], ['xt'], 'xld')
        op('dve', lambda e: e.tensor_reduce(out=stt[:, 4:5], in_=ssq[:, tt, :], axis=AX.X, op=ALU.add), reads=['ssq%d' % tt], writes=['stt'])
        op('act', lambda e: e.activation(out=stt[:, 5:6], in_=stt[:, 4:5], func=AF.Ln, scale=1.0 / D, bias=epsc[:, :]), reads=['stt', 'epsc'], writes=['stt'])
        op('act', lambda e: e.activation(out=stt[:, 6:7], in_=stt[:, 5:6], func=AF.Exp, scale=-0.5), reads=['stt'], writes=['stt'])
        op('dve', lambda e: e.scalar_tensor_tensor(out=outacc[:, tt, :], in0=outacc[:, tt, :], scalar=stt[:, 6:7], in1=gpost_b[:, :], op0=ALU.mult, op1=ALU.mult),
           reads=['oacc%d' % tt, 'stt', 'gpost_b', 'RT'], writes=['oacc%d' % tt])
        op('dve', lambda e: e.tensor_tensor(out=outacc[:, tt, :], in0=outacc[:, tt, :], in1=xt[:, :], op=ALU.add),
           reads=['oacc%d' % tt, 'xt', 'RT'], writes=['oacc%d' % tt])
        dma('sp', ydst[tt * 128:(tt + 1) * 128, :], outacc[:, tt, :], ['oacc%d' % tt, 'RT'], [], 'yst%d' % tt)


    def sample_attn(j):
        for dup in range(2):
            dma('pool', Kc[:, :, dup, :], ck[:, :, 64 * j:64 * j + 64].rearrange("b s d -> s b d"), ['RT'], ['Kc'], 'sk0')
        dma('pool', Vc_pad[:, :, 0, 0:64], cv[:, :, 64 * j:64 * j + 64].rearrange("b s d -> s b d"), ['RT'], ['Vc_pad'], 'sk1')
        dma('pool', Vc_pad[:, :, 1, 64:128], cv[:, :, 64 * j:64 * j + 64].rearrange("b s d -> s b d"), ['RT'], ['Vc_pad'], 'sk1')
        dma('pool', Vn_pad[0:8, :, 0, 0:64], vscr[:, 64 * j:64 * j + 64].rearrange("(b t) d -> t b d", t=8), ['RT', 'vscr'], ['Vn_pad'], 'sk2')
        dma('pool', Vn_pad[0:8, :, 1, 64:128], vscr[:, 64 * j:64 * j + 64].rearrange("(b t) d -> t b d", t=8), ['RT', 'vscr'], ['Vn_pad'], 'sk2')
        for half in range(2):
            bi = 2 + half
            pv = ps16(bi).rearrange("p (b s) -> p b s", b=8)
            for b8 in range(8):
                b = half * 8 + b8
                op('pe', lambda e, b=b, b8=b8, pv=pv: e.transpose(out=pv[:, b8, :], in_=Kc[:, b, :, :].rearrange("p e d -> p (e d)"), identity=identb[:, :]),
                   reads=['Kc', 'identb', 'RT'], writes=[PSN[bi]])
            op('act', lambda e, half=half, pv=pv: e.activation(out=KTc[:, half * 8:half * 8 + 8, :], in_=pv, func=AF.Identity),
               reads=[PSN[bi], 'RT'], writes=['KTc'])
        for b in range(16):
            for h2 in range(2):
                hs = slice(64 * h2, 64 * h2 + 64)
                qv = QTu[hs, 0, :, 8 * b:8 * b + 8]
                op('pe', lambda e, b=b, h2=h2, hs=hs, qv=qv: e.matmul(ps32(2 + h2)[:, 32 * b:32 * b + 32], lhsT=KTc[hs, b, :], rhs=qv, start=True, stop=True),
                   reads=['KTc', 'QTu', 'RT'], writes=[PSN[2 + h2]])
                op('pe', lambda e, b=b, h2=h2, hs=hs, qv=qv: e.matmul(ps32(4 + h2)[0:8, 32 * b:32 * b + 32], lhsT=KT[hs, j, 128 + 8 * b:128 + 8 * b + 8], rhs=qv, start=True, stop=True),
                   reads=['KT%d' % j, 'QTu', 'RT'], writes=[PSN[4 + h2]])
        for h2 in range(2):
            op('act', lambda e, h2=h2: e.activation(out=Eb[:, 2 * h2, :], in_=ps32(2 + h2), func=AF.Exp, scale=0.125),
               reads=[PSN[2 + h2], 'RT'], writes=['Eb%d' % (2 * h2)])
            op('act', lambda e, h2=h2: e.activation(out=Eb[0:8, 2 * h2 + 1, :], in_=ps32(4 + h2)[0:8, :], func=AF.Exp, scale=0.125),
               reads=[PSN[4 + h2], 'RT'], writes=['Eb%d' % (2 * h2 + 1)])
            op('dve', lambda e, h2=h2: e.tensor_tensor(out=Eb[:, 2 * h2, :].rearrange("p (g t) -> p g t", t=8), in0=Eb[:, 2 * h2, :].rearrange("p (g t) -> p g t", t=8),
                                                       in1=maskb[:, 0, 0:8].unsqueeze(1).broadcast_to([128, 64, 8]), op=ALU.mult),
               reads=['Eb%d' % (2 * h2), 'maskb', 'RT'], writes=['Eb%d' % (2 * h2)])
            op('dve', lambda e, h2=h2: e.tensor_tensor(out=Eb[0:8, 2 * h2 + 1, :].rearrange("p (g t) -> p g t", t=8), in0=Eb[0:8, 2 * h2 + 1, :].rearrange("p (g t) -> p g t", t=8),
                                                        in1=maskb[0:8, 1, 0:8].unsqueeze(1).broadcast_to([8, 64, 8]), op=ALU.mult),
               reads=['Eb%d' % (2 * h2 + 1), 'maskb', 'RT'], writes=['Eb%d' % (2 * h2 + 1)])
        for (bank, vp_, vn_, nm) in ((7, Vc_pad, Vn_pad, 'pv'), (2, None, None, 'den')):
            for b in range(16):
                cs = slice(32 * b, 32 * b + 32)
                idx = 0
                for h2 in range(2):
                    for kb in range(2):
                        if nm == 'pv':
                            lh = vp_[:, b, h2, :] if kb == 0 else vn_[0:8, b, h2, :]
                        else:
                            lh = onespad[:, h2, :] if kb == 0 else onespad[0:8, h2, :]
                        rh = Eb[:, 2 * h2, cs] if kb == 0 else Eb[0:8, 2 * h2 + 1, cs]
                        op('pe', lambda e, bank=bank, cs=cs, lh=lh, rh=rh, idx=idx: e.matmul(ps32(bank)[:, cs], lhsT=lh, rhs=rh, start=(idx == 0), stop=(idx == 3)),
                           reads=['Vc_pad', 'Vn_pad', 'onespad', 'Eb%d' % (2 * h2 + kb), 'RT'], writes=[PSN[bank]])
                        idx += 1
        rdv = rd.rearrange("p (b c t) -> p b c t", b=16, c=4)
        dnv = ps32(2).rearrange("p (b c t) -> p b c t", b=16, c=4)
        for c in range(4):
            op('dve', lambda e, c=c: e.tensor_scalar(out=rdv[:, :, c, :], in0=dnv[:, :, c, :], scalar1=Ecol[:, 4 * j + c:4 * j + c + 1], scalar2=None, op0=ALU.add),
               reads=['ps2', 'Ecol', 'RT'], writes=['rd'])
        op('act', lambda e: e.activation(out=rd, in_=rd, func=AF.Ln), reads=['rd', 'RT'], writes=['rd'])
        op('act', lambda e: e.activation(out=rd, in_=rd, func=AF.Exp, scale=-1.0), reads=['rd', 'RT'], writes=['rd'])
        op('dve', lambda e: e.tensor_tensor(out=o32, in0=ps32(7), in1=rd, op=ALU.mult), reads=['ps7', 'rd', 'RT'], writes=['o32'])
        mv = mixT[:, 4 * j:4 * j + 4, 0:128]
        op('dve', lambda e: e.tensor_tensor(out=mv.rearrange("p c (b t) -> p b c t", t=8), in0=o32.rearrange("p (b c t) -> p b c t", b=16, c=4),
                                             in1=gs[:, :, 0:128].rearrange("p c (b t) -> p b c t", t=8), op=ALU.mult),
           reads=['o32', 'gs', 'RT'], writes=['mixA0'])
        op('act', lambda e: e.activation(out=sqa.rearrange("p (c q) -> p c q", c=4), in_=mv, func=AF.Square), reads=['mixA0', 'RT'], writes=['sqa'])
        for c in range(4):
            op('pe', lambda e, c=c: e.matmul(ps32(6)[:, 0:128], lhsT=onesb[:, :], rhs=sqa[:, c * 128:(c + 1) * 128], start=(j == 0 and c == 0), stop=(j == 3 and c == 3), skip_group_check=True),
               reads=['onesb', 'sqa', 'RT'], writes=['ps6'])

    def sample_conv(g):
        n = 128
        scv = sconv.rearrange("b k c -> (b k) c")
        dma('sp', cst_in[0:48, 0:4, :], scv[:, 512 * g:512 * g + 512].rearrange("r (c n) -> r c n", c=4), ['RT'], ['cst_in'], 'sk3')
        dma('sp', cst_in[0:48, 4, :], scv[:, 2048 + 128 * g:2048 + 128 * g + 128], ['RT'], ['cst_in'], 'sk3')
        dma('sp', cst_in[0:48, 5, :], scv[:, 2560 + 128 * g:2560 + 128 * g + 128], ['RT'], ['cst_in'], 'sk3')
        ph = ps32(5)[:, 0:288].rearrange("p (c r) -> p c r", c=6)
        for ci in range(6):
            op('pe', lambda e, ci=ci: e.transpose(out=ph[:, ci, :], in_=cst_in[0:48, ci, :], identity=ident_f[0:48, 0:48]),
               reads=['cst_in', 'cf', 'RT'], writes=['ps5'])
        op('act', lambda e: e.activation(out=xpre_s[:, :, :, 0:3], in_=ph.rearrange("p c (b k) -> p c b k", k=3), func=AF.Identity),
           reads=['ps5', 'RT'], writes=['xpre%d' % ci for ci in range(6)])
        for ci in range(6):
            ch = (4 * g + ci) if ci < 4 else (16 + g if ci == 4 else 20 + g)
            acc = ctmp[:, ci % 2, 0:n].rearrange("p (b t) -> p b t", t=8)
            an = 'ctmp%d' % (ci % 2)
            op('dve', lambda e, ci=ci, ch=ch, acc=acc: e.tensor_scalar(out=acc, in0=xpre_s[:, ci, :, 0:8], scalar1=convw[:, 0, ch:ch + 1], scalar2=None, op0=ALU.mult),
               reads=['xpre%d' % ci, 'convw', 'RT'], writes=[an])
            for k in range(1, 4):
                op('dve', lambda e, ci=ci, ch=ch, k=k, acc=acc: e.scalar_tensor_tensor(
                    out=acc, in0=xpre_s[:, ci, :, k:k + 8], scalar=convw[:, k, ch:ch + 1], in1=acc, op0=ALU.mult, op1=ALU.add),
                   reads=['xpre%d' % ci, 'convw', an, 'RT'], writes=[an])
            dst = xc[:, ci, 0:n] if ci < 4 else BCt[:, ci - 4, 0:n]
            dn = ('xc%d' % ci) if ci < 4 else ('BCt%d' % (ci - 4))
            op('act', lambda e, ch=ch, ci=ci, dst=dst: e.activation(out=dst, in_=ctmp[:, ci % 2, 0:n], func=AF.Silu, bias=convb[:, ch:ch + 1]),
               reads=[an, 'convb', 'RT'], writes=[dn])
        op('act', lambda e: e.activation(out=cst_o.rearrange("p c (b k) -> p c b k", k=3), in_=xpre_s[:, :, :, 8:11], func=AF.Identity),
           reads=['xpre%d' % ci for ci in range(6)] + ['RT'], writes=['cst_o'])
        p5 = ps32(5)[0:48, :].rearrange("p (c n) -> p c n", c=4)
        p7 = ps32(7)[0:48, 0:256].rearrange("p (c n) -> p c n", c=2)
        for ci in range(6):
            tgt = p5[:, ci, :] if ci < 4 else p7[:, ci - 4, :]
            bn = 'ps5' if ci < 4 else 'ps7'
            op('pe', lambda e, ci=ci, tgt=tgt: e.transpose(out=tgt, in_=cst_o[:, ci, :], identity=ident_f), reads=['cst_o', 'cf', 'RT'], writes=[bn])
        op('act', lambda e: e.activation(out=cstage[0:48, 0:4, :], in_=p5, func=AF.Identity), reads=['ps5', 'ctmp0', 'ctmp1', 'RT'], writes=['ctmp0', 'ctmp1', 'cstage'])
        op('act', lambda e: e.activation(out=cstage[0:48, 4:6, :], in_=p7, func=AF.Identity), reads=['ps7', 'RT'], writes=['cstage2'])
        cso = csm.rearrange("b k c -> (b k) c")
        dma('sp', cso[:, 512 * g:512 * g + 512].rearrange("r (c n) -> r c n", c=4), cstage[0:48, 0:4, :], ['cstage', 'ctmp0', 'ctmp1', 'RT'], [], 'so1')
        dma('sp', cso[:, 2048 + 128 * g:2048 + 128 * g + 128], cstage[0:48, 4, :], ['cstage2', 'ctmp0', 'ctmp1', 'RT'], [], 'so1')
        dma('sp', cso[:, 2560 + 128 * g:2560 + 128 * g + 128], cstage[0:48, 5, :], ['cstage2', 'ctmp0', 'ctmp1', 'RT'], [], 'so1')

    def sample_ssd_hook(g):
        def hook(Y):
            op('dve', lambda e: e.tensor_tensor(out=Bm, in0=Btok.unsqueeze(1).broadcast_to([128, 16, 128]),
                                                in1=cf[:, 640:656].unsqueeze(2).broadcast_to([128, 16, 128]), op=ALU.mult),
               reads=['Btok', 'cf', 'RT'], writes=['Bm'])
            aTl = aT_all[:, 0, :].rearrange("p (b t) -> p b t", t=8)[:, :, 7]
            for c in range(4):
                cg = 4 * g + c
                R2 = Rm[0:32, 0:32].rearrange("p (e b) -> p e b", e=2)
                op('dve', lambda e, cg=cg, R2=R2: e.tensor_tensor(out=R2, in0=aTl.unsqueeze(1).broadcast_to([32, 2, 16]),
                                                                  in1=ident_f[0:32, 2 * cg:2 * cg + 2].unsqueeze(2).broadcast_to([32, 2, 16]), op=ALU.mult),
                   reads=['aT0', 'cf', 'RT'], writes=['Rm'])
                for h2 in range(2):
                    op('pe', lambda e, h2=h2, R2=R2: e.matmul(ps32(4)[:, 256:272], lhsT=onespad_f[:, h2, :], rhs=R2[:, h2, :], start=(h2 == 0), stop=(h2 == 1)),
                       reads=['onespad_f', 'Rm', 'RT'], writes=['ps4'])
                op('act', lambda e: e.activation(out=Rm[:, 64:80], in_=ps32(4)[:, 256:272], func=AF.Exp), reads=['ps4', 'RT'], writes=['cdT'])
                for hb in range(2):
                    bs = slice(8 * hb, 8 * hb + 8)
                    src = sssm[bs, 2 * cg:2 * cg + 2].rearrange("b e p n -> (e p) b n")
                    dma('sp', h0f, src, ['RT'], ['h0f'], 'sk4')
                    dma('pool', h0b, src, ['RT'], ['h0b'], 'sk5')
                    pv = ps16(3).rearrange("p (b q) -> p b q", b=8)
                    for b8 in range(8):
                        op('pe', lambda e, b8=b8, pv=pv: e.transpose(out=pv[:, b8, :], in_=h0b[:, b8, :], identity=identb[:, :]),
                           reads=['h0b', 'identb', 'RT'], writes=['ps3'])
                    op('act', lambda e, pv=pv: e.activation(out=hps[:, :, 0, 0:64], in_=pv[:, :, 0:64], func=AF.Identity), reads=['ps3', 'RT'], writes=['hps'])
                    op('dve', lambda e, pv=pv: e.tensor_copy(out=hps[:, :, 1, 64:128], in_=pv[:, :, 64:128]), reads=['ps3', 'RT'], writes=['hps'])
                    for b8 in range(8):
                        b = 8 * hb + b8
                        for h2 in range(2):
                            op('pe', lambda e, c=c, b=b, b8=b8, h2=h2: e.matmul(Y[:, c, 8 * b:8 * b + 8], lhsT=hps[:, b8, h2, :], rhs=CTs[:, 2 * c + h2, 8 * b:8 * b + 8],
                                                                             start=False, stop=(h2 == 1), skip_group_check=True),
                               reads=['hps', 'CTs', 'RT'], writes=['ps6'])
                    for q in range(2):
                        op('pe', lambda e, c=c, hb=hb, q=q: e.matmul(ps32(q), lhsT=xw[:, c * 128:(c + 1) * 128],
                                                                    rhs=Bm[:, 8 * hb + 4 * q:8 * hb + 4 * q + 4, :], start=True, stop=True),
                           reads=['xw', 'Bm', 'RT'], writes=[PSN[q]])
                    op('dve', lambda e, hb=hb: e.tensor_tensor(out=h0f, in0=h0f, in1=Rm[:, 64 + 8 * hb:64 + 8 * hb + 8].unsqueeze(2).broadcast_to([128, 8, 128]), op=ALU.mult),
                       reads=['h0f', 'cdT', 'RT'], writes=['h0f'])
                    for q in range(2):
                        op('dve', lambda e, q=q: e.tensor_tensor(out=h0f[:, 4 * q:4 * q + 4, :], in0=h0f[:, 4 * q:4 * q + 4, :],
                                                                 in1=ps32(q).rearrange("p (b n) -> p b n", b=4), op=ALU.add),
                           reads=['h0f', PSN[q], 'RT'], writes=['h0f'])
                    dma('sp', hsm[bs, 2 * cg:2 * cg + 2].rearrange("b e p n -> (e p) b n"), h0f, ['h0f', 'RT'], [], 'so2')
        return hook

    def add_sample_group():
        NT = 1
        n = 128
        gi = len(grp_x)
        grp_x.append((xsm, NT))

        def ld_kd(slot):
            for j in range(4):
                for dup in range(2):
                    wload((C_K + 64 * j, 64), slot, j * 128 + dup * 64)

        def cp_kd(slot):
            fence('A')
            cur.update(QTu=QTu, gs=gs, zs=zs, xc=xc, BCt=BCt, sfx='')
            op('dve', lambda e: e.memset(Vc_pad, 0.0), reads=['RT'], writes=['Vc_pad'])
            op('dve', lambda e: e.memset(Vn_pad, 0.0), reads=['RT'], writes=['Vn_pad'])
            for b in range(16):
                dma('sp', ksm[b, 0:120, :], ck[b, 8:128, :], [], [], 'so3')
                dma('sp', vsm[b, 0:120, :], cv[b, 8:128, :], [], [], 'so3')
            if gi == 0:
                phase0(xsm, NT)
            for j in range(4):
                inproj_chunk(slot, j, n, lambda pa, bn, j=j: op('act', lambda e: e.activation(out=KT[:, j, 128:128 + n], in_=pa, func=AF.Identity),
                                                              reads=[bn], writes=['KT%d' % j]))
        tasks.append((ld_kd, cp_kd))

        def ld_vdt(slot):
            wload((C_V, 256), slot, 0)
            wload((C_DT, 32), slot, 256)

        def cp_vdt(slot):
            phase1_tile(slot, 0, NT, True, False, 0, tri=cf[:, 384:512], blk=cf[:, 512:640], sample=True)
        tasks.append((ld_vdt, cp_vdt))

        def ld_kt(slot):
            wload((C_K, 256), slot, 0)

        def cp_kt(slot):
            w = wsl[slot]
            bi = pbank[0] % 2
            pbank[0] += 1
            for k in range(16):
                op('pe', lambda e, k=k: e.matmul(ps32(bi)[:, 0:256], lhsT=hT[:, k, 0:128], rhs=w[:, k, 0:256], start=(k == 0), stop=(k == 15)),
                   reads=['w%d' % slot, 'hT'], writes=[PSN[bi]])
            op('dve', lambda e: e.tensor_copy(out=kstage, in_=ps32(bi)[:, 0:256]), reads=[PSN[bi], 'RT'], writes=['kstage'])
            for b in range(16):
                dma('sp', ksm[b, 120:128, :], kstage[8 * b:8 * b + 8, :], ['kstage', 'RT'], [], 'so4')
        tasks.append((ld_kt, cp_kt))

        for j in range(4):
            def ld_q(slot, j=j):
                wload((C_Q + 512 * j, 512), slot)

            def cp_q(slot, j=j):
                for c in range(4):
                    inproj_chunk(slot, c, n, lambda pa, bn, c=c: op('act', lambda e: e.activation(out=QTu[:, 0, c, :], in_=pa, func=AF.Identity),
                                                                    reads=[bn, 'RT'], writes=['QTu']))
            tasks.append((ld_q, cp_q))

            def ld_ga(slot, j=j):
                wload((C_GA + 512 * j, 512), slot)

            def cp_ga(slot, j=j):
                for c in range(4):
                    inproj_chunk(slot, c, n, lambda pa, bn, c=c: op('act', lambda e: e.activation(out=gs[:, c, 0:n], in_=pa, func=AF.Silu),
                                                                    reads=[bn, 'RT'], writes=['gs']))
                sample_attn(j)
                if j == 3:
                    attn_finish(NT)
            tasks.append((ld_ga, cp_ga))

        for g in range(4):
            def ld_z(slot, g=g):
                wload((C_Z + 512 * g, 512), slot)

            def cp_z(slot, g=g):
                if g == 0:
                    fence('S')
                    op('dve', lambda e: e.memset(xdtp, 0.0), reads=['RT'], writes=['xdtp'])
                    op('dve', lambda e: e.memset(hps, 0.0), reads=['RT'], writes=['hps'])
                for c in range(4):
                    inproj_chunk(slot, c, n, lambda pa, bn, c=c: op('act', lambda e: e.activation(out=zs[:, c, 0:n], in_=pa, func=AF.Silu),
                                                                    reads=[bn, 'RT'], writes=['zs']))
            tasks.append((ld_z, cp_z))

            def ld_x(slot, g=g):
                wload((C_X + 512 * g, 512), slot)

            def cp_x(slot, g=g):
                for c in range(4):
                    inproj_chunk(slot, c, n, lambda pa, bn, c=c: op('act', lambda e: e.activation(
                        out=xpre_s[:, c, :, 3:11], in_=pa.rearrange("p (b t) -> p b t", t=8), func=AF.Identity), reads=[bn, 'RT'], writes=['xpre%d' % c]))
            tasks.append((ld_x, cp_x))

            def ld_bc(slot, g=g):
                wload((C_B + 128 * g, 128), slot, 0)
                wload((C_C + 128 * g, 128), slot, 128)

            def cp_bc(slot, g=g):
                for c in range(2):
                    inproj_chunk(slot, c, n, lambda pa, bn, c=c: op('act', lambda e: e.activation(
                        out=xpre_s[:, 4 + c, :, 3:11], in_=pa.rearrange("p (b t) -> p b t", t=8), func=AF.Identity), reads=[bn, 'RT'], writes=['xpre%d' % (4 + c)]))
                sample_conv(g)
                ssd_tile(g, 0, True, mask_ap=triub_blk[:, :], sample_hook=sample_ssd_hook(g))
            tasks.append((ld_bc, cp_bc))

        for cb in range(4):
            for kh in range(2):
                def ld_o(slot, cb=cb, kh=kh):
                    wload_rows(kh * 2048, cb * 512, slot)

                def cp_o(slot, cb=cb, kh=kh):
                    if cb == 0 and kh == 0:
                        fence('O')
                        dma('sp', gpost_b, bcast_dram(norm_post, D), ['RT'], ['gpost_b'], 'c10')
                    w = wsl[slot]
                    for k in range(16):
                        kk = kh * 16 + k
                        mn = ['mixA0'] if kk < 16 else ['mixS0']
                        op('pe', lambda e, k=k, kk=kk: e.matmul(ps32(2), lhsT=mixT[:, kk, 0:128], rhs=w[:, k, :], start=(kk == 0), stop=(kk == 31)),
                           reads=['w%d' % slot] + mn, writes=[PSN[2]])
                    if kh == 1:
                        op('act', lambda e: e.activation(out=outacc[:, 0, cb * 512:(cb + 1) * 512], in_=ps32(2), func=AF.Identity),
                           reads=[PSN[2], 'RT'], writes=['oacc0'])
                        op('act', lambda e: e.activation(out=ojunk, in_=ps32(2), func=AF.Square, accum_out=ssq[:, 0, cb:cb + 1]),
                           reads=[PSN[2], 'RT'], writes=['ojunk', 'ssq0'])
                        if cb == 3:
                            post_tile(0, xsm, ysm)
                tasks.append((ld_o, cp_o))

    def add_prompt_group(b, G):
        NT = 4
        n = 512
        first_group = (G == 0)
        last_group = (G == NG - 1)
        xsrc = xp[b * SEQ + G * 512: b * SEQ + (G + 1) * 512, :]
        ydst = yp[b * SEQ + G * 512: b * SEQ + (G + 1) * 512, :]
        gi = len(grp_x)
        grp_x.append((xsrc, NT))

        def ld_kd(slot):
            w = wsl[slot]
            for j in range(4):
                for dup in range(2):
                    wload((C_K + 64 * j, 64), slot, j * 128 + dup * 64)

        def cp_kd(slot):
            fence('A')
            if first_group:
                op('dve', lambda e: e.memset(hist[:, :, :], 0.0), reads=[], writes=['hist%d' % c for c in range(24)])
            if gi == 0:
                phase0(xsrc, NT)
            for j in range(4):
                inproj_chunk(slot, j, n, lambda pa, bn, j=j: op('act', lambda e: e.activation(out=KT[:, j, 128:128 + n], in_=pa, func=AF.Identity),
                                                              reads=[bn], writes=['KT%d' % j]))
        tasks.append((ld_kd, cp_kd))

        def ld_vdt(slot):
            wload((C_V, 256), slot, 0)
            wload((C_DT, 32), slot, 256)

        def cp_vdt(slot):
            for tt in range(NT):
                phase1_tile(slot, tt, NT, first_group and tt == 0, last_group and tt == NT - 1, b)
        tasks.append((ld_vdt, cp_vdt))
        if last_group:
            def ld_kt(slot):
                wload((C_K, 256), slot, 0)

            def cp_kt(slot):
                w = wsl[slot]
                tt = NT - 1
                bi = pbank[0] % 2
                pbank[0] += 1
                for k in range(16):
                    op('pe', lambda e, k=k: e.matmul(ps32(bi)[:, 0:256], lhsT=hT[:, k, tt * 128:(tt + 1) * 128], rhs=w[:, k, 0:256], start=(k == 0), stop=(k == 15)),
                       reads=['w%d' % slot, 'hT'], writes=[PSN[bi]])
                op('dve', lambda e: e.tensor_copy(out=kstage, in_=ps32(bi)[:, 0:256]), reads=[PSN[bi], 'RT'], writes=['kstage'])
                dma('sp', kp[b], kstage, ['kstage', 'RT'], [], 'kst')
            tasks.append((ld_kt, cp_kt))

        for j in range(4):
            def ld_q(slot, j=j):
                wload((C_Q + 512 * j, 512), slot)

            def cp_q(slot, j=j):
                drain(None, older_than=j - 1)
                cur.update(QTu=[QTu, QTuB][j % 2], gs=[gs, gsB][j % 2], sfx='ab'[j % 2])
                Q_, sfx = cur['QTu'], cur['sfx']
                for c in range(4):
                    inproj_chunk(slot, c, n, lambda pa, bn, c=c: op('act', lambda e: e.activation(
                        out=Q_[:, :, c, :], in_=pa.rearrange("p (t q) -> p t q", t=4), func=AF.Identity), reads=[bn, 'RT'], writes=['QTu' + sfx]))
            tasks.append((ld_q, cp_q))

            def ld_ga(slot, j=j):
                wload((C_GA + 512 * j, 512), slot)

            def cp_ga(slot, j=j):
                G_, sfx = cur['gs'], cur['sfx']
                for c in range(4):
                    inproj_chunk(slot, c, n, lambda pa, bn, c=c: op('act', lambda e: e.activation(out=G_[:, c, :], in_=pa, func=AF.Identity),
                                                                    reads=[bn, 'RT'], writes=['gs' + sfx]))
                op('act', lambda e: e.activation(out=G_[:, :, :], in_=G_[:, :, :], func=AF.Silu), reads=['gs' + sfx, 'RT'], writes=['gs' + sfx])
                attn_unit(j, NT, first_group)
                if j == 3:
                    drain()
                    attn_finish(NT)
                    op('act', lambda e: e.activation(out=KT[:, :, 0:128], in_=KT[:, :, 512:640], func=AF.Identity), reads=['KT%d' % jj for jj in range(4)],
                       writes=['KT%d' % jj for jj in range(4)])
                    op('dve', lambda e: e.tensor_copy(out=Vpad[:, 0, :, :, :], in_=Vpad[:, 4, :, :, :]), reads=['Vpad4'], writes=['Vpad0'])
            tasks.append((ld_ga, cp_ga))

        for g in range(4):
            def ld_z(slot, g=g):
                wload((C_Z + 512 * g, 512), slot)

            def cp_z(slot, g=g):
                if g == 0:
                    fence('S')
                    op('dve', lambda e: e.memset(xdtp, 0.0), reads=['RT'], writes=['xdtp'])
                drain(None, older_than=10 + g - 1)
                cur.update(zs=[zs, zsB][g % 2], xc=[xc, xcB][g % 2], BCt=[BCt, BCtB][g % 2], sfx='ab'[g % 2])
                Z_, sfx = cur['zs'], cur['sfx']
                for c in range(4):
                    inproj_chunk(slot, c, n, lambda pa, bn, c=c: op('act', lambda e: e.activation(out=Z_[:, c, :], in_=pa, func=AF.Identity),
                                                                    reads=[bn, 'RT'], writes=['zs' + sfx]))
                op('act', lambda e: e.activation(out=Z_[:, :, :], in_=Z_[:, :, :], func=AF.Silu), reads=['zs' + sfx, 'RT'], writes=['zs' + sfx])
            tasks.append((ld_z, cp_z))

            def ld_x(slot, g=g):
                wload((C_X + 512 * g, 512), slot)

            def cp_x(slot, g=g):
                for c in range(4):
                    eng = 'act' if c % 2 == 0 else 'dve'
                    if eng == 'act':
                        inproj_chunk(slot, c, n, lambda pa, bn, c=c: op('act', lambda e: e.activation(out=xpre[:, c, 3:3 + n], in_=pa, func=AF.Identity),
                                                                        reads=[bn, 'RT'], writes=['xpre%d' % c]))
                    else:
                        inproj_chunk(slot, c, n, lambda pa, bn, c=c: op('dve', lambda e: e.tensor_copy(out=xpre[:, c, 3:3 + n], in_=pa),
                                                                        reads=[bn, 'RT'], writes=['xpre%d' % c]))
            tasks.append((ld_x, cp_x))

            def ld_bc(slot, g=g):
                wload((C_B + 128 * g, 128), slot, 0)
                wload((C_C + 128 * g, 128), slot, 128)

            def cp_bc(slot, g=g):
                for c in range(2):
                    inproj_chunk(slot, c, n, lambda pa, bn, c=c: op('act', lambda e: e.activation(out=xpre[:, 4 + c, 3:3 + n], in_=pa, func=AF.Identity),
                                                                    reads=[bn, 'RT'], writes=['xpre%d' % (4 + c)]))
                conv_group(g, NT, b, last_group)
                tg = 10 + g
                defer(tg, lambda: ssd_prep(g, 0))
                for tt in range(NT + 1):
                    def piece(tt=tt):
                        if tt >= 1:
                            ssd_back(g, tt - 1)
                        if tt < NT:
                            ssd_mid(g, tt, first_group and tt == 0)
                            if tt + 1 < NT:
                                ssd_prep(g, tt + 1)
                    defer(tg, piece)
                if last_group:
                    defer(tg, lambda: ssm_out(g, hp[b]))
            tasks.append((ld_bc, cp_bc))

        for cb in range(4):
            for kh in range(2):
                def ld_o(slot, cb=cb, kh=kh):
                    wload_rows(kh * 2048, cb * 512, slot)

                def cp_o(slot, cb=cb, kh=kh):
                    if cb == 0 and kh == 0:
                        fence('O')
                        dma('sp', gpost_b, bcast_dram(norm_post, D), ['RT'], ['gpost_b'], 'c10')
                    w = wsl[slot]
                    for k in range(16):
                        kk = kh * 16 + k
                        for tt in range(NT):
                            mn = ['mixA%d' % tt] if kk < 16 else ['mixS%d' % tt]
                            op('pe', lambda e, k=k, kk=kk, tt=tt: e.matmul(ps32(2 + tt), lhsT=mixT[:, kk, tt * 128:(tt + 1) * 128], rhs=w[:, k, :],
                                                                            start=(kk == 0), stop=(kk == 31)),
                               reads=['w%d' % slot] + mn, writes=[PSN[2 + tt]])
                    if gi + 1 < len(grp_x):
                        nx, nnt = grp_x[gi + 1]
                        ib = cb * 2 + kh
                        if ib < nnt:
                            phase0(nx, nnt, tiles=[ib])
                    if kh == 1:
                        for tt in range(NT):
                            op('act', lambda e, tt=tt: e.activation(out=outacc[:, tt, cb * 512:(cb + 1) * 512], in_=ps32(2 + tt), func=AF.Identity),
                               reads=[PSN[2 + tt], 'RT'], writes=['oacc%d' % tt])
                        for tt in range(NT):
                            op('act', lambda e, tt=tt: e.activation(out=ojunk, in_=outacc[:, tt, cb * 512:(cb + 1) * 512], func=AF.Square, accum_out=ssq[:, tt, cb:cb + 1]),
                               reads=['oacc%d' % tt, 'RT'], writes=['ojunk', 'ssq%d' % tt])
                        if cb == 3:
                            for tt in range(NT):
                                post_tile(tt, xsrc, ydst)
                tasks.append((ld_o, cp_o))

    for b in range(NP):
        for G in range(NG):
            add_prompt_group(b, G)
    if SAMPLE:
        add_sample_group()

    import os as _os
    if _os.environ.get("KSTOP"):
        del tasks[int(_os.environ["KSTOP"]):]
    if _os.environ.get("KSKIP"):
        del tasks[:int(_os.environ["KSKIP"])]
    nt = len(tasks)
    for i in range(min(NSLOT, nt)):
        tasks[i][0](i % NSLOT)
    for i in range(nt):
        tasks[i][1](i % NSLOT)
        if i + NSLOT < nt:
            tasks[i + NSLOT][0]((i + NSLOT) % NSLOT)
    outsems = ['yst0', 'yst1', 'yst2', 'yst3', 'vst', 'kst', 'cst', 'hst_o', 'so0', 'so1', 'so2', 'so3', 'so4']
    for s in outsems:
        if any(o['dma'] == s for o in P.ops):
            last = max(i for i, o in enumerate(P.ops) if o['dma'] == s)
            P.ops.append(dict(eng='sp', fn=lambda e: e.nop(), deps={last}, dma=None, sig=False, sem=None, val=0))
    P.emit(nc, st)
    st.close()
    return nc, P


_CACHE = {}


def kernel(x_prompt, x_sample, cache_k, cache_v, state_conv, state_ssm, norm_pre, w_in, attn_sink, attn_norm,
           conv_w, conv_b, dt_bias, a_log, d_skip, ssm_norm, w_out, norm_post):
    f = lambda a: np.ascontiguousarray(np.asarray(a, dtype=np.float32))
    B, S, _ = x_prompt.shape
    NPc = B // NCORES
    DB = x_sample.shape[0] // NCORES
    assert DB == 16 and x_sample.shape[1] == 8
    key = (NPc, S)
    if key not in _CACHE:
        _CACHE[key] = build(NPc, S, SAMPLE=True)[0]
    nc = _CACHE[key]
    cst = host_consts()
    shared = dict(w_in=f(w_in[0]), w_out=f(w_out[0]), norm_pre=f(norm_pre[0]), attn_sink=f(attn_sink[0]), attn_norm=f(attn_norm[0]),
                  conv_w=f(conv_w[0]), conv_b=f(conv_b[0]), dt_bias=f(dt_bias[0]), a_log=f(a_log[0]), d_skip=f(d_skip[0]),
                  ssm_norm=f(ssm_norm[0]), norm_post=f(norm_post[0]), consts=cst)
    in_maps = []
    for c in range(NCORES):
        m = dict(shared)
        m["xp"] = f(x_prompt[c * NPc:(c + 1) * NPc]).reshape(NPc * S, D)
        sl = slice(c * DB, (c + 1) * DB)
        m["xsm"] = f(x_sample[sl]).reshape(DB * 8, D)
        m["ck"] = f(cache_k[0, sl]).reshape(DB, 128, 256)
        m["cv"] = f(cache_v[0, sl]).reshape(DB, 128, 256)
        m["sconv"] = f(state_conv[0, sl])
        m["sssm"] = f(state_ssm[0, sl])
        in_maps.append(m)
    res = run_bass_kernel_spmd(nc, in_maps, core_ids=list(range(NCORES)))
    R = res.results
    cat = lambda k: np.concatenate([np.asarray(r[k]) for r in R], axis=0)
    y_prompt = cat("yp").reshape(B, S, D)
    y_sample = cat("ysm").reshape(NCORES * DB, 8, D)
    k_prompt = cat("kp").reshape(1, B, 128, 4, 64)
    v_prompt = cat("vp").reshape(1, B, 128, 4, 64)
    conv_prompt = cat("cp").reshape(1, B, 3, 3072)
    ssm_prompt = cat("hp").reshape(1, B, 32, 64, 128)
    k_sample = cat("ksm").reshape(1, NCORES * DB, 128, 4, 64)
    v_sample = cat("vsm").reshape(1, NCORES * DB, 128, 4, 64)
    conv_sample = cat("csm").reshape(1, NCORES * DB, 3, 3072)
    ssm_sample = cat("hsm").reshape(1, NCORES * DB, 32, 64, 128)
    return (y_prompt.astype(np.float32), y_sample.astype(np.float32), k_prompt, v_prompt, conv_prompt, ssm_prompt,
            k_sample, v_sample, conv_sample, ssm_sample)
```

```python
import numpy as np
from contextlib import ExitStack
import concourse.bass as bass
import concourse.mybir as mybir
from concourse.bass_utils import run_bass_kernel_spmd

F32 = mybir.dt.float32
BF16 = mybir.dt.bfloat16
ALU = mybir.AluOpType
AF = mybir.ActivationFunctionType
AX = mybir.AxisListType

D = 2048
IN_DIM = 9760
C_Q, C_K, C_V, C_GA, C_Z, C_X, C_B, C_C, C_DT = 0, 2048, 2304, 2560, 4608, 6656, 8704, 9216, 9728
EPS = 1e-6
NCORES = 8
NSLOT = 3
DRAIN_N = 1


class Prog:
    ENG = ('pe', 'act', 'dve', 'pool', 'sp')

    def __init__(self):
        self.ops = []
        self.last_w = {}
        self.readers = {}

    def begin_capture(self):
        self._cap = []

    def end_capture(self):
        lst, self._cap = self._cap, None
        return lst

    def op(self, eng, fn, reads=(), writes=(), dma=None):
        if getattr(self, '_cap', None) is not None:
            self._cap.append((eng, fn, tuple(reads), tuple(writes), dma))
            return None
        i = len(self.ops)
        deps = set()
        for b in reads:
            w = self.last_w.get(b)
            if w is not None:
                deps.add(w)
        for b in writes:
            w = self.last_w.get(b)
            if w is not None:
                deps.add(w)
            for r in self.readers.get(b, ()):
                deps.add(r)
        for b in reads:
            lst = self.readers.setdefault(b, [])
            if dma is None:
                lst[:] = [r for r in lst if not (self.ops[r]['dma'] is None and self.ops[r]['eng'] == eng)]
            lst.append(i)
        for b in writes:
            self.last_w[b] = i
            self.readers[b] = []
        self.ops.append(dict(eng=eng, fn=fn, deps=deps, dma=dma, sig=False, sem=None, val=0))
        return i

    @staticmethod
    def _skip(dep, op):
        return dep['dma'] is None and dep['eng'] == op['eng'] and op['eng'] == 'pe'

    def schedule(self):
        ops = self.ops
        for op in ops:
            for d in op['deps']:
                dep = ops[d]
                if self._skip(dep, op):
                    continue
                if dep['dma'] is None:
                    dep['sig'] = True
        cnt = {e: 0 for e in self.ENG}
        dmacnt = {}
        for op in ops:
            if op['dma'] is not None:
                dmacnt[op['dma']] = dmacnt.get(op['dma'], 0) + 16
                op['sem'] = ('dma', op['dma'])
                op['val'] = dmacnt[op['dma']]
            elif op['sig']:
                cnt[op['eng']] += 1
                op['sem'] = ('eng', op['eng'])
                op['val'] = cnt[op['eng']]
        known = {e: {} for e in self.ENG}
        streams = {e: [] for e in self.ENG}
        snaps = {}
        nwait = 0
        for op in ops:
            e = op['eng']
            need = {}
            for d in op['deps']:
                dep = ops[d]
                if self._skip(dep, op):
                    continue
                key, val = dep['sem'], dep['val']
                if known[e].get(key, 0) >= val:
                    continue
                need[key] = max(need.get(key, 0), val)
            for key, val in sorted(need.items(), key=lambda kv: str(kv[0])):
                if known[e].get(key, 0) >= val:
                    continue
                streams[e].append(('wait', key, val))
                nwait += 1
                known[e][key] = val
                for k2, v2 in snaps[(key, val)].items():
                    if known[e].get(k2, 0) < v2:
                        known[e][k2] = v2
            streams[e].append(('op', op))
            if op['sem'] is not None:
                s = dict(known[e])
                if op['dma'] is None:
                    s[op['sem']] = op['val']
                snaps[(op['sem'], op['val'])] = s
        self.streams = streams
        self.sem_keys = sorted({op['sem'] for op in ops if op['sem'] is not None}, key=str)
        self.stats = dict(nops=len(ops), nwait=nwait, cnt=cnt, dmacnt=dmacnt)
        return streams

    def emit(self, nc, stack):
        streams = self.schedule()
        sems = {}
        for k in self.sem_keys:
            sems[k] = stack.enter_context(nc.semaphore("s_%s_%s" % (k[0], k[1])))
        block = stack.enter_context(nc.Block())

        def replay(name, eng):
            for rec in streams[name]:
                if rec[0] == 'wait':
                    eng.wait_ge(sems[rec[1]], rec[2])
                else:
                    op = rec[1]
                    ins = op['fn'](eng)
                    if op['dma'] is not None:
                        ins.then_inc(sems[op['sem']], 16)
                    elif op['sig']:
                        ins.then_inc(sems[op['sem']], 1)

        @block.tensor
        def _(eng):
            replay('pe', eng)

        @block.scalar
        def _(eng):
            replay('act', eng)

        @block.vector
        def _(eng):
            replay('dve', eng)

        @block.gpsimd
        def _(eng):
            replay('pool', eng)

        @block.sync
        def _(eng):
            replay('sp', eng)


def host_consts():
    c = np.zeros((128, 5 * 128 + 16), np.float32)
    i = np.arange(128)
    c[:, 0:128] = np.eye(128)
    c[:, 128:256] = (i[:, None] <= i[None, :])
    c[:, 256:384] = 1.0
    same = (i[:, None] // 8) == (i[None, :] // 8)
    c[:, 384:512] = same & (i[:, None] <= i[None, :])
    c[:, 512:640] = same
    c[:, 640:656] = (i[:, None] // 8) == np.arange(16)[None, :]
    return c


def build(NP, SEQ, SAMPLE=True):
    NG = SEQ // 512
    nc = bass.Bass("TRN2", target_bir_lowering=False)
    din = lambda name, shape: nc.dram_tensor(name, list(shape), F32, kind="ExternalInput").ap()
    dout = lambda name, shape: nc.dram_tensor(name, list(shape), F32, kind="ExternalOutput").ap()
    xp = din("xp", [NP * SEQ, D])
    w_in = din("w_in", [D, IN_DIM])
    w_out = din("w_out", [2 * D, D])
    norm_pre = din("norm_pre", [D])
    attn_sink = din("attn_sink", [32])
    attn_norm = din("attn_norm", [D])
    conv_w = din("conv_w", [4, 3072])
    conv_b = din("conv_b", [3072])
    dt_bias = din("dt_bias", [32])
    a_log = din("a_log", [32])
    d_skip = din("d_skip", [32])
    ssm_norm = din("ssm_norm", [D])
    norm_post = din("norm_post", [D])
    consts = din("consts", [128, 656])
    yp = dout("yp", [NP * SEQ, D])
    kp = dout("kp", [NP, 128, 256])
    vp = dout("vp", [NP, 128, 256])
    cp = dout("cp", [NP, 3, 3072])
    hp = dout("hp", [NP, 32, 64, 128])
    if SAMPLE:
        xsm = din("xsm", [128, D])
        ck = din("ck", [16, 128, 256])
        cv = din("cv", [16, 128, 256])
        sconv = din("sconv", [16, 3, 3072])
        sssm = din("sssm", [16, 32, 64, 128])
        ysm = dout("ysm", [128, D])
        ksm = dout("ksm", [16, 128, 256])
        vsm = dout("vsm", [16, 128, 256])
        csm = dout("csm", [16, 3, 3072])
        hsm = dout("hsm", [16, 32, 64, 128])
        vscr = nc.dram_tensor("vscr", [128, 256], F32, kind="Internal").ap()

    P = Prog()
    st = ExitStack()
    sb = lambda name, shape, dt: st.enter_context(nc.sbuf_tensor(name, list(shape), dt))
    cf = sb("cf", [128, 656], F32)
    ident_f, triu_f, ones_f = cf[:, 0:128], cf[:, 128:256], cf[:, 256:384]
    identb = sb("identb", [128, 128], BF16)
    maskb = sb("maskb", [128, 2, 128], BF16)
    onespad = sb("onespad", [128, 2, 128], BF16)
    onesb = sb("onesb", [128, 128], BF16)
    triub_blk = sb("triub_blk", [128, 128], BF16)
    onespad_f = sb("onespad_f", [32, 2, 128], F32)
    gpre = sb("gpre", [128, 16], F32)
    anorm = sb("anorm", [128, 16], F32)
    snorm = sb("snorm", [128, 16], F32)
    convw = sb("convw", [128, 4, 24], F32)
    convb = sb("convb", [128, 24], F32)
    Dcol = sb("Dcol", [128, 16], F32)
    Ecol = sb("Ecol", [128, 16], F32)
    dtb_b = sb("dtb_b", [128, 32], F32)
    A_b = sb("A_b", [128, 32], F32)
    epsc = sb("epsc", [128, 1], F32)
    onec = sb("onec", [128, 1], F32)
    hT = sb("hT", [128, 16, 512], BF16)
    mixT = sb("mixT", [128, 32, 512], BF16)
    wsl = [sb("w%d" % i, [128, 16, 512], BF16) for i in range(NSLOT)]
    xt = sb("xt", [128, D], F32)
    stt = sb("stt", [128, 8], F32)
    fscr = sb("fscr", [128, 1], F32)
    KT = sb("KT", [128, 4, 640], BF16)
    Vpad = sb("Vpad", [128, 5, 4, 2, 128], BF16)
    hstate = sb("hstate", [128, D], F32)
    hpad = sb("hpad", [128, 32, 128], BF16)
    hist = sb("hist", [128, 24, 3], F32)
    dt_all = sb("dt_all", [128, 4, 32], F32)
    dtA_all = sb("dtA_all", [128, 4, 32], F32)
    a_all = sb("a_all", [128, 4, 32], F32)
    w_all = sb("w_all", [128, 4, 32], F32)
    cd_all = sb("cd_all", [128, 4, 32], F32)
    aT_all = sb("aT_all", [32, 4, 128], F32)
    ssq = sb("ssq", [128, 4, 4], F32)
    RTW = 15360
    RT = sb("RT", [128, RTW], F32)

    def rt32(off, n):
        return RT[:, off:off + n]

    def rt16(off, n):
        return RT[:, off:off + n].bitcast(BF16)

    QTu = rt16(0, 1024).rearrange("p (t c q) -> p t c q", t=4, c=4)
    gs = rt16(1024, 1024).rearrange("p (c t) -> p c t", c=4)
    Eb = rt16(2048, 1024).rearrange("p (e t) -> p e t", e=4)
    EbB = rt16(4992, 1024).rearrange("p (e t) -> p e t", e=4)
    Ebs = [Eb, EbB]
    rstd_a = rt32(11136, 512)
    QTuB = rt16(6016, 1024).rearrange("p (t c q) -> p t c q", t=4, c=4)
    gsB = rt16(7040, 1024).rearrange("p (c t) -> p c t", c=4)
    rd = rt32(3072, 512)
    o32 = rt32(3584, 512)
    sqa = rt16(4096, 256)
    vstage = rt32(4352, 256)
    kstage = rt32(4608, 256)
    dtmp = rt32(4864, 128)
    xpre = rt32(0, 3090).rearrange("p (c t) -> p c t", c=6)
    ctmp = rt32(3090, 1024).rearrange("p (e t) -> p e t", e=2)
    xc = rt16(4114, 1024).rearrange("p (c t) -> p c t", c=4)
    BCt = rt16(5138, 512).rearrange("p (c t) -> p c t", c=2)
    zs = rt16(5650, 1024).rearrange("p (c t) -> p c t", c=4)
    dm = rt32(6674, 1024).rearrange("p (h l) -> p h l", h=8)
    Eb2 = rt16(7698, 512).rearrange("p (h l) -> p h l", h=8)
    MT = rt16(8210, 512).rearrange("p (h l) -> p h l", h=8)
    EA = rt16(8722, 512).rearrange("p (h l) -> p h l", h=8)
    CTs = rt16(9234, 512).rearrange("p (h l) -> p h l", h=8)
    xdtp = rt16(9746, 512).rearrange("p (c e q) -> p c e q", c=4, e=2)
    xw = rt16(10258, 256)
    Btok = rt16(10514, 64)
    cbm = rt16(10578, 64)
    yd = rt32(10642, 512)
    rg = rt32(11154, 128)
    sqs = rt16(11282, 256)
    Rm = rt32(11538, 128 * 1)
    hstage = rt32(6674, 512).rearrange("p (c n) -> p c n", c=4)
    cstage = rt32(3090, 768).rearrange("p (c n) -> p c n", c=6)
    Kc = rt16(4992, 1024).rearrange("p (b e d) -> p b e d", b=16, e=2)
    KTc = rt16(6016, 1024).rearrange("p (b s) -> p b s", b=16)
    Vc_pad = rt16(7040, 2048).rearrange("p (b e q) -> p b e q", b=16, e=2)
    Vn_pad = rt16(9088, 2048).rearrange("p (b e q) -> p b e q", b=16, e=2)
    xpre_s = rt32(0, 1056).rearrange("p (c b t) -> p c b t", c=6, b=16)
    cst_in = rt32(1056, 768).rearrange("p (c n) -> p c n", c=6)
    cst_o = rt32(1824, 288).rearrange("p (c n) -> p c n", c=6)
    cstage_s = rt32(3090, 768).rearrange("p (c n) -> p c n", c=6)
    h0f = rt32(11776, 1024).rearrange("p (b n) -> p b n", b=8)
    hps = rt16(12800, 1024).rearrange("p (b e q) -> p b e q", b=8, e=2)
    h0b = rt16(13824, 512).rearrange("p (b n) -> p b n", b=8)
    Bm = rt16(14336, 1024).rearrange("p (b n) -> p b n", b=16)
    zsB = rt16(11776, 1024).rearrange("p (c t) -> p c t", c=4)
    xcB = rt16(12800, 1024).rearrange("p (c t) -> p c t", c=4)
    BCtB = rt16(13824, 512).rearrange("p (c t) -> p c t", c=2)
    cur = dict(QTu=QTu, gs=gs, zs=zs, xc=xc, BCt=BCt, sfx='')
    outacc = rt32(0, 8192).rearrange("p (t c) -> p t c", t=4)
    ojunk = rt16(8192, 256)
    xsb = rt16(8448, 1024)
    gpost_b = rt32(9472, 2048)

    psb = [st.enter_context(nc.psum_tensor("ps%d" % i, [128, 512], F32)) for i in range(8)]

    def ps32(i):
        return psb[i][:, :]

    def ps16(i):
        return psb[i][:, :].bitcast(BF16)

    op = P.op
    PSN = ['ps%d' % i for i in range(8)]

    pend = []

    def defer(tag, fn):
        P.begin_capture()
        fn()
        pend.append((tag, P.end_capture()))

    def drain(k=None, older_than=None):
        n = 0
        while pend and (k is None or n < k):
            if older_than is not None and pend[0][0] >= older_than:
                break
            tag, lst = pend.pop(0)
            for rec in lst:
                P.op(*rec)
            n += 1

    def fence(tag):
        drain()
        op('dve', lambda e: e.memset(fscr[:, :], 0.0), reads=[], writes=['RT'])

    def bcast_dram(vec, n, parts=128):
        return bass.AP(tensor=vec.tensor, offset=0, ap=[[0, parts], [1, n]])

    dma_ctr = [0]

    def dma(q, out, in_, r, w, sem, **kw):
        op(q, lambda e: e.dma_start(out=out, in_=in_, **kw), reads=r, writes=w, dma=sem)

    dma('sp', cf[:, :], consts, [], ['cf'], 'c0')
    dma('sp', gpre[:, :], norm_pre.rearrange("(k p) -> p k", p=128), [], ['gpre'], 'c1', allow_slow_non_contiguous=True)
    dma('sp', anorm[:, :], attn_norm.rearrange("(k p) -> p k", p=128), [], ['anorm'], 'c2', allow_slow_non_contiguous=True)
    dma('sp', snorm[:, :], ssm_norm.rearrange("(k p) -> p k", p=128), [], ['snorm'], 'c3', allow_slow_non_contiguous=True)
    for k in range(4):
        dma('sp', convw[:, k, :], conv_w[k].rearrange("(c p) -> p c", p=128), [], ['convw'], 'c4', allow_slow_non_contiguous=True)
    dma('sp', convb[:, :], conv_b.rearrange("(c p) -> p c", p=128), [], ['convb'], 'c5', allow_slow_non_contiguous=True)
    for h2 in range(2):
        dma('sp', Dcol[64 * h2:64 * h2 + 64, :], bass.AP(tensor=d_skip.tensor, offset=h2, ap=[[0, 64], [2, 16]]),
            [], ['Dcol'], 'c6', allow_slow_non_contiguous=True)
        dma('sp', Ecol[64 * h2:64 * h2 + 64, :], bass.AP(tensor=attn_sink.tensor, offset=h2, ap=[[0, 64], [2, 16]]),
            [], ['Ecol'], 'c7', allow_slow_non_contiguous=True)
    dma('sp', dtb_b[:, :], bcast_dram(dt_bias, 32), [], ['dtb_b'], 'c8')
    dma('sp', A_b[:, :], bcast_dram(a_log, 32), [], ['A_b'], 'c9')
    op('dve', lambda e: e.tensor_copy(out=identb[:, :], in_=ident_f), reads=['cf'], writes=['identb'])
    op('dve', lambda e: e.tensor_copy(out=maskb[:, 1, :], in_=triu_f), reads=['cf'], writes=['maskb'])
    op('dve', lambda e: e.tensor_scalar(out=maskb[:, 0, :], in0=triu_f, scalar1=-1.0, scalar2=1.0, op0=ALU.mult, op1=ALU.add),
       reads=['cf'], writes=['maskb'])
    op('dve', lambda e: e.tensor_copy(out=onesb[:, :], in_=ones_f), reads=['cf'], writes=['onesb'])
    op('pool', lambda e: e.memset(onespad[:, :, :], 0.0), writes=['onespad'])
    op('dve', lambda e: e.tensor_copy(out=triub_blk[:, :], in_=cf[:, 384:512]), reads=['cf'], writes=['maskb'])
    op('pool', lambda e: e.memset(onespad_f[:, :, :], 0.0), writes=['onespad_f'])
    op('pool', lambda e: e.memset(onespad_f[:, 0, 0:64], 1.0), writes=['onespad_f'])
    op('pool', lambda e: e.memset(onespad_f[:, 1, 64:128], 1.0), writes=['onespad_f'])
    op('pool', lambda e: e.memset(onespad[:, 0, 0:64], 1.0), reads=[], writes=['onespad'])
    op('pool', lambda e: e.memset(onespad[:, 1, 64:128], 1.0), reads=[], writes=['onespad'])
    op('pool', lambda e: e.memset(epsc[:, :], EPS), writes=['epsc'])
    op('pool', lambda e: e.memset(onec[:, :], 1.0), writes=['onec'])
    op('pool', lambda e: e.memset(Vpad[:, :, :, :, :], 0.0), writes=['Vpad%d' % i for i in range(5)])
    op('pool', lambda e: e.memset(hpad[:, :, :], 0.0), writes=['hpad%d' % i for i in range(4)])
    op('act', lambda e: e.activation(out=Ecol[:, :], in_=Ecol[:, :], func=AF.Exp), reads=['Ecol'], writes=['Ecol'])
    op('act', lambda e: e.activation(out=A_b[:, :], in_=A_b[:, :], func=AF.Exp), reads=['A_b'], writes=['A_b'])
    op('dve', lambda e: e.tensor_scalar(out=A_b[:, :], in0=A_b[:, :], scalar1=-1.0, scalar2=None, op0=ALU.mult),
       reads=['A_b'], writes=['A_b'])

    tasks = []
    grp_x = []

    def wload(cols, slot, base=0, src=None):
        src = w_in if src is None else src
        w = wsl[slot]
        c0, n = cols
        dma('pool', w[:, :, base:base + n], src[:, c0:c0 + n].rearrange("(k p) c -> p k c", p=128),
            [], ['w%d' % slot], 'wl%d' % slot)

    def wload_rows(r0, c0, slot):
        w = wsl[slot]
        dma('pool', w[:, :, :], w_out[r0:r0 + 2048, c0:c0 + 512].rearrange("(k p) c -> p k c", p=128),
            [], ['w%d' % slot], 'wl%d' % slot)

    pbank = [0]

    def inproj_chunk(slot, wc, ntok, evac):
        bi = pbank[0] % 2
        pbank[0] += 1
        w = wsl[slot]
        for k in range(16):
            op('pe', lambda e, k=k: e.matmul(ps32(bi)[:, 0:ntok], lhsT=w[:, k, wc * 128:(wc + 1) * 128], rhs=hT[:, k, 0:ntok],
                                              start=(k == 0), stop=(k == 15)),
               reads=['w%d' % slot, 'hT'], writes=[PSN[bi]])
        evac(ps32(bi)[:, 0:ntok], PSN[bi])
        drain(DRAIN_N)

    def phase0(xsrc, NT, tiles=None):
        for tt in (range(NT) if tiles is None else tiles):
            dma('sp', xt[:, :], xsrc[tt * 128:(tt + 1) * 128, :], [], ['xt'], 'xld')
            op('act', lambda e: e.activation(out=xsb[:, :], in_=xt[:, :], func=AF.Square, accum_out=stt[:, 0:1]),
               reads=['xt', 'RT'], writes=['xsb', 'stt'])
            op('act', lambda e: e.activation(out=stt[:, 1:2], in_=stt[:, 0:1], func=AF.Ln, scale=1.0 / D, bias=epsc[:, :]),
               reads=['stt', 'epsc'], writes=['stt'])
            op('act', lambda e: e.activation(out=stt[:, 2:3], in_=stt[:, 1:2], func=AF.Exp, scale=-0.5),
               reads=['stt'], writes=['stt'])
            op('dve', lambda e: e.tensor_scalar(out=xsb[:, :], in0=xt[:, :], scalar1=stt[:, 2:3], scalar2=None, op0=ALU.mult),
               reads=['xt', 'stt', 'RT'], writes=['xsb'])
            for half in range(2):
                bi = 6 + half
                pv = ps16(bi).rearrange("p (k t) -> p k t", k=8)
                for k8 in range(8):
                    k = half * 8 + k8
                    op('pe', lambda e, k=k, k8=k8, pv=pv: e.transpose(out=pv[:, k8, :], in_=xsb[:, k * 128:(k + 1) * 128], identity=identb[:, :]),
                       reads=['xsb', 'identb', 'RT'], writes=[PSN[bi]])
                op('dve', lambda e, half=half, tt=tt, pv=pv: e.tensor_tensor(
                    out=hT[:, half * 8:half * 8 + 8, tt * 128:(tt + 1) * 128], in0=pv,
                    in1=gpre[:, half * 8:half * 8 + 8].unsqueeze(2).broadcast_to([128, 8, 128]), op=ALU.mult),
                   reads=[PSN[bi], 'gpre'], writes=['hT'])

    def phase1_tile(slot, tt, NT, first_tile, last_tile_of_seq, seq_b, tri=None, blk=None, sample=False):
        tri = triu_f if tri is None else tri
        blk = ones_f if blk is None else blk
        w = wsl[slot]
        bi = pbank[0] % 2
        pbank[0] += 1
        for k in range(16):
            op('pe', lambda e, k=k: e.matmul(ps32(bi)[:, 0:288], lhsT=hT[:, k, tt * 128:(tt + 1) * 128], rhs=w[:, k, 0:288],
                                              start=(k == 0), stop=(k == 15)),
               reads=['w%d' % slot, 'hT'], writes=[PSN[bi]])
        import os as _os
        lvl = int(_os.environ.get("PH1", "99"))
        if lvl < 1:
            return
        pv = ps32(bi)
        vsrc = pv[:, 0:256].rearrange("p (j d) -> p j d", j=4)
        if not sample:
            op('act', lambda e: e.activation(out=Vpad[:, tt + 1, :, 0, 0:64], in_=vsrc, func=AF.Identity),
               reads=[PSN[bi]], writes=['Vpad%d' % (tt + 1)])
            op('dve', lambda e: e.tensor_copy(out=Vpad[:, tt + 1, :, 1, 64:128], in_=vsrc),
               reads=[PSN[bi]], writes=['Vpad%d' % (tt + 1)])
        else:
            op('dve', lambda e: e.tensor_copy(out=vstage, in_=pv[:, 0:256]), reads=[PSN[bi], 'RT'], writes=['vstage'])
            if not _os.environ.get('NOVSCR'):
                dma('sp', vscr, vstage, ['vstage', 'RT'], ['vscr'], 'vscr')
            for b in range(16):
                if _os.environ.get('NOVROWS'):
                    break
                dma('sp', vsm[b, 120:128, :], vstage[8 * b:8 * b + 8, :], ['vstage', 'RT'], [], 'so0')
        if last_tile_of_seq and not _os.environ.get('NOVST'):
            op('dve', lambda e: e.tensor_copy(out=vstage, in_=pv[:, 0:256]), reads=[PSN[bi], 'RT'], writes=['vstage'])
            if not _os.environ.get('NOVDMA'):
                dma('sp', vp[seq_b], vstage, ['vstage', 'RT'], [], 'vst')
        if lvl < 2:
            return
        op('dve', lambda e: e.tensor_tensor(out=dt_all[:, tt, :], in0=pv[:, 256:288], in1=dtb_b[:, :], op=ALU.add),
           reads=[PSN[bi], 'dtb_b'], writes=['dt%d' % tt])
        op('act', lambda e: e.activation(out=dt_all[:, tt, :], in_=dt_all[:, tt, :], func=AF.Exp), reads=['dt%d' % tt], writes=['dt%d' % tt])
        op('act', lambda e: e.activation(out=dt_all[:, tt, :], in_=dt_all[:, tt, :], func=AF.Ln, bias=onec[:, :]),
           reads=['dt%d' % tt, 'onec'], writes=['dt%d' % tt])
        op('dve', lambda e: e.tensor_tensor(out=dtA_all[:, tt, :], in0=dt_all[:, tt, :], in1=A_b[:, :], op=ALU.mult),
           reads=['dt%d' % tt, 'A_b'], writes=['dtA%d' % tt])
        if lvl < 3:
            return
        p2 = ps32(2)
        op('pe', lambda e: e.matmul(p2[:, 0:32], lhsT=tri, rhs=dtA_all[:, tt, :], start=True, stop=True),
           reads=['cf', 'dtA%d' % tt], writes=['ps2'])
        op('pe', lambda e: e.matmul(p2[:, 32:64], lhsT=blk, rhs=dtA_all[:, tt, :], start=True, stop=True),
           reads=['cf', 'dtA%d' % tt], writes=['ps2'])
        op('pe', lambda e: e.matmul(p2[0:32, 64:192], lhsT=dtA_all[:, tt, :], rhs=tri, start=True, stop=True),
           reads=['cf', 'dtA%d' % tt], writes=['ps2'])
        if lvl < 4:
            return
        op('act', lambda e: e.activation(out=a_all[:, tt, :], in_=p2[:, 0:32], func=AF.Identity), reads=['ps2'], writes=['a%d' % tt])
        op('act', lambda e: e.activation(out=aT_all[:, tt, :], in_=p2[0:32, 64:192], func=AF.Identity), reads=['ps2'], writes=['aT%d' % tt])
        op('act', lambda e: e.activation(out=cd_all[:, tt, :], in_=p2[:, 32:64], func=AF.Exp), reads=['ps2'], writes=['cd%d' % tt])
        op('dve', lambda e: e.tensor_tensor(out=w_all[:, tt, :], in0=p2[:, 32:64], in1=a_all[:, tt, :], op=ALU.subtract),
           reads=['ps2', 'a%d' % tt], writes=['w%d_' % tt])
        op('act', lambda e: e.activation(out=w_all[:, tt, :], in_=w_all[:, tt, :], func=AF.Exp), reads=['w%d_' % tt], writes=['w%d_' % tt])
        op('dve', lambda e: e.tensor_tensor(out=w_all[:, tt, :], in0=w_all[:, tt, :], in1=dt_all[:, tt, :], op=ALU.mult),
           reads=['w%d_' % tt, 'dt%d' % tt], writes=['w%d_' % tt])

    def attn_front(j, tt, has_prev):
        EB = Ebs[tt % 2]
        en = 'Eb%d_' % (tt % 2)
        Q_, sfx = cur['QTu'], cur['sfx']
        kbs = [0, 1] if has_prev else [1]
        for h2 in range(2):
            for kb in kbs:
                e_i = h2 * 2 + kb
                bi = (2, 3, 7, 2)[e_i]
                op('pe', lambda e, h2=h2, kb=kb, bi=bi: e.matmul(
                    ps32(bi), lhsT=KT[64 * h2:64 * h2 + 64, j, (tt + kb) * 128:(tt + kb + 1) * 128],
                    rhs=Q_[64 * h2:64 * h2 + 64, tt, :, :], start=True, stop=True),
                   reads=['KT%d' % j, 'QTu' + sfx, 'RT'], writes=[PSN[bi]])
                op('act', lambda e, e_i=e_i, bi=bi: e.activation(out=EB[:, e_i, :], in_=ps32(bi), func=AF.Exp, scale=0.125),
                   reads=[PSN[bi], 'RT'], writes=[en + str(e_i)])
                op('dve', lambda e, e_i=e_i, kb=kb: e.tensor_tensor(
                    out=EB[:, e_i, :].rearrange("p (c q) -> p c q", c=4), in0=EB[:, e_i, :].rearrange("p (c q) -> p c q", c=4),
                    in1=maskb[:, kb, :].unsqueeze(1).broadcast_to([128, 4, 128]), op=ALU.mult),
                   reads=[en + str(e_i), 'maskb', 'RT'], writes=[en + str(e_i)])

    def attn_back(j, tt, has_prev):
        EB = Ebs[tt % 2]
        en = 'Eb%d_' % (tt % 2)
        G_, sfx = cur['gs'], cur['sfx']
        kbs = [0, 1] if has_prev else [1]
        lst = [(h2, kb) for h2 in range(2) for kb in kbs]
        for idx, (h2, kb) in enumerate(lst):
            op('pe', lambda e, h2=h2, kb=kb, idx=idx: e.matmul(
                ps32(4), lhsT=Vpad[:, tt + kb, j, h2, :], rhs=EB[:, h2 * 2 + kb, :], start=(idx == 0), stop=(idx == len(lst) - 1)),
               reads=['Vpad%d' % (tt + kb), en + str(h2 * 2 + kb), 'RT'], writes=['ps4'])
        for idx, (h2, kb) in enumerate(lst):
            op('pe', lambda e, h2=h2, kb=kb, idx=idx: e.matmul(
                ps32(5), lhsT=onespad[:, h2, :], rhs=EB[:, h2 * 2 + kb, :], start=(idx == 0), stop=(idx == len(lst) - 1)),
               reads=['onespad', en + str(h2 * 2 + kb), 'RT'], writes=['ps5'])
        for c in range(4):
            op('dve', lambda e, c=c: e.tensor_scalar(out=rd[:, c * 128:(c + 1) * 128], in0=ps32(5)[:, c * 128:(c + 1) * 128],
                                                      scalar1=Ecol[:, 4 * j + c:4 * j + c + 1], scalar2=None, op0=ALU.add),
               reads=['ps5', 'Ecol', 'RT'], writes=['rd'])
        op('act', lambda e: e.activation(out=rd, in_=rd, func=AF.Ln), reads=['rd', 'RT'], writes=['rd'])
        op('act', lambda e: e.activation(out=rd, in_=rd, func=AF.Exp, scale=-1.0), reads=['rd', 'RT'], writes=['rd'])
        op('dve', lambda e: e.tensor_tensor(out=o32, in0=ps32(4), in1=rd, op=ALU.mult), reads=['ps4', 'rd', 'RT'], writes=['o32'])
        mv = mixT[:, 4 * j:4 * j + 4, tt * 128:(tt + 1) * 128]
        op('dve', lambda e: e.tensor_tensor(out=mv, in0=o32.rearrange("p (c q) -> p c q", c=4),
                                             in1=G_[:, :, tt * 128:(tt + 1) * 128], op=ALU.mult),
           reads=['o32', 'gs' + sfx, 'RT'], writes=['mixA%d' % tt])
        op('act', lambda e: e.activation(out=sqa.rearrange("p (c q) -> p c q", c=4), in_=mv, func=AF.Square),
           reads=['mixA%d' % tt, 'RT'], writes=['sqa'])

    def attn_stats(j, tt, first):
        for c in range(4):
            op('pe', lambda e, c=c: e.matmul(ps32(6)[:, tt * 128:(tt + 1) * 128], lhsT=onesb[:, :], rhs=sqa[:, c * 128:(c + 1) * 128],
                                              start=(first and c == 0), stop=(j == 3 and c == 3), skip_group_check=True),
               reads=['onesb', 'sqa', 'RT'], writes=['ps6'])

    def attn_unit(j, NT, first_group):
        hp_ = lambda t: not (first_group and t == 0)
        defer(j, lambda: attn_front(j, 0, hp_(0)))
        for tt in range(NT):
            def piece(tt=tt):
                if tt + 1 < NT:
                    attn_front(j, tt + 1, hp_(tt + 1))
                if tt > 0:
                    attn_stats(j, tt - 1, j == 0 and tt - 1 == 0)
                attn_back(j, tt, hp_(tt))
            defer(j, piece)
        defer(j, lambda: attn_stats(j, NT - 1, j == 0 and NT - 1 == 0))

    def attn_finish(NT):
        n = NT * 128
        op('act', lambda e: e.activation(out=rstd_a[:, 0:n], in_=ps32(6)[:, 0:n], func=AF.Ln, scale=1.0 / D, bias=epsc[:, :]),
           reads=['ps6', 'epsc', 'RT'], writes=['rstd_a'])
        op('act', lambda e: e.activation(out=rstd_a[:, 0:n], in_=rstd_a[:, 0:n], func=AF.Exp, scale=-0.5),
           reads=['rstd_a', 'RT'], writes=['rstd_a'])
        for c16 in range(16):
            eng = 'dve'
            op(eng, lambda e, c16=c16: e.scalar_tensor_tensor(out=mixT[:, c16, 0:n], in0=mixT[:, c16, 0:n], scalar=anorm[:, c16:c16 + 1],
                                                              in1=rstd_a[:, 0:n], op0=ALU.mult, op1=ALU.mult),
               reads=['rstd_a', 'anorm', 'RT'] + ['mixA%d' % t for t in range(NT)], writes=['mixA%d' % t for t in range(NT)])

    def conv_group(g, NT, seq_b, last_group):
        n = NT * 128
        for ci in range(6):
            ch = (4 * g + ci) if ci < 4 else (16 + g if ci == 4 else 20 + g)
            op('act', lambda e, ci=ci, ch=ch: e.activation(out=xpre[:, ci, 0:3], in_=hist[:, ch, :], func=AF.Identity),
               reads=['hist%d' % ch, 'RT'], writes=['xpre%d' % ci])
        chof = lambda ci: (4 * g + ci) if ci < 4 else (16 + g if ci == 4 else 20 + g)

        def tap0(ci):
            ch = chof(ci)
            acc = ctmp[:, ci % 2, 0:n]
            op('act', lambda e: e.activation(out=acc, in_=xpre[:, ci, 0:n], func=AF.Identity, scale=convw[:, 0, ch:ch + 1]),
               reads=['xpre%d' % ci, 'convw', 'RT'], writes=['ctmp%d' % (ci % 2)])
        tap0(0)
        tap0(1)
        for ci in range(6):
            ch = (4 * g + ci) if ci < 4 else (16 + g if ci == 4 else 20 + g)
            eng = 'dve'
            acc = ctmp[:, ci % 2, 0:n]
            an = 'ctmp%d' % (ci % 2)
            for k in range(1, 4):
                op(eng, lambda e, ci=ci, ch=ch, k=k, acc=acc: e.scalar_tensor_tensor(
                    out=acc, in0=xpre[:, ci, k:k + n], scalar=convw[:, k, ch:ch + 1], in1=acc, op0=ALU.mult, op1=ALU.add),
                   reads=['xpre%d' % ci, 'convw', an, 'RT'], writes=[an])
            dst = cur['xc'][:, ci, 0:n] if ci < 4 else cur['BCt'][:, ci - 4, 0:n]
            dn = (('xc%d' % ci) if ci < 4 else ('BCt%d' % (ci - 4))) + cur['sfx']
            op('act', lambda e, ch=ch, acc=acc, dst=dst: e.activation(out=dst, in_=acc, func=AF.Silu, bias=convb[:, ch:ch + 1]),
               reads=[an, 'convb', 'RT'], writes=[dn])
            op('act', lambda e, ci=ci, ch=ch: e.activation(out=hist[:, ch, :], in_=xpre[:, ci, n:n + 3], func=AF.Identity),
               reads=['xpre%d' % ci, 'RT'], writes=['hist%d' % ch])
            if ci + 2 < 6:
                tap0(ci + 2)
        if last_group:
            p5 = ps32(5)[0:3, :].rearrange("p (c n) -> p c n", c=4)
            p7 = ps32(7)[0:3, 0:256].rearrange("p (c n) -> p c n", c=2)
            for ci in range(6):
                tgt = p5[:, ci, :] if ci < 4 else p7[:, ci - 4, :]
                bn = 'ps5' if ci < 4 else 'ps7'
                op('pe', lambda e, ci=ci, tgt=tgt: e.transpose(out=tgt, in_=xpre[:, ci, n:n + 3], identity=ident_f),
                   reads=['xpre%d' % ci, 'cf', 'RT'], writes=[bn])
            op('act', lambda e: e.activation(out=cstage[0:3, 0:4, :], in_=p5, func=AF.Identity), reads=['ps5', 'ctmp0', 'ctmp1', 'RT'],
               writes=['ctmp0', 'ctmp1', 'cstage'])
            op('act', lambda e: e.activation(out=cstage[0:3, 4:6, :], in_=p7, func=AF.Identity), reads=['ps7', 'RT'], writes=['cstage2'])
            dma('sp', cp[seq_b, :, 512 * g:512 * g + 512].rearrange("r (c n) -> r c n", c=4), cstage[0:3, 0:4, :], ['cstage', 'ctmp0', 'ctmp1', 'RT'], [], 'cst')
            dma('sp', cp[seq_b, :, 2048 + 128 * g:2048 + 128 * g + 128], cstage[0:3, 4, :], ['cstage2', 'ctmp0', 'ctmp1', 'RT'], [], 'cst')
            dma('sp', cp[seq_b, :, 2560 + 128 * g:2560 + 128 * g + 128], cstage[0:3, 5, :], ['cstage2', 'ctmp0', 'ctmp1', 'RT'], [], 'cst')

    def ssd_tile(g, tt, first_tile, mask_ap=None, sample_hook=None):
        ssd_prep(g, tt, mask_ap)
        ssd_mid(g, tt, first_tile, sample_hook)
        ssd_back(g, tt)
        ssd_back_b(g, tt)

    def ssd_prep(g, tt, mask_ap=None):
        tsl = slice(tt * 128, (tt + 1) * 128)
        mk = maskb[:, 1, :] if mask_ap is None else mask_ap
        xc, BCt, sfx = cur['xc'], cur['BCt'], cur['sfx']
        op('pe', lambda e: e.matmul(ps32(4)[:, 0:128], lhsT=BCt[:, 0, tsl], rhs=BCt[:, 1, tsl], start=True, stop=True),
           reads=['BCt0' + sfx, 'BCt1' + sfx, 'RT'], writes=['ps4'])
        op('dve', lambda e: e.tensor_tensor(out=cbm, in0=ps32(4)[:, 0:128], in1=mk, op=ALU.mult), reads=['ps4', 'maskb', 'RT'], writes=['cbm'])
        Rv = dm[0:32, :, :]
        op('dve', lambda e: e.tensor_tensor(out=Rv, in0=aT_all[:, tt, :].unsqueeze(1).broadcast_to([32, 8, 128]),
                                            in1=ident_f[0:32, 8 * g:8 * g + 8].unsqueeze(2).broadcast_to([32, 8, 128]), op=ALU.mult),
           reads=['aT%d' % tt, 'cf', 'RT'], writes=['dm'])
        for hf in range(2):
            op('pe', lambda e, hf=hf: e.matmul(ps32(2 + hf), lhsT=ones_f[0:32, :], rhs=dm[0:32, 4 * hf:4 * hf + 4, :], start=True, stop=True),
               reads=['cf', 'dm', 'RT'], writes=[PSN[2 + hf]])
        for hf in range(2):
            op('act', lambda e, hf=hf: e.activation(out=EA[:, 4 * hf:4 * hf + 4, :], in_=ps32(2 + hf).rearrange("p (h l) -> p h l", h=4), func=AF.Exp),
               reads=[PSN[2 + hf], 'RT'], writes=['EA'])
        op('dve', lambda e: e.tensor_tensor(out=CTs, in0=EA, in1=BCt[:, 1, tsl].unsqueeze(1).broadcast_to([128, 8, 128]), op=ALU.mult),
           reads=['EA', 'BCt1' + sfx, 'RT'], writes=['CTs'])
        for hf in range(2):
            op('dve', lambda e, hf=hf: e.tensor_tensor(
                out=dm[:, 4 * hf:4 * hf + 4, :], in0=ps32(2 + hf).rearrange("p (h l) -> p h l", h=4),
                in1=a_all[:, tt, 8 * g + 4 * hf:8 * g + 4 * hf + 4].unsqueeze(2).broadcast_to([128, 4, 128]), op=ALU.subtract),
               reads=[PSN[2 + hf], 'a%d' % tt, 'RT'], writes=['dm'])
        op('dve', lambda e: e.tensor_scalar(out=dm, in0=dm, scalar1=0.0, scalar2=None, op0=ALU.min), reads=['dm', 'RT'], writes=['dm'])
        op('act', lambda e: e.activation(out=Eb2, in_=dm, func=AF.Exp), reads=['dm', 'RT'], writes=['Eb2'])
        op('dve', lambda e: e.tensor_tensor(out=MT, in0=Eb2, in1=cbm.unsqueeze(1).broadcast_to([128, 8, 128]), op=ALU.mult),
           reads=['Eb2', 'cbm', 'RT'], writes=['MT'])
        pT = ps16(5).rearrange("p (c q) -> p c q", c=8)
        for c in range(4):
            op('pe', lambda e, c=c: e.transpose(out=pT[:, c, :], in_=xc[:, c, tsl], identity=identb[:, :]),
               reads=['xc%d' % c + sfx, 'identb', 'RT'], writes=['ps5'])
        op('pe', lambda e: e.transpose(out=pT[:, 4, :], in_=BCt[:, 0, tsl], identity=identb[:, :]), reads=['BCt0' + sfx, 'identb', 'RT'], writes=['ps5'])
        for h2 in range(2):
            hs = slice(64 * h2, 64 * h2 + 64)
            dsl = bass.AP(tensor=dt_all[:, :, :].tensor, offset=dt_all[:, tt, 8 * g + h2:8 * g + h2 + 1].offset, ap=[list(dt_all[:, :, :].ap[0]), [2, 4], [0, 64]])
            wsl_ = bass.AP(tensor=w_all[:, :, :].tensor, offset=w_all[:, tt, 8 * g + h2:8 * g + h2 + 1].offset, ap=[list(w_all[:, :, :].ap[0]), [2, 4], [0, 64]])
            op('dve', lambda e, h2=h2, hs=hs, dsl=dsl: e.tensor_tensor(out=xdtp[:, :, h2, hs], in0=pT[:, 0:4, hs], in1=dsl, op=ALU.mult),
               reads=['ps5', 'dt%d' % tt, 'RT'], writes=['xdtp'])
            op('dve', lambda e, h2=h2, hs=hs, wsl_=wsl_: e.tensor_tensor(out=xw.rearrange("p (c q) -> p c q", c=4)[:, :, hs], in0=pT[:, 0:4, hs], in1=wsl_, op=ALU.mult),
               reads=['ps5', 'w%d_' % tt, 'RT'], writes=['xw'])
        op('act', lambda e: e.activation(out=Btok, in_=pT[:, 4, :], func=AF.Identity), reads=['ps5', 'RT'], writes=['Btok'])

    def ssd_mid(g, tt, first_tile, sample_hook=None):
        tsl = slice(tt * 128, (tt + 1) * 128)
        Y = ps32(6).rearrange("p (c l) -> p c l", c=4)
        for c in range(4):
            seq = []
            for h2 in range(2):
                seq.append(('intra', h2))
                if not first_tile:
                    seq.append(('inter', h2))
            for idx, (kind, h2) in enumerate(seq):
                hl = 2 * c + h2
                if kind == 'intra':
                    op('pe', lambda e, c=c, h2=h2, hl=hl, idx=idx, ns=len(seq): e.matmul(
                        Y[:, c, :], lhsT=xdtp[:, c, h2, :], rhs=MT[:, hl, :], start=(idx == 0 and (sample_hook is None or c == 0)), stop=(idx == ns - 1),
                        skip_group_check=(sample_hook is not None)),
                       reads=['xdtp', 'MT', 'RT'], writes=['ps6'])
                else:
                    op('pe', lambda e, c=c, h2=h2, hl=hl, idx=idx, ns=len(seq): e.matmul(
                        Y[:, c, :], lhsT=hpad[:, 8 * g + hl, :], rhs=CTs[:, hl, :], start=(idx == 0), stop=(idx == ns - 1)),
                       reads=['hpad%d' % g, 'CTs', 'RT'], writes=['ps6'])
        if sample_hook is not None:
            sample_hook(Y)
        if sample_hook is None:
            op('pe', lambda e: e.matmul(ps32(7), lhsT=Btok, rhs=xw, start=True, stop=True), reads=['Btok', 'xw', 'RT'], writes=['ps7'])
        hsv = hstate[:, 512 * g:512 * g + 512]
        if sample_hook is not None:
            pass
        elif first_tile:
            op('act', lambda e: e.activation(out=hsv, in_=ps32(7), func=AF.Identity), reads=['ps7'], writes=['hst%d' % g])
        else:
            op('dve', lambda e: e.tensor_tensor(out=hsv.rearrange("p (h q) -> p h q", h=8), in0=hsv.rearrange("p (h q) -> p h q", h=8),
                                                 in1=cd_all[:, tt, 8 * g:8 * g + 8].unsqueeze(2).broadcast_to([128, 8, 64]), op=ALU.mult),
               reads=['hst%d' % g, 'cd%d' % tt], writes=['hst%d' % g])
            op('dve', lambda e: e.tensor_tensor(out=hsv, in0=hsv, in1=ps32(7), op=ALU.add), reads=['hst%d' % g, 'ps7'], writes=['hst%d' % g])
        hv = hstate[:, 512 * g:512 * g + 512].rearrange("p (c e q) -> p c e q", c=4, e=2)
        hpv = hpad[:, 8 * g:8 * g + 8, :].rearrange("p (c e) q -> p c e q", c=4)
        for h2 in range(2):
            if sample_hook is not None:
                break
            hs = slice(64 * h2, 64 * h2 + 64)
            op('act', lambda e, h2=h2, hs=hs: e.activation(out=hpv[:, :, h2, hs], in_=hv[:, :, h2, :], func=AF.Identity),
               reads=['hst%d' % g], writes=['hpad%d' % g])

    def ssd_back(g, tt):
        tsl = slice(tt * 128, (tt + 1) * 128)
        Y = ps32(6).rearrange("p (c l) -> p c l", c=4)
        ydv = yd.rearrange("p (c l) -> p c l", c=4)
        xc, zs, sfx = cur['xc'], cur['zs'], cur['sfx']
        for c in range(4):
            op('act', lambda e, c=c: e.activation(out=ydv[:, c, :], in_=xc[:, c, tsl], func=AF.Identity, scale=Dcol[:, 4 * g + c:4 * g + c + 1]),
               reads=['xc%d' % c + sfx, 'Dcol', 'RT'], writes=['yd'])
        op('dve', lambda e: e.tensor_tensor(out=ydv, in0=ydv, in1=Y, op=ALU.add), reads=['yd', 'ps6', 'RT'], writes=['yd'])

    def ssd_back_b(g, tt):
        tsl = slice(tt * 128, (tt + 1) * 128)
        ydv = yd.rearrange("p (c l) -> p c l", c=4)
        xc, zs, sfx = cur['xc'], cur['zs'], cur['sfx']
        op('dve', lambda e: e.tensor_tensor(out=ydv, in0=ydv, in1=zs[:, :, tsl], op=ALU.mult), reads=['yd', 'zs' + sfx, 'RT'], writes=['yd'])
        op('act', lambda e: e.activation(out=sqs, in_=yd, func=AF.Square), reads=['yd', 'RT'], writes=['sqs'])
        for c in range(4):
            op('pe', lambda e, c=c: e.matmul(ps32(4)[:, 128:256], lhsT=onesb[:, :], rhs=sqs[:, c * 128:(c + 1) * 128], start=(c == 0), stop=(c == 3)),
               reads=['onesb', 'sqs', 'RT'], writes=['ps4'])
        op('act', lambda e: e.activation(out=rg, in_=ps32(4)[:, 128:256], func=AF.Ln, scale=1.0 / 512, bias=epsc[:, :]), reads=['ps4', 'epsc', 'RT'], writes=['rg'])
        op('act', lambda e: e.activation(out=rg, in_=rg, func=AF.Exp, scale=-0.5), reads=['rg', 'RT'], writes=['rg'])
        op('dve', lambda e: e.tensor_tensor(out=ydv, in0=ydv, in1=rg.unsqueeze(1).broadcast_to([128, 4, 128]), op=ALU.mult), reads=['yd', 'rg', 'RT'], writes=['yd'])
        for c in range(4):
            op('act', lambda e, c=c: e.activation(out=mixT[:, 16 + 4 * g + c, tsl], in_=ydv[:, c, :], func=AF.Identity, scale=snorm[:, 4 * g + c:4 * g + c + 1]),
               reads=['yd', 'snorm', 'RT'], writes=['mixS%d' % tt])

    def ssm_out(g, dst):
        pv = ps32(5).rearrange("p (c n) -> p c n", c=4)
        for c in range(4):
            op('pe', lambda e, c=c: e.transpose(out=pv[:, c, :], in_=hstate[:, 512 * g + 128 * c:512 * g + 128 * c + 128], identity=ident_f),
               reads=['hst%d' % g, 'cf'], writes=['ps5'])
        op('act', lambda e: e.activation(out=hstage, in_=pv, func=AF.Identity), reads=['ps5', 'dm', 'RT'], writes=['dm', 'hstage'])
        dma('sp', dst[8 * g:8 * g + 8].rearrange("(c e) p n -> (e p) c n", e=2), hstage, ['hstage', 'dm', 'RT'], [], 'hst_o')

    def post_tile(tt, xsrc, ydst):
        dma('sp', xt[:, :], xsrc[tt * 128:(tt + 1) * 128, :], [], ['xt'], 'xld')
        op('dve', lambda e: e.tensor_reduce(out=stt[:, 4:5], in_=ssq[:, tt, :], axis=AX.X, op=ALU.add), reads=['ssq%d' % tt], writes=['stt'])
        op('act', lambda e: e.activation(out=stt[:, 5:6], in_=stt[:, 4:5], func=AF.Ln, scale=1.0 / D, bias=epsc[:, :]), reads=['stt', 'epsc'], writes=['stt'])
        op('act', lambda e: e.activation(out=stt[:, 6:7], in_=stt[:, 5:6], func=AF.Exp, scale=-0.5), reads=['stt'], writes=['stt'])
        op('dve', lambda e: e.scalar_tensor_tensor(out=outacc[:, tt, :], in0=outacc[:, tt, :], scalar=stt[:, 6:7], in1=gpost_b[:, :], op0=ALU.mult, op1=ALU.mult),
           reads=['oacc%d' % tt, 'stt', 'gpost_b', 'RT'], writes=['oacc%d' % tt])
        op('dve', lambda e: e.tensor_tensor(out=outacc[:, tt, :], in0=outacc[:, tt, :], in1=xt[:, :], op=ALU.add),
           reads=['oacc%d' % tt, 'xt', 'RT'], writes=['oacc%d' % tt])
        dma('sp', ydst[tt * 128:(tt + 1) * 128, :], outacc[:, tt, :], ['oacc%d' % tt, 'RT'], [], 'yst%d' % tt)


    def sample_attn(j):
        for dup in range(2):
            dma('pool', Kc[:, :, dup, :], ck[:, :, 64 * j:64 * j + 64].rearrange("b s d -> s b d"), ['RT'], ['Kc'], 'sk0')
        dma('pool', Vc_pad[:, :, 0, 0:64], cv[:, :, 64 * j:64 * j + 64].rearrange("b s d -> s b d"), ['RT'], ['Vc_pad'], 'sk1')
        dma('pool', Vc_pad[:, :, 1, 64:128], cv[:, :, 64 * j:64 * j + 64].rearrange("b s d -> s b d"), ['RT'], ['Vc_pad'], 'sk1')
        dma('pool', Vn_pad[0:8, :, 0, 0:64], vscr[:, 64 * j:64 * j + 64].rearrange("(b t) d -> t b d", t=8), ['RT', 'vscr'], ['Vn_pad'], 'sk2')
        dma('pool', Vn_pad[0:8, :, 1, 64:128], vscr[:, 64 * j:64 * j + 64].rearrange("(b t) d -> t b d", t=8), ['RT', 'vscr'], ['Vn_pad'], 'sk2')
        for half in range(2):
            bi = 2 + half
            pv = ps16(bi).rearrange("p (b s) -> p b s", b=8)
            for b8 in range(8):
                b = half * 8 + b8
                op('pe', lambda e, b=b, b8=b8, pv=pv: e.transpose(out=pv[:, b8, :], in_=Kc[:, b, :, :].rearrange("p e d -> p (e d)"), identity=identb[:, :]),
                   reads=['Kc', 'identb', 'RT'], writes=[PSN[bi]])
            op('act', lambda e, half=half, pv=pv: e.activation(out=KTc[:, half * 8:half * 8 + 8, :], in_=pv, func=AF.Identity),
               reads=[PSN[bi], 'RT'], writes=['KTc'])
        for b in range(16):
            for h2 in range(2):
                hs = slice(64 * h2, 64 * h2 + 64)
                qv = QTu[hs, 0, :, 8 * b:8 * b + 8]
                op('pe', lambda e, b=b, h2=h2, hs=hs, qv=qv: e.matmul(ps32(2 + h2)[:, 32 * b:32 * b + 32], lhsT=KTc[hs, b, :], rhs=qv, start=True, stop=True),
                   reads=['KTc', 'QTu', 'RT'], writes=[PSN[2 + h2]])
                op('pe', lambda e, b=b, h2=h2, hs=hs, qv=qv: e.matmul(ps32(4 + h2)[0:8, 32 * b:32 * b + 32], lhsT=KT[hs, j, 128 + 8 * b:128 + 8 * b + 8], rhs=qv, start=True, stop=True),
                   reads=['KT%d' % j, 'QTu', 'RT'], writes=[PSN[4 + h2]])
        for h2 in range(2):
            op('act', lambda e, h2=h2: e.activation(out=Eb[:, 2 * h2, :], in_=ps32(2 + h2), func=AF.Exp, scale=0.125),
               reads=[PSN[2 + h2], 'RT'], writes=['Eb%d' % (2 * h2)])
            op('act', lambda e, h2=h2: e.activation(out=Eb[0:8, 2 * h2 + 1, :], in_=ps32(4 + h2)[0:8, :], func=AF.Exp, scale=0.125),
               reads=[PSN[4 + h2], 'RT'], writes=['Eb%d' % (2 * h2 + 1)])
            op('dve', lambda e, h2=h2: e.tensor_tensor(out=Eb[:, 2 * h2, :].rearrange("p (g t) -> p g t", t=8), in0=Eb[:, 2 * h2, :].rearrange("p (g t) -> p g t", t=8),
                                                       in1=maskb[:, 0, 0:8].unsqueeze(1).broadcast_to([128, 64, 8]), op=ALU.mult),
               reads=['Eb%d' % (2 * h2), 'maskb', 'RT'], writes=['Eb%d' % (2 * h2)])
            op('dve', lambda e, h2=h2: e.tensor_tensor(out=Eb[0:8, 2 * h2 + 1, :].rearrange("p (g t) -> p g t", t=8), in0=Eb[0:8, 2 * h2 + 1, :].rearrange("p (g t) -> p g t", t=8),
                                                        in1=maskb[0:8, 1, 0:8].unsqueeze(1).broadcast_to([8, 64, 8]), op=ALU.mult),
               reads=['Eb%d' % (2 * h2 + 1), 'maskb', 'RT'], writes=['Eb%d' % (2 * h2 + 1)])
        for (bank, vp_, vn_, nm) in ((7, Vc_pad, Vn_pad, 'pv'), (2, None, None, 'den')):
            for b in range(16):
                cs = slice(32 * b, 32 * b + 32)
                idx = 0
                for h2 in range(2):
                    for kb in range(2):
                        if nm == 'pv':
                            lh = vp_[:, b, h2, :] if kb == 0 else vn_[0:8, b, h2, :]
                        else:
                            lh = onespad[:, h2, :] if kb == 0 else onespad[0:8, h2, :]
                        rh = Eb[:, 2 * h2, cs] if kb == 0 else Eb[0:8, 2 * h2 + 1, cs]
                        op('pe', lambda e, bank=bank, cs=cs, lh=lh, rh=rh, idx=idx: e.matmul(ps32(bank)[:, cs], lhsT=lh, rhs=rh, start=(idx == 0), stop=(idx == 3)),
                           reads=['Vc_pad', 'Vn_pad', 'onespad', 'Eb%d' % (2 * h2 + kb), 'RT'], writes=[PSN[bank]])
                        idx += 1
        rdv = rd.rearrange("p (b c t) -> p b c t", b=16, c=4)
        dnv = ps32(2).rearrange("p (b c t) -> p b c t", b=16, c=4)
        for c in range(4):
            op('dve', lambda e, c=c: e.tensor_scalar(out=rdv[:, :, c, :], in0=dnv[:, :, c, :], scalar1=Ecol[:, 4 * j + c:4 * j + c + 1], scalar2=None, op0=ALU.add),
               reads=['ps2', 'Ecol', 'RT'], writes=['rd'])
        op('act', lambda e: e.activation(out=rd, in_=rd, func=AF.Ln), reads=['rd', 'RT'], writes=['rd'])
        op('act', lambda e: e.activation(out=rd, in_=rd, func=AF.Exp, scale=-1.0), reads=['rd', 'RT'], writes=['rd'])
        op('dve', lambda e: e.tensor_tensor(out=o32, in0=ps32(7), in1=rd, op=ALU.mult), reads=['ps7', 'rd', 'RT'], writes=['o32'])
        mv = mixT[:, 4 * j:4 * j + 4, 0:128]
        op('dve', lambda e: e.tensor_tensor(out=mv.rearrange("p c (b t) -> p b c t", t=8), in0=o32.rearrange("p (b c t) -> p b c t", b=16, c=4),
                                             in1=gs[:, :, 0:128].rearrange("p c (b t) -> p b c t", t=8), op=ALU.mult),
           reads=['o32', 'gs', 'RT'], writes=['mixA0'])
        op('act', lambda e: e.activation(out=sqa.rearrange("p (c q) -> p c q", c=4), in_=mv, func=AF.Square), reads=['mixA0', 'RT'], writes=['sqa'])
        for c in range(4):
            op('pe', lambda e, c=c: e.matmul(ps32(6)[:, 0:128], lhsT=onesb[:, :], rhs=sqa[:, c * 128:(c + 1) * 128], start=(j == 0 and c == 0), stop=(j == 3 and c == 3), skip_group_check=True),
               reads=['onesb', 'sqa', 'RT'], writes=['ps6'])

    def sample_conv(g):
        n = 128
        scv = sconv.rearrange("b k c -> (b k) c")
        dma('sp', cst_in[0:48, 0:4, :], scv[:, 512 * g:512 * g + 512].rearrange("r (c n) -> r c n", c=4), ['RT'], ['cst_in'], 'sk3')
        dma('sp', cst_in[0:48, 4, :], scv[:, 2048 + 128 * g:2048 + 128 * g + 128], ['RT'], ['cst_in'], 'sk3')
        dma('sp', cst_in[0:48, 5, :], scv[:, 2560 + 128 * g:2560 + 128 * g + 128], ['RT'], ['cst_in'], 'sk3')
        ph = ps32(5)[:, 0:288].rearrange("p (c r) -> p c r", c=6)
        for ci in range(6):
            op('pe', lambda e, ci=ci: e.transpose(out=ph[:, ci, :], in_=cst_in[0:48, ci, :], identity=ident_f[0:48, 0:48]),
               reads=['cst_in', 'cf', 'RT'], writes=['ps5'])
        op('act', lambda e: e.activation(out=xpre_s[:, :, :, 0:3], in_=ph.rearrange("p c (b k) -> p c b k", k=3), func=AF.Identity),
           reads=['ps5', 'RT'], writes=['xpre%d' % ci for ci in range(6)])
        for ci in range(6):
            ch = (4 * g + ci) if ci < 4 else (16 + g if ci == 4 else 20 + g)
            acc = ctmp[:, ci % 2, 0:n].rearrange("p (b t) -> p b t", t=8)
            an = 'ctmp%d' % (ci % 2)
            op('dve', lambda e, ci=ci, ch=ch, acc=acc: e.tensor_scalar(out=acc, in0=xpre_s[:, ci, :, 0:8], scalar1=convw[:, 0, ch:ch + 1], scalar2=None, op0=ALU.mult),
               reads=['xpre%d' % ci, 'convw', 'RT'], writes=[an])
            for k in range(1, 4):
                op('dve', lambda e, ci=ci, ch=ch, k=k, acc=acc: e.scalar_tensor_tensor(
                    out=acc, in0=xpre_s[:, ci, :, k:k + 8], scalar=convw[:, k, ch:ch + 1], in1=acc, op0=ALU.mult, op1=ALU.add),
                   reads=['xpre%d' % ci, 'convw', an, 'RT'], writes=[an])
            dst = xc[:, ci, 0:n] if ci < 4 else BCt[:, ci - 4, 0:n]
            dn = ('xc%d' % ci) if ci < 4 else ('BCt%d' % (ci - 4))
            op('act', lambda e, ch=ch, ci=ci, dst=dst: e.activation(out=dst, in_=ctmp[:, ci % 2, 0:n], func=AF.Silu, bias=convb[:, ch:ch + 1]),
               reads=[an, 'convb', 'RT'], writes=[dn])
        op('act', lambda e: e.activation(out=cst_o.rearrange("p c (b k) -> p c b k", k=3), in_=xpre_s[:, :, :, 8:11], func=AF.Identity),
           reads=['xpre%d' % ci for ci in range(6)] + ['RT'], writes=['cst_o'])
        p5 = ps32(5)[0:48, :].rearrange("p (c n) -> p c n", c=4)
        p7 = ps32(7)[0:48, 0:256].rearrange("p (c n) -> p c n", c=2)
        for ci in range(6):
            tgt = p5[:, ci, :] if ci < 4 else p7[:, ci - 4, :]
            bn = 'ps5' if ci < 4 else 'ps7'
            op('pe', lambda e, ci=ci, tgt=tgt: e.transpose(out=tgt, in_=cst_o[:, ci, :], identity=ident_f), reads=['cst_o', 'cf', 'RT'], writes=[bn])
        op('act', lambda e: e.activation(out=cstage[0:48, 0:4, :], in_=p5, func=AF.Identity), reads=['ps5', 'ctmp0', 'ctmp1', 'RT'], writes=['ctmp0', 'ctmp1', 'cstage'])
        op('act', lambda e: e.activation(out=cstage[0:48, 4:6, :], in_=p7, func=AF.Identity), reads=['ps7', 'RT'], writes=['cstage2'])
        cso = csm.rearrange("b k c -> (b k) c")
        dma('sp', cso[:, 512 * g:512 * g + 512].rearrange("r (c n) -> r c n", c=4), cstage[0:48, 0:4, :], ['cstage', 'ctmp0', 'ctmp1', 'RT'], [], 'so1')
        dma('sp', cso[:, 2048 + 128 * g:2048 + 128 * g + 128], cstage[0:48, 4, :], ['cstage2', 'ctmp0', 'ctmp1', 'RT'], [], 'so1')
        dma('sp', cso[:, 2560 + 128 * g:2560 + 128 * g + 128], cstage[0:48, 5, :], ['cstage2', 'ctmp0', 'ctmp1', 'RT'], [], 'so1')

    def sample_ssd_hook(g):
        def hook(Y):
            op('dve', lambda e: e.tensor_tensor(out=Bm, in0=Btok.unsqueeze(1).broadcast_to([128, 16, 128]),
                                                in1=cf[:, 640:656].unsqueeze(2).broadcast_to([128, 16, 128]), op=ALU.mult),
               reads=['Btok', 'cf', 'RT'], writes=['Bm'])
            aTl = aT_all[:, 0, :].rearrange("p (b t) -> p b t", t=8)[:, :, 7]
            for c in range(4):
                cg = 4 * g + c
                R2 = Rm[0:32, 0:32].rearrange("p (e b) -> p e b", e=2)
                op('dve', lambda e, cg=cg, R2=R2: e.tensor_tensor(out=R2, in0=aTl.unsqueeze(1).broadcast_to([32, 2, 16]),
                                                                  in1=ident_f[0:32, 2 * cg:2 * cg + 2].unsqueeze(2).broadcast_to([32, 2, 16]), op=ALU.mult),
                   reads=['aT0', 'cf', 'RT'], writes=['Rm'])
                for h2 in range(2):
                    op('pe', lambda e, h2=h2, R2=R2: e.matmul(ps32(4)[:, 256:272], lhsT=onespad_f[:, h2, :], rhs=R2[:, h2, :], start=(h2 == 0), stop=(h2 == 1)),
                       reads=['onespad_f', 'Rm', 'RT'], writes=['ps4'])
                op('act', lambda e: e.activation(out=Rm[:, 64:80], in_=ps32(4)[:, 256:272], func=AF.Exp), reads=['ps4', 'RT'], writes=['cdT'])
                for hb in range(2):
                    bs = slice(8 * hb, 8 * hb + 8)
                    src = sssm[bs, 2 * cg:2 * cg + 2].rearrange("b e p n -> (e p) b n")
                    dma('sp', h0f, src, ['RT'], ['h0f'], 'sk4')
                    dma('pool', h0b, src, ['RT'], ['h0b'], 'sk5')
                    pv = ps16(3).rearrange("p (b q) -> p b q", b=8)
                    for b8 in range(8):
                        op('pe', lambda e, b8=b8, pv=pv: e.transpose(out=pv[:, b8, :], in_=h0b[:, b8, :], identity=identb[:, :]),
                           reads=['h0b', 'identb', 'RT'], writes=['ps3'])
                    op('act', lambda e, pv=pv: e.activation(out=hps[:, :, 0, 0:64], in_=pv[:, :, 0:64], func=AF.Identity), reads=['ps3', 'RT'], writes=['hps'])
                    op('dve', lambda e, pv=pv: e.tensor_copy(out=hps[:, :, 1, 64:128], in_=pv[:, :, 64:128]), reads=['ps3', 'RT'], writes=['hps'])
                    for b8 in range(8):
                        b = 8 * hb + b8
                        for h2 in range(2):
                            op('pe', lambda e, c=c, b=b, b8=b8, h2=h2: e.matmul(Y[:, c, 8 * b:8 * b + 8], lhsT=hps[:, b8, h2, :], rhs=CTs[:, 2 * c + h2, 8 * b:8 * b + 8],
                                                                             start=False, stop=(h2 == 1), skip_group_check=True),
                               reads=['hps', 'CTs', 'RT'], writes=['ps6'])
                    for q in range(2):
                        op('pe', lambda e, c=c, hb=hb, q=q: e.matmul(ps32(q), lhsT=xw[:, c * 128:(c + 1) * 128],
                                                                    rhs=Bm[:, 8 * hb + 4 * q:8 * hb + 4 * q + 4, :], start=True, stop=True),
                           reads=['xw', 'Bm', 'RT'], writes=[PSN[q]])
                    op('dve', lambda e, hb=hb: e.tensor_tensor(out=h0f, in0=h0f, in1=Rm[:, 64 + 8 * hb:64 + 8 * hb + 8].unsqueeze(2).broadcast_to([128, 8, 128]), op=ALU.mult),
                       reads=['h0f', 'cdT', 'RT'], writes=['h0f'])
                    for q in range(2):
                        op('dve', lambda e, q=q: e.tensor_tensor(out=h0f[:, 4 * q:4 * q + 4, :], in0=h0f[:, 4 * q:4 * q + 4, :],
                                                                 in1=ps32(q).rearrange("p (b n) -> p b n", b=4), op=ALU.add),
                           reads=['h0f', PSN[q], 'RT'], writes=['h0f'])
                    dma('sp', hsm[bs, 2 * cg:2 * cg + 2].rearrange("b e p n -> (e p) b n"), h0f, ['h0f', 'RT'], [], 'so2')
        return hook

    def add_sample_group():
        NT = 1
        n = 128
        gi = len(grp_x)
        grp_x.append((xsm, NT))

        def ld_kd(slot):
            for j in range(4):
                for dup in range(2):
                    wload((C_K + 64 * j, 64), slot, j * 128 + dup * 64)

        def cp_kd(slot):
            fence('A')
            cur.update(QTu=QTu, gs=gs, zs=zs, xc=xc, BCt=BCt, sfx='')
            op('dve', lambda e: e.memset(Vc_pad, 0.0), reads=['RT'], writes=['Vc_pad'])
            op('dve', lambda e: e.memset(Vn_pad, 0.0), reads=['RT'], writes=['Vn_pad'])
            for b in range(16):
                dma('sp', ksm[b, 0:120, :], ck[b, 8:128, :], [], [], 'so3')
                dma('sp', vsm[b, 0:120, :], cv[b, 8:128, :], [], [], 'so3')
            if gi == 0:
                phase0(xsm, NT)
            for j in range(4):
                inproj_chunk(slot, j, n, lambda pa, bn, j=j: op('act', lambda e: e.activation(out=KT[:, j, 128:128 + n], in_=pa, func=AF.Identity),
                                                              reads=[bn], writes=['KT%d' % j]))
        tasks.append((ld_kd, cp_kd))

        def ld_vdt(slot):
            wload((C_V, 256), slot, 0)
            wload((C_DT, 32), slot, 256)

        def cp_vdt(slot):
            phase1_tile(slot, 0, NT, True, False, 0, tri=cf[:, 384:512], blk=cf[:, 512:640], sample=True)
        tasks.append((ld_vdt, cp_vdt))

        def ld_kt(slot):
            wload((C_K, 256), slot, 0)

        def cp_kt(slot):
            w = wsl[slot]
            bi = pbank[0] % 2
            pbank[0] += 1
            for k in range(16):
                op('pe', lambda e, k=k: e.matmul(ps32(bi)[:, 0:256], lhsT=hT[:, k, 0:128], rhs=w[:, k, 0:256], start=(k == 0), stop=(k == 15)),
                   reads=['w%d' % slot, 'hT'], writes=[PSN[bi]])
            op('dve', lambda e: e.tensor_copy(out=kstage, in_=ps32(bi)[:, 0:256]), reads=[PSN[bi], 'RT'], writes=['kstage'])
            for b in range(16):
                dma('sp', ksm[b, 120:128, :], kstage[8 * b:8 * b + 8, :], ['kstage', 'RT'], [], 'so4')
        tasks.append((ld_kt, cp_kt))

        for j in range(4):
            def ld_q(slot, j=j):
                wload((C_Q + 512 * j, 512), slot)

            def cp_q(slot, j=j):
                for c in range(4):
                    inproj_chunk(slot, c, n, lambda pa, bn, c=c: op('act', lambda e: e.activation(out=QTu[:, 0, c, :], in_=pa, func=AF.Identity),
                                                                    reads=[bn, 'RT'], writes=['QTu']))
            tasks.append((ld_q, cp_q))

            def ld_ga(slot, j=j):
                wload((C_GA + 512 * j, 512), slot)

            def cp_ga(slot, j=j):
                for c in range(4):
                    inproj_chunk(slot, c, n, lambda pa, bn, c=c: op('act', lambda e: e.activation(out=gs[:, c, 0:n], in_=pa, func=AF.Silu),
                                                                    reads=[bn, 'RT'], writes=['gs']))
                sample_attn(j)
                if j == 3:
                    attn_finish(NT)
            tasks.append((ld_ga, cp_ga))

        for g in range(4):
            def ld_z(slot, g=g):
                wload((C_Z + 512 * g, 512), slot)

            def cp_z(slot, g=g):
                if g == 0:
                    fence('S')
                    op('dve', lambda e: e.memset(xdtp, 0.0), reads=['RT'], writes=['xdtp'])
                    op('dve', lambda e: e.memset(hps, 0.0), reads=['RT'], writes=['hps'])
                for c in range(4):
                    inproj_chunk(slot, c, n, lambda pa, bn, c=c: op('act', lambda e: e.activation(out=zs[:, c, 0:n], in_=pa, func=AF.Silu),
                                                                    reads=[bn, 'RT'], writes=['zs']))
            tasks.append((ld_z, cp_z))

            def ld_x(slot, g=g):
                wload((C_X + 512 * g, 512), slot)

            def cp_x(slot, g=g):
                for c in range(4):
                    inproj_chunk(slot, c, n, lambda pa, bn, c=c: op('act', lambda e: e.activation(
                        out=xpre_s[:, c, :, 3:11], in_=pa.rearrange("p (b t) -> p b t", t=8), func=AF.Identity), reads=[bn, 'RT'], writes=['xpre%d' % c]))
            tasks.append((ld_x, cp_x))

            def ld_bc(slot, g=g):
                wload((C_B + 128 * g, 128), slot, 0)
                wload((C_C + 128 * g, 128), slot, 128)

            def cp_bc(slot, g=g):
                for c in range(2):
                    inproj_chunk(slot, c, n, lambda pa, bn, c=c: op('act', lambda e: e.activation(
                        out=xpre_s[:, 4 + c, :, 3:11], in_=pa.rearrange("p (b t) -> p b t", t=8), func=AF.Identity), reads=[bn, 'RT'], writes=['xpre%d' % (4 + c)]))
                sample_conv(g)
                ssd_tile(g, 0, True, mask_ap=triub_blk[:, :], sample_hook=sample_ssd_hook(g))
            tasks.append((ld_bc, cp_bc))

        for cb in range(4):
            for kh in range(2):
                def ld_o(slot, cb=cb, kh=kh):
                    wload_rows(kh * 2048, cb * 512, slot)

                def cp_o(slot, cb=cb, kh=kh):
                    if cb == 0 and kh == 0:
                        fence('O')
                        dma('sp', gpost_b, bcast_dram(norm_post, D), ['RT'], ['gpost_b'], 'c10')
                    w = wsl[slot]
                    for k in range(16):
                        kk = kh * 16 + k
                        mn = ['mixA0'] if kk < 16 else ['mixS0']
                        op('pe', lambda e, k=k, kk=kk: e.matmul(ps32(2), lhsT=mixT[:, kk, 0:128], rhs=w[:, k, :], start=(kk == 0), stop=(kk == 31)),
                           reads=['w%d' % slot] + mn, writes=[PSN[2]])
                    if kh == 1:
                        op('act', lambda e: e.activation(out=outacc[:, 0, cb * 512:(cb + 1) * 512], in_=ps32(2), func=AF.Identity),
                           reads=[PSN[2], 'RT'], writes=['oacc0'])
                        op('act', lambda e: e.activation(out=ojunk, in_=ps32(2), func=AF.Square, accum_out=ssq[:, 0, cb:cb + 1]),
                           reads=[PSN[2], 'RT'], writes=['ojunk', 'ssq0'])
                        if cb == 3:
                            post_tile(0, xsm, ysm)
                tasks.append((ld_o, cp_o))

    def add_prompt_group(b, G):
        NT = 4
        n = 512
        first_group = (G == 0)
        last_group = (G == NG - 1)
        xsrc = xp[b * SEQ + G * 512: b * SEQ + (G + 1) * 512, :]
        ydst = yp[b * SEQ + G * 512: b * SEQ + (G + 1) * 512, :]
        gi = len(grp_x)
        grp_x.append((xsrc, NT))

        def ld_kd(slot):
            w = wsl[slot]
            for j in range(4):
                for dup in range(2):
                    wload((C_K + 64 * j, 64), slot, j * 128 + dup * 64)

        def cp_kd(slot):
            fence('A')
            if first_group:
                op('dve', lambda e: e.memset(hist[:, :, :], 0.0), reads=[], writes=['hist%d' % c for c in range(24)])
            if gi == 0:
                phase0(xsrc, NT)
            for j in range(4):
                inproj_chunk(slot, j, n, lambda pa, bn, j=j: op('act', lambda e: e.activation(out=KT[:, j, 128:128 + n], in_=pa, func=AF.Identity),
                                                              reads=[bn], writes=['KT%d' % j]))
        tasks.append((ld_kd, cp_kd))

        def ld_vdt(slot):
            wload((C_V, 256), slot, 0)
            wload((C_DT, 32), slot, 256)

        def cp_vdt(slot):
            for tt in range(NT):
                phase1_tile(slot, tt, NT, first_group and tt == 0, last_group and tt == NT - 1, b)
        tasks.append((ld_vdt, cp_vdt))
        if last_group:
            def ld_kt(slot):
                wload((C_K, 256), slot, 0)

            def cp_kt(slot):
                w = wsl[slot]
                tt = NT - 1
                bi = pbank[0] % 2
                pbank[0] += 1
                for k in range(16):
                    op('pe', lambda e, k=k: e.matmul(ps32(bi)[:, 0:256], lhsT=hT[:, k, tt * 128:(tt + 1) * 128], rhs=w[:, k, 0:256], start=(k == 0), stop=(k == 15)),
                       reads=['w%d' % slot, 'hT'], writes=[PSN[bi]])
                op('dve', lambda e: e.tensor_copy(out=kstage, in_=ps32(bi)[:, 0:256]), reads=[PSN[bi], 'RT'], writes=['kstage'])
                dma('sp', kp[b], kstage, ['kstage', 'RT'], [], 'kst')
            tasks.append((ld_kt, cp_kt))

        for j in range(4):
            def ld_q(slot, j=j):
                wload((C_Q + 512 * j, 512), slot)

            def cp_q(slot, j=j):
                drain(None, older_than=j - 1)
                cur.update(QTu=[QTu, QTuB][j % 2], gs=[gs, gsB][j % 2], sfx='ab'[j % 2])
                Q_, sfx = cur['QTu'], cur['sfx']
                for c in range(4):
                    inproj_chunk(slot, c, n, lambda pa, bn, c=c: op('act', lambda e: e.activation(
                        out=Q_[:, :, c, :], in_=pa.rearrange("p (t q) -> p t q", t=4), func=AF.Identity), reads=[bn, 'RT'], writes=['QTu' + sfx]))
            tasks.append((ld_q, cp_q))

            def ld_ga(slot, j=j):
                wload((C_GA + 512 * j, 512), slot)

            def cp_ga(slot, j=j):
                G_, sfx = cur['gs'], cur['sfx']
                for c in range(4):
                    inproj_chunk(slot, c, n, lambda pa, bn, c=c: op('act', lambda e: e.activation(out=G_[:, c, :], in_=pa, func=AF.Identity),
                                                                    reads=[bn, 'RT'], writes=['gs' + sfx]))
                op('act', lambda e: e.activation(out=G_[:, :, :], in_=G_[:, :, :], func=AF.Silu), reads=['gs' + sfx, 'RT'], writes=['gs' + sfx])
                attn_unit(j, NT, first_group)
                if j == 3:
                    drain()
                    attn_finish(NT)
                    op('act', lambda e: e.activation(out=KT[:, :, 0:128], in_=KT[:, :, 512:640], func=AF.Identity), reads=['KT%d' % jj for jj in range(4)],
                       writes=['KT%d' % jj for jj in range(4)])
                    op('dve', lambda e: e.tensor_copy(out=Vpad[:, 0, :, :, :], in_=Vpad[:, 4, :, :, :]), reads=['Vpad4'], writes=['Vpad0'])
            tasks.append((ld_ga, cp_ga))

        for g in range(4):
            def ld_z(slot, g=g):
                wload((C_Z + 512 * g, 512), slot)

            def cp_z(slot, g=g):
                if g == 0:
                    fence('S')
                    op('dve', lambda e: e.memset(xdtp, 0.0), reads=['RT'], writes=['xdtp'])
                drain(None, older_than=10 + g - 1)
                cur.update(zs=[zs, zsB][g % 2], xc=[xc, xcB][g % 2], BCt=[BCt, BCtB][g % 2], sfx='ab'[g % 2])
                Z_, sfx = cur['zs'], cur['sfx']
                for c in range(4):
                    inproj_chunk(slot, c, n, lambda pa, bn, c=c: op('act', lambda e: e.activation(out=Z_[:, c, :], in_=pa, func=AF.Identity),
                                                                    reads=[bn, 'RT'], writes=['zs' + sfx]))
                op('act', lambda e: e.activation(out=Z_[:, :, :], in_=Z_[:, :, :], func=AF.Silu), reads=['zs' + sfx, 'RT'], writes=['zs' + sfx])
            tasks.append((ld_z, cp_z))

            def ld_x(slot, g=g):
                wload((C_X + 512 * g, 512), slot)

            def cp_x(slot, g=g):
                for c in range(4):
                    eng = 'act' if c % 2 == 0 else 'dve'
                    if eng == 'act':
                        inproj_chunk(slot, c, n, lambda pa, bn, c=c: op('act', lambda e: e.activation(out=xpre[:, c, 3:3 + n], in_=pa, func=AF.Identity),
                                                                        reads=[bn, 'RT'], writes=['xpre%d' % c]))
                    else:
                        inproj_chunk(slot, c, n, lambda pa, bn, c=c: op('dve', lambda e: e.tensor_copy(out=xpre[:, c, 3:3 + n], in_=pa),
                                                                        reads=[bn, 'RT'], writes=['xpre%d' % c]))
            tasks.append((ld_x, cp_x))

            def ld_bc(slot, g=g):
                wload((C_B + 128 * g, 128), slot, 0)
                wload((C_C + 128 * g, 128), slot, 128)

            def cp_bc(slot, g=g):
                for c in range(2):
                    inproj_chunk(slot, c, n, lambda pa, bn, c=c: op('act', lambda e: e.activation(out=xpre[:, 4 + c, 3:3 + n], in_=pa, func=AF.Identity),
                                                                    reads=[bn, 'RT'], writes=['xpre%d' % (4 + c)]))
                conv_group(g, NT, b, last_group)
                tg = 10 + g
                defer(tg, lambda: ssd_prep(g, 0))
                for tt in range(NT + 1):
                    def piece(tt=tt):
                        if tt >= 1:
                            ssd_back_b(g, tt - 1)
                        if tt < NT:
                            ssd_mid(g, tt, first_group and tt == 0)
                            if tt + 1 < NT:
                                ssd_prep(g, tt + 1)
                            ssd_back(g, tt)
                    defer(tg, piece)
                if last_group:
                    defer(tg, lambda: ssm_out(g, hp[b]))
            tasks.append((ld_bc, cp_bc))

        for cb in range(4):
            for kh in range(2):
                def ld_o(slot, cb=cb, kh=kh):
                    wload_rows(kh * 2048, cb * 512, slot)

                def cp_o(slot, cb=cb, kh=kh):
                    if cb == 0 and kh == 0:
                        fence('O')
                        dma('sp', gpost_b, bcast_dram(norm_post, D), ['RT'], ['gpost_b'], 'c10')
                    w = wsl[slot]
                    for k in range(16):
                        kk = kh * 16 + k
                        for tt in range(NT):
                            mn = ['mixA%d' % tt] if kk < 16 else ['mixS%d' % tt]
                            op('pe', lambda e, k=k, kk=kk, tt=tt: e.matmul(ps32(2 + tt), lhsT=mixT[:, kk, tt * 128:(tt + 1) * 128], rhs=w[:, k, :],
                                                                            start=(kk == 0), stop=(kk == 31)),
                               reads=['w%d' % slot] + mn, writes=[PSN[2 + tt]])
                    if gi + 1 < len(grp_x):
                        nx, nnt = grp_x[gi + 1]
                        ib = cb * 2 + kh
                        if ib < nnt:
                            phase0(nx, nnt, tiles=[ib])
                    if kh == 1:
                        for tt in range(NT):
                            op('act', lambda e, tt=tt: e.activation(out=outacc[:, tt, cb * 512:(cb + 1) * 512], in_=ps32(2 + tt), func=AF.Identity),
                               reads=[PSN[2 + tt], 'RT'], writes=['oacc%d' % tt])
                            op('act', lambda e, tt=tt: e.activation(out=ojunk, in_=ps32(2 + tt), func=AF.Square, accum_out=ssq[:, tt, cb:cb + 1]),
                               reads=[PSN[2 + tt], 'RT'], writes=['ojunk', 'ssq%d' % tt])
                        if cb == 3:
                            for tt in range(NT):
                                post_tile(tt, xsrc, ydst)
                tasks.append((ld_o, cp_o))

    for b in range(NP):
        for G in range(NG):
            add_prompt_group(b, G)
    if SAMPLE:
        add_sample_group()

    import os as _os
    if _os.environ.get("KSTOP"):
        del tasks[int(_os.environ["KSTOP"]):]
    if _os.environ.get("KSKIP"):
        del tasks[:int(_os.environ["KSKIP"])]
    nt = len(tasks)
    for i in range(min(NSLOT, nt)):
        tasks[i][0](i % NSLOT)
    for i in range(nt):
        tasks[i][1](i % NSLOT)
        if i + NSLOT < nt:
            tasks[i + NSLOT][0]((i + NSLOT) % NSLOT)
    outsems = ['yst0', 'yst1', 'yst2', 'yst3', 'vst', 'kst', 'cst', 'hst_o', 'so0', 'so1', 'so2', 'so3', 'so4']
    for s in outsems:
        if any(o['dma'] == s for o in P.ops):
            last = max(i for i, o in enumerate(P.ops) if o['dma'] == s)
            P.ops.append(dict(eng='sp', fn=lambda e: e.nop(), deps={last}, dma=None, sig=False, sem=None, val=0))
    P.emit(nc, st)
    st.close()
    return nc, P


_CACHE = {}


def kernel(x_prompt, x_sample, cache_k, cache_v, state_conv, state_ssm, norm_pre, w_in, attn_sink, attn_norm,
           conv_w, conv_b, dt_bias, a_log, d_skip, ssm_norm, w_out, norm_post):
    f = lambda a: np.ascontiguousarray(np.asarray(a, dtype=np.float32))
    B, S, _ = x_prompt.shape
    NPc = B // NCORES
    DB = x_sample.shape[0] // NCORES
    assert DB == 16 and x_sample.shape[1] == 8
    key = (NPc, S)
    if key not in _CACHE:
        _CACHE[key] = build(NPc, S, SAMPLE=True)[0]
    nc = _CACHE[key]
    cst = host_consts()
    shared = dict(w_in=f(w_in[0]), w_out=f(w_out[0]), norm_pre=f(norm_pre[0]), attn_sink=f(attn_sink[0]), attn_norm=f(attn_norm[0]),
                  conv_w=f(conv_w[0]), conv_b=f(conv_b[0]), dt_bias=f(dt_bias[0]), a_log=f(a_log[0]), d_skip=f(d_skip[0]),
                  ssm_norm=f(ssm_norm[0]), norm_post=f(norm_post[0]), consts=cst)
    in_maps = []
    for c in range(NCORES):
        m = dict(shared)
        m["xp"] = f(x_prompt[c * NPc:(c + 1) * NPc]).reshape(NPc * S, D)
        sl = slice(c * DB, (c + 1) * DB)
        m["xsm"] = f(x_sample[sl]).reshape(DB * 8, D)
        m["ck"] = f(cache_k[0, sl]).reshape(DB, 128, 256)
        m["cv"] = f(cache_v[0, sl]).reshape(DB, 128, 256)
        m["sconv"] = f(state_conv[0, sl])
        m["sssm"] = f(state_ssm[0, sl])
        in_maps.append(m)
    res = run_bass_kernel_spmd(nc, in_maps, core_ids=list(range(NCORES)))
    R = res.results
    cat = lambda k: np.concatenate([np.asarray(r[k]) for r in R], axis=0)
    y_prompt = cat("yp").reshape(B, S, D)
    y_sample = cat("ysm").reshape(NCORES * DB, 8, D)
    k_prompt = cat("kp").reshape(1, B, 128, 4, 64)
    v_prompt = cat("vp").reshape(1, B, 128, 4, 64)
    conv_prompt = cat("cp").reshape(1, B, 3, 3072)
    ssm_prompt = cat("hp").reshape(1, B, 32, 64, 128)
    k_sample = cat("ksm").reshape(1, NCORES * DB, 128, 4, 64)
    v_sample = cat("vsm").reshape(1, NCORES * DB, 128, 4, 64)
    conv_sample = cat("csm").reshape(1, NCORES * DB, 3, 3072)
    ssm_sample = cat("hsm").reshape(1, NCORES * DB, 32, 64, 128)
    return (y_prompt.astype(np.float32), y_sample.astype(np.float32), k_prompt, v_prompt, conv_prompt, ssm_prompt,
            k_sample, v_sample, conv_sample, ssm_sample)
```

```python
import numpy as np
from contextlib import ExitStack
import concourse.bass as bass
import concourse.mybir as mybir
from concourse.bass_utils import run_bass_kernel_spmd

F32 = mybir.dt.float32
BF16 = mybir.dt.bfloat16
ALU = mybir.AluOpType
AF = mybir.ActivationFunctionType
AX = mybir.AxisListType

D = 2048
IN_DIM = 9760
C_Q, C_K, C_V, C_GA, C_Z, C_X, C_B, C_C, C_DT = 0, 2048, 2304, 2560, 4608, 6656, 8704, 9216, 9728
EPS = 1e-6
NCORES = 8
NSLOT = 3
DRAIN_N = 1


class Prog:
    ENG = ('pe', 'act', 'dve', 'pool', 'sp')

    def __init__(self):
        self.ops = []
        self.last_w = {}
        self.readers = {}

    def begin_capture(self):
        self._cap = []

    def end_capture(self):
        lst, self._cap = self._cap, None
        return lst

    def op(self, eng, fn, reads=(), writes=(), dma=None):
        if getattr(self, '_cap', None) is not None:
            self._cap.append((eng, fn, tuple(reads), tuple(writes), dma))
            return None
        i = len(self.ops)
        deps = set()
        for b in reads:
            w = self.last_w.get(b)
            if w is not None:
                deps.add(w)
        for b in writes:
            w = self.last_w.get(b)
            if w is not None:
                deps.add(w)
            for r in self.readers.get(b, ()):
                deps.add(r)
        for b in reads:
            lst = self.readers.setdefault(b, [])
            if dma is None:
                lst[:] = [r for r in lst if not (self.ops[r]['dma'] is None and self.ops[r]['eng'] == eng)]
            lst.append(i)
        for b in writes:
            self.last_w[b] = i
            self.readers[b] = []
        self.ops.append(dict(eng=eng, fn=fn, deps=deps, dma=dma, sig=False, sem=None, val=0))
        return i

    @staticmethod
    def _skip(dep, op):
        return dep['dma'] is None and dep['eng'] == op['eng'] and op['eng'] == 'pe'

    def schedule(self):
        ops = self.ops
        for op in ops:
            for d in op['deps']:
                dep = ops[d]
                if self._skip(dep, op):
                    continue
                if dep['dma'] is None:
                    dep['sig'] = True
        cnt = {e: 0 for e in self.ENG}
        dmacnt = {}
        for op in ops:
            if op['dma'] is not None:
                dmacnt[op['dma']] = dmacnt.get(op['dma'], 0) + 16
                op['sem'] = ('dma', op['dma'])
                op['val'] = dmacnt[op['dma']]
            elif op['sig']:
                cnt[op['eng']] += 1
                op['sem'] = ('eng', op['eng'])
                op['val'] = cnt[op['eng']]
        known = {e: {} for e in self.ENG}
        streams = {e: [] for e in self.ENG}
        snaps = {}
        nwait = 0
        for op in ops:
            e = op['eng']
            need = {}
            for d in op['deps']:
                dep = ops[d]
                if self._skip(dep, op):
                    continue
                key, val = dep['sem'], dep['val']
                if known[e].get(key, 0) >= val:
                    continue
                need[key] = max(need.get(key, 0), val)
            for key, val in sorted(need.items(), key=lambda kv: str(kv[0])):
                if known[e].get(key, 0) >= val:
                    continue
                streams[e].append(('wait', key, val))
                nwait += 1
                known[e][key] = val
                for k2, v2 in snaps[(key, val)].items():
                    if known[e].get(k2, 0) < v2:
                        known[e][k2] = v2
            streams[e].append(('op', op))
            if op['sem'] is not None:
                s = dict(known[e])
                if op['dma'] is None:
                    s[op['sem']] = op['val']
                snaps[(op['sem'], op['val'])] = s
        self.streams = streams
        self.sem_keys = sorted({op['sem'] for op in ops if op['sem'] is not None}, key=str)
        self.stats = dict(nops=len(ops), nwait=nwait, cnt=cnt, dmacnt=dmacnt)
        return streams

    def emit(self, nc, stack):
        streams = self.schedule()
        sems = {}
        for k in self.sem_keys:
            sems[k] = stack.enter_context(nc.semaphore("s_%s_%s" % (k[0], k[1])))
        block = stack.enter_context(nc.Block())

        def replay(name, eng):
            for rec in streams[name]:
                if rec[0] == 'wait':
                    eng.wait_ge(sems[rec[1]], rec[2])
                else:
                    op = rec[1]
                    ins = op['fn'](eng)
                    if op['dma'] is not None:
                        ins.then_inc(sems[op['sem']], 16)
                    elif op['sig']:
                        ins.then_inc(sems[op['sem']], 1)

        @block.tensor
        def _(eng):
            replay('pe', eng)

        @block.scalar
        def _(eng):
            replay('act', eng)

        @block.vector
        def _(eng):
            replay('dve', eng)

        @block.gpsimd
        def _(eng):
            replay('pool', eng)

        @block.sync
        def _(eng):
            replay('sp', eng)


def host_consts():
    c = np.zeros((128, 5 * 128 + 16), np.float32)
    i = np.arange(128)
    c[:, 0:128] = np.eye(128)
    c[:, 128:256] = (i[:, None] <= i[None, :])
    c[:, 256:384] = 1.0
    same = (i[:, None] // 8) == (i[None, :] // 8)
    c[:, 384:512] = same & (i[:, None] <= i[None, :])
    c[:, 512:640] = same
    c[:, 640:656] = (i[:, None] // 8) == np.arange(16)[None, :]
    return c


def build(NP, SEQ, SAMPLE=True):
    NG = SEQ // 512
    nc = bass.Bass("TRN2", target_bir_lowering=False)
    din = lambda name, shape: nc.dram_tensor(name, list(shape), F32, kind="ExternalInput").ap()
    dout = lambda name, shape: nc.dram_tensor(name, list(shape), F32, kind="ExternalOutput").ap()
    xp = din("xp", [NP * SEQ, D])
    w_in = din("w_in", [D, IN_DIM])
    w_out = din("w_out", [2 * D, D])
    norm_pre = din("norm_pre", [D])
    attn_sink = din("attn_sink", [32])
    attn_norm = din("attn_norm", [D])
    conv_w = din("conv_w", [4, 3072])
    conv_b = din("conv_b", [3072])
    dt_bias = din("dt_bias", [32])
    a_log = din("a_log", [32])
    d_skip = din("d_skip", [32])
    ssm_norm = din("ssm_norm", [D])
    norm_post = din("norm_post", [D])
    consts = din("consts", [128, 656])
    yp = dout("yp", [NP * SEQ, D])
    kp = dout("kp", [NP, 128, 256])
    vp = dout("vp", [NP, 128, 256])
    cp = dout("cp", [NP, 3, 3072])
    hp = dout("hp", [NP, 32, 64, 128])
    if SAMPLE:
        xsm = din("xsm", [128, D])
        ck = din("ck", [16, 128, 256])
        cv = din("cv", [16, 128, 256])
        sconv = din("sconv", [16, 3, 3072])
        sssm = din("sssm", [16, 32, 64, 128])
        ysm = dout("ysm", [128, D])
        ksm = dout("ksm", [16, 128, 256])
        vsm = dout("vsm", [16, 128, 256])
        csm = dout("csm", [16, 3, 3072])
        hsm = dout("hsm", [16, 32, 64, 128])
        vscr = nc.dram_tensor("vscr", [128, 256], F32, kind="Internal").ap()

    P = Prog()
    st = ExitStack()
    sb = lambda name, shape, dt: st.enter_context(nc.sbuf_tensor(name, list(shape), dt))
    cf = sb("cf", [128, 656], F32)
    ident_f, triu_f, ones_f = cf[:, 0:128], cf[:, 128:256], cf[:, 256:384]
    identb = sb("identb", [128, 128], BF16)
    maskb = sb("maskb", [128, 2, 128], BF16)
    onespad = sb("onespad", [128, 2, 128], BF16)
    onesb = sb("onesb", [128, 128], BF16)
    triub_blk = sb("triub_blk", [128, 128], BF16)
    onespad_f = sb("onespad_f", [32, 2, 128], F32)
    gpre = sb("gpre", [128, 16], F32)
    anorm = sb("anorm", [128, 16], F32)
    snorm = sb("snorm", [128, 16], F32)
    convw = sb("convw", [128, 4, 24], F32)
    convb = sb("convb", [128, 24], F32)
    Dcol = sb("Dcol", [128, 16], F32)
    Ecol = sb("Ecol", [128, 16], F32)
    dtb_b = sb("dtb_b", [128, 32], F32)
    A_b = sb("A_b", [128, 32], F32)
    epsc = sb("epsc", [128, 1], F32)
    onec = sb("onec", [128, 1], F32)
    hT = sb("hT", [128, 16, 512], BF16)
    mixT = sb("mixT", [128, 32, 512], BF16)
    wsl = [sb("w%d" % i, [128, 16, 512], BF16) for i in range(NSLOT)]
    xt = sb("xt", [128, D], F32)
    stt = sb("stt", [128, 8], F32)
    fscr = sb("fscr", [128, 1], F32)
    KT = sb("KT", [128, 4, 640], BF16)
    Vpad = sb("Vpad", [128, 5, 4, 2, 128], BF16)
    hstate = sb("hstate", [128, D], F32)
    hpad = sb("hpad", [128, 32, 128], BF16)
    hist = sb("hist", [128, 24, 3], F32)
    dt_all = sb("dt_all", [128, 4, 32], F32)
    dtA_all = sb("dtA_all", [128, 4, 32], F32)
    a_all = sb("a_all", [128, 4, 32], F32)
    w_all = sb("w_all", [128, 4, 32], F32)
    cd_all = sb("cd_all", [128, 4, 32], F32)
    aT_all = sb("aT_all", [32, 4, 128], F32)
    ssq = sb("ssq", [128, 4, 4], F32)
    RTW = 15360
    RT = sb("RT", [128, RTW], F32)

    def rt32(off, n):
        return RT[:, off:off + n]

    def rt16(off, n):
        return RT[:, off:off + n].bitcast(BF16)

    QTu = rt16(0, 1024).rearrange("p (t c q) -> p t c q", t=4, c=4)
    gs = rt16(1024, 1024).rearrange("p (c t) -> p c t", c=4)
    Eb = rt16(2048, 1024).rearrange("p (e t) -> p e t", e=4)
    EbB = rt16(4992, 1024).rearrange("p (e t) -> p e t", e=4)
    Ebs = [Eb, EbB]
    rstd_a = rt32(11136, 512)
    QTuB = rt16(6016, 1024).rearrange("p (t c q) -> p t c q", t=4, c=4)
    gsB = rt16(7040, 1024).rearrange("p (c t) -> p c t", c=4)
    rd = rt32(3072, 512)
    o32 = rt32(3584, 512)
    sqa = rt16(4096, 256)
    vstage = rt32(4352, 256)
    kstage = rt32(4608, 256)
    dtmp = rt32(4864, 128)
    xpre = rt32(0, 3090).rearrange("p (c t) -> p c t", c=6)
    ctmp = rt32(3090, 1024).rearrange("p (e t) -> p e t", e=2)
    xc = rt16(4114, 1024).rearrange("p (c t) -> p c t", c=4)
    BCt = rt16(5138, 512).rearrange("p (c t) -> p c t", c=2)
    zs = rt16(5650, 1024).rearrange("p (c t) -> p c t", c=4)
    dm = rt32(6674, 1024).rearrange("p (h l) -> p h l", h=8)
    Eb2 = rt16(7698, 512).rearrange("p (h l) -> p h l", h=8)
    MT = rt16(8210, 512).rearrange("p (h l) -> p h l", h=8)
    EA = rt16(8722, 512).rearrange("p (h l) -> p h l", h=8)
    CTs = rt16(9234, 512).rearrange("p (h l) -> p h l", h=8)
    xdtp = rt16(9746, 512).rearrange("p (c e q) -> p c e q", c=4, e=2)
    xw = rt16(10258, 256)
    Btok = rt16(10514, 64)
    cbm = rt16(10578, 64)
    yd = rt32(10642, 512)
    rg = rt32(11154, 128)
    sqs = rt16(11282, 256)
    Rm = rt32(11538, 128 * 1)
    hstage = rt32(6674, 512).rearrange("p (c n) -> p c n", c=4)
    cstage = rt32(3090, 768).rearrange("p (c n) -> p c n", c=6)
    Kc = rt16(4992, 1024).rearrange("p (b e d) -> p b e d", b=16, e=2)
    KTc = rt16(6016, 1024).rearrange("p (b s) -> p b s", b=16)
    Vc_pad = rt16(7040, 2048).rearrange("p (b e q) -> p b e q", b=16, e=2)
    Vn_pad = rt16(9088, 2048).rearrange("p (b e q) -> p b e q", b=16, e=2)
    xpre_s = rt32(0, 1056).rearrange("p (c b t) -> p c b t", c=6, b=16)
    cst_in = rt32(1056, 768).rearrange("p (c n) -> p c n", c=6)
    cst_o = rt32(1824, 288).rearrange("p (c n) -> p c n", c=6)
    cstage_s = rt32(3090, 768).rearrange("p (c n) -> p c n", c=6)
    h0f = rt32(11776, 1024).rearrange("p (b n) -> p b n", b=8)
    hps = rt16(12800, 1024).rearrange("p (b e q) -> p b e q", b=8, e=2)
    h0b = rt16(13824, 512).rearrange("p (b n) -> p b n", b=8)
    Bm = rt16(14336, 1024).rearrange("p (b n) -> p b n", b=16)
    zsB = rt16(11776, 1024).rearrange("p (c t) -> p c t", c=4)
    xcB = rt16(12800, 1024).rearrange("p (c t) -> p c t", c=4)
    BCtB = rt16(13824, 512).rearrange("p (c t) -> p c t", c=2)
    cur = dict(QTu=QTu, gs=gs, zs=zs, xc=xc, BCt=BCt, sfx='')
    outacc = rt32(0, 8192).rearrange("p (t c) -> p t c", t=4)
    ojunk = rt16(8192, 256)
    xsb = rt16(8448, 1024)
    gpost_b = rt32(9472, 2048)

    psb = [st.enter_context(nc.psum_tensor("ps%d" % i, [128, 512], F32)) for i in range(8)]

    def ps32(i):
        return psb[i][:, :]

    def ps16(i):
        return psb[i][:, :].bitcast(BF16)

    op = P.op
    PSN = ['ps%d' % i for i in range(8)]

    pend = []

    def defer(tag, fn):
        P.begin_capture()
        fn()
        pend.append((tag, P.end_capture()))

    def drain(k=None, older_than=None):
        n = 0
        while pend and (k is None or n < k):
            if older_than is not None and pend[0][0] >= older_than:
                break
            tag, lst = pend.pop(0)
            for rec in lst:
                P.op(*rec)
            n += 1

    def fence(tag):
        drain()
        op('dve', lambda e: e.memset(fscr[:, :], 0.0), reads=[], writes=['RT'])

    def bcast_dram(vec, n, parts=128):
        return bass.AP(tensor=vec.tensor, offset=0, ap=[[0, parts], [1, n]])

    dma_ctr = [0]

    def dma(q, out, in_, r, w, sem, **kw):
        op(q, lambda e: e.dma_start(out=out, in_=in_, **kw), reads=r, writes=w, dma=sem)

    dma('sp', cf[:, :], consts, [], ['cf'], 'c0')
    dma('sp', gpre[:, :], norm_pre.rearrange("(k p) -> p k", p=128), [], ['gpre'], 'c1', allow_slow_non_contiguous=True)
    dma('sp', anorm[:, :], attn_norm.rearrange("(k p) -> p k", p=128), [], ['anorm'], 'c2', allow_slow_non_contiguous=True)
    dma('sp', snorm[:, :], ssm_norm.rearrange("(k p) -> p k", p=128), [], ['snorm'], 'c3', allow_slow_non_contiguous=True)
    for k in range(4):
        dma('sp', convw[:, k, :], conv_w[k].rearrange("(c p) -> p c", p=128), [], ['convw'], 'c4', allow_slow_non_contiguous=True)
    dma('sp', convb[:, :], conv_b.rearrange("(c p) -> p c", p=128), [], ['convb'], 'c5', allow_slow_non_contiguous=True)
    for h2 in range(2):
        dma('sp', Dcol[64 * h2:64 * h2 + 64, :], bass.AP(tensor=d_skip.tensor, offset=h2, ap=[[0, 64], [2, 16]]),
            [], ['Dcol'], 'c6', allow_slow_non_contiguous=True)
        dma('sp', Ecol[64 * h2:64 * h2 + 64, :], bass.AP(tensor=attn_sink.tensor, offset=h2, ap=[[0, 64], [2, 16]]),
            [], ['Ecol'], 'c7', allow_slow_non_contiguous=True)
    dma('sp', dtb_b[:, :], bcast_dram(dt_bias, 32), [], ['dtb_b'], 'c8')
    dma('sp', A_b[:, :], bcast_dram(a_log, 32), [], ['A_b'], 'c9')
    op('dve', lambda e: e.tensor_copy(out=identb[:, :], in_=ident_f), reads=['cf'], writes=['identb'])
    op('dve', lambda e: e.tensor_copy(out=maskb[:, 1, :], in_=triu_f), reads=['cf'], writes=['maskb'])
    op('dve', lambda e: e.tensor_scalar(out=maskb[:, 0, :], in0=triu_f, scalar1=-1.0, scalar2=1.0, op0=ALU.mult, op1=ALU.add),
       reads=['cf'], writes=['maskb'])
    op('dve', lambda e: e.tensor_copy(out=onesb[:, :], in_=ones_f), reads=['cf'], writes=['onesb'])
    op('pool', lambda e: e.memset(onespad[:, :, :], 0.0), writes=['onespad'])
    op('dve', lambda e: e.tensor_copy(out=triub_blk[:, :], in_=cf[:, 384:512]), reads=['cf'], writes=['maskb'])
    op('pool', lambda e: e.memset(onespad_f[:, :, :], 0.0), writes=['onespad_f'])
    op('pool', lambda e: e.memset(onespad_f[:, 0, 0:64], 1.0), writes=['onespad_f'])
    op('pool', lambda e: e.memset(onespad_f[:, 1, 64:128], 1.0), writes=['onespad_f'])
    op('pool', lambda e: e.memset(onespad[:, 0, 0:64], 1.0), reads=[], writes=['onespad'])
    op('pool', lambda e: e.memset(onespad[:, 1, 64:128], 1.0), reads=[], writes=['onespad'])
    op('pool', lambda e: e.memset(epsc[:, :], EPS), writes=['epsc'])
    op('pool', lambda e: e.memset(onec[:, :], 1.0), writes=['onec'])
    op('pool', lambda e: e.memset(Vpad[:, :, :, :, :], 0.0), writes=['Vpad%d' % i for i in range(5)])
    op('pool', lambda e: e.memset(hpad[:, :, :], 0.0), writes=['hpad%d' % i for i in range(4)])
    op('act', lambda e: e.activation(out=Ecol[:, :], in_=Ecol[:, :], func=AF.Exp), reads=['Ecol'], writes=['Ecol'])
    op('act', lambda e: e.activation(out=A_b[:, :], in_=A_b[:, :], func=AF.Exp), reads=['A_b'], writes=['A_b'])
    op('dve', lambda e: e.tensor_scalar(out=A_b[:, :], in0=A_b[:, :], scalar1=-1.0, scalar2=None, op0=ALU.mult),
       reads=['A_b'], writes=['A_b'])

    tasks = []
    grp_x = []

    def wload(cols, slot, base=0, src=None):
        src = w_in if src is None else src
        w = wsl[slot]
        c0, n = cols
        dma('pool', w[:, :, base:base + n], src[:, c0:c0 + n].rearrange("(k p) c -> p k c", p=128),
            [], ['w%d' % slot], 'wl%d' % slot)

    def wload_rows(r0, c0, slot):
        w = wsl[slot]
        dma('pool', w[:, :, :], w_out[r0:r0 + 2048, c0:c0 + 512].rearrange("(k p) c -> p k c", p=128),
            [], ['w%d' % slot], 'wl%d' % slot)

    pbank = [0]

    def inproj_chunk(slot, wc, ntok, evac):
        bi = pbank[0] % 2
        pbank[0] += 1
        w = wsl[slot]
        for k in range(16):
            op('pe', lambda e, k=k: e.matmul(ps32(bi)[:, 0:ntok], lhsT=w[:, k, wc * 128:(wc + 1) * 128], rhs=hT[:, k, 0:ntok],
                                              start=(k == 0), stop=(k == 15)),
               reads=['w%d' % slot, 'hT'], writes=[PSN[bi]])
        evac(ps32(bi)[:, 0:ntok], PSN[bi])
        drain(DRAIN_N)

    def phase0(xsrc, NT, tiles=None):
        for tt in (range(NT) if tiles is None else tiles):
            dma('sp', xt[:, :], xsrc[tt * 128:(tt + 1) * 128, :], [], ['xt'], 'xld')
            op('act', lambda e: e.activation(out=xsb[:, :], in_=xt[:, :], func=AF.Square, accum_out=stt[:, 0:1]),
               reads=['xt', 'RT'], writes=['xsb', 'stt'])
            op('act', lambda e: e.activation(out=stt[:, 1:2], in_=stt[:, 0:1], func=AF.Ln, scale=1.0 / D, bias=epsc[:, :]),
               reads=['stt', 'epsc'], writes=['stt'])
            op('act', lambda e: e.activation(out=stt[:, 2:3], in_=stt[:, 1:2], func=AF.Exp, scale=-0.5),
               reads=['stt'], writes=['stt'])
            op('dve', lambda e: e.tensor_scalar(out=xsb[:, :], in0=xt[:, :], scalar1=stt[:, 2:3], scalar2=None, op0=ALU.mult),
               reads=['xt', 'stt', 'RT'], writes=['xsb'])
            for half in range(2):
                bi = 6 + half
                pv = ps16(bi).rearrange("p (k t) -> p k t", k=8)
                for k8 in range(8):
                    k = half * 8 + k8
                    op('pe', lambda e, k=k, k8=k8, pv=pv: e.transpose(out=pv[:, k8, :], in_=xsb[:, k * 128:(k + 1) * 128], identity=identb[:, :]),
                       reads=['xsb', 'identb', 'RT'], writes=[PSN[bi]])
                op('dve', lambda e, half=half, tt=tt, pv=pv: e.tensor_tensor(
                    out=hT[:, half * 8:half * 8 + 8, tt * 128:(tt + 1) * 128], in0=pv,
                    in1=gpre[:, half * 8:half * 8 + 8].unsqueeze(2).broadcast_to([128, 8, 128]), op=ALU.mult),
                   reads=[PSN[bi], 'gpre'], writes=['hT'])

    def phase1_tile(slot, tt, NT, first_tile, last_tile_of_seq, seq_b, tri=None, blk=None, sample=False):
        tri = triu_f if tri is None else tri
        blk = ones_f if blk is None else blk
        w = wsl[slot]
        bi = pbank[0] % 2
        pbank[0] += 1
        for k in range(16):
            op('pe', lambda e, k=k: e.matmul(ps32(bi)[:, 0:288], lhsT=hT[:, k, tt * 128:(tt + 1) * 128], rhs=w[:, k, 0:288],
                                              start=(k == 0), stop=(k == 15)),
               reads=['w%d' % slot, 'hT'], writes=[PSN[bi]])
        import os as _os
        lvl = int(_os.environ.get("PH1", "99"))
        if lvl < 1:
            return
        pv = ps32(bi)
        vsrc = pv[:, 0:256].rearrange("p (j d) -> p j d", j=4)
        if not sample:
            op('act', lambda e: e.activation(out=Vpad[:, tt + 1, :, 0, 0:64], in_=vsrc, func=AF.Identity),
               reads=[PSN[bi]], writes=['Vpad%d' % (tt + 1)])
            op('dve', lambda e: e.tensor_copy(out=Vpad[:, tt + 1, :, 1, 64:128], in_=vsrc),
               reads=[PSN[bi]], writes=['Vpad%d' % (tt + 1)])
        else:
            op('dve', lambda e: e.tensor_copy(out=vstage, in_=pv[:, 0:256]), reads=[PSN[bi], 'RT'], writes=['vstage'])
            if not _os.environ.get('NOVSCR'):
                dma('sp', vscr, vstage, ['vstage', 'RT'], ['vscr'], 'vscr')
            for b in range(16):
                if _os.environ.get('NOVROWS'):
                    break
                dma('sp', vsm[b, 120:128, :], vstage[8 * b:8 * b + 8, :], ['vstage', 'RT'], [], 'so0')
        if last_tile_of_seq and not _os.environ.get('NOVST'):
            op('dve', lambda e: e.tensor_copy(out=vstage, in_=pv[:, 0:256]), reads=[PSN[bi], 'RT'], writes=['vstage'])
            if not _os.environ.get('NOVDMA'):
                dma('sp', vp[seq_b], vstage, ['vstage', 'RT'], [], 'vst')
        if lvl < 2:
            return
        op('dve', lambda e: e.tensor_tensor(out=dt_all[:, tt, :], in0=pv[:, 256:288], in1=dtb_b[:, :], op=ALU.add),
           reads=[PSN[bi], 'dtb_b'], writes=['dt%d' % tt])
        op('act', lambda e: e.activation(out=dt_all[:, tt, :], in_=dt_all[:, tt, :], func=AF.Exp), reads=['dt%d' % tt], writes=['dt%d' % tt])
        op('act', lambda e: e.activation(out=dt_all[:, tt, :], in_=dt_all[:, tt, :], func=AF.Ln, bias=onec[:, :]),
           reads=['dt%d' % tt, 'onec'], writes=['dt%d' % tt])
        op('dve', lambda e: e.tensor_tensor(out=dtA_all[:, tt, :], in0=dt_all[:, tt, :], in1=A_b[:, :], op=ALU.mult),
           reads=['dt%d' % tt, 'A_b'], writes=['dtA%d' % tt])
        if lvl < 3:
            return
        p2 = ps32(2)
        op('pe', lambda e: e.matmul(p2[:, 0:32], lhsT=tri, rhs=dtA_all[:, tt, :], start=True, stop=True),
           reads=['cf', 'dtA%d' % tt], writes=['ps2'])
        op('pe', lambda e: e.matmul(p2[:, 32:64], lhsT=blk, rhs=dtA_all[:, tt, :], start=True, stop=True),
           reads=['cf', 'dtA%d' % tt], writes=['ps2'])
        op('pe', lambda e: e.matmul(p2[0:32, 64:192], lhsT=dtA_all[:, tt, :], rhs=tri, start=True, stop=True),
           reads=['cf', 'dtA%d' % tt], writes=['ps2'])
        if lvl < 4:
            return
        op('act', lambda e: e.activation(out=a_all[:, tt, :], in_=p2[:, 0:32], func=AF.Identity), reads=['ps2'], writes=['a%d' % tt])
        op('act', lambda e: e.activation(out=aT_all[:, tt, :], in_=p2[0:32, 64:192], func=AF.Identity), reads=['ps2'], writes=['aT%d' % tt])
        op('act', lambda e: e.activation(out=cd_all[:, tt, :], in_=p2[:, 32:64], func=AF.Exp), reads=['ps2'], writes=['cd%d' % tt])
        op('dve', lambda e: e.tensor_tensor(out=w_all[:, tt, :], in0=p2[:, 32:64], in1=a_all[:, tt, :], op=ALU.subtract),
           reads=['ps2', 'a%d' % tt], writes=['w%d_' % tt])
        op('act', lambda e: e.activation(out=w_all[:, tt, :], in_=w_all[:, tt, :], func=AF.Exp), reads=['w%d_' % tt], writes=['w%d_' % tt])
        op('dve', lambda e: e.tensor_tensor(out=w_all[:, tt, :], in0=w_all[:, tt, :], in1=dt_all[:, tt, :], op=ALU.mult),
           reads=['w%d_' % tt, 'dt%d' % tt], writes=['w%d_' % tt])

    def attn_front(j, tt, has_prev):
        EB = Ebs[tt % 2]
        en = 'Eb%d_' % (tt % 2)
        Q_, sfx = cur['QTu'], cur['sfx']
        kbs = [0, 1] if has_prev else [1]
        for h2 in range(2):
            for kb in kbs:
                e_i = h2 * 2 + kb
                bi = (2, 3, 7, 2)[e_i]
                op('pe', lambda e, h2=h2, kb=kb, bi=bi: e.matmul(
                    ps32(bi), lhsT=KT[64 * h2:64 * h2 + 64, j, (tt + kb) * 128:(tt + kb + 1) * 128],
                    rhs=Q_[64 * h2:64 * h2 + 64, tt, :, :], start=True, stop=True),
                   reads=['KT%d' % j, 'QTu' + sfx, 'RT'], writes=[PSN[bi]])
                op('act', lambda e, e_i=e_i, bi=bi: e.activation(out=EB[:, e_i, :], in_=ps32(bi), func=AF.Exp, scale=0.125),
                   reads=[PSN[bi], 'RT'], writes=[en + str(e_i)])
                op('dve', lambda e, e_i=e_i, kb=kb: e.tensor_tensor(
                    out=EB[:, e_i, :].rearrange("p (c q) -> p c q", c=4), in0=EB[:, e_i, :].rearrange("p (c q) -> p c q", c=4),
                    in1=maskb[:, kb, :].unsqueeze(1).broadcast_to([128, 4, 128]), op=ALU.mult),
                   reads=[en + str(e_i), 'maskb', 'RT'], writes=[en + str(e_i)])

    def attn_back(j, tt, has_prev):
        EB = Ebs[tt % 2]
        en = 'Eb%d_' % (tt % 2)
        G_, sfx = cur['gs'], cur['sfx']
        kbs = [0, 1] if has_prev else [1]
        lst = [(h2, kb) for h2 in range(2) for kb in kbs]
        for idx, (h2, kb) in enumerate(lst):
            op('pe', lambda e, h2=h2, kb=kb, idx=idx: e.matmul(
                ps32(4), lhsT=Vpad[:, tt + kb, j, h2, :], rhs=EB[:, h2 * 2 + kb, :], start=(idx == 0), stop=(idx == len(lst) - 1)),
               reads=['Vpad%d' % (tt + kb), en + str(h2 * 2 + kb), 'RT'], writes=['ps4'])
        for idx, (h2, kb) in enumerate(lst):
            op('pe', lambda e, h2=h2, kb=kb, idx=idx: e.matmul(
                ps32(5), lhsT=onespad[:, h2, :], rhs=EB[:, h2 * 2 + kb, :], start=(idx == 0), stop=(idx == len(lst) - 1)),
               reads=['onespad', en + str(h2 * 2 + kb), 'RT'], writes=['ps5'])
        for c in range(4):
            op('dve', lambda e, c=c: e.tensor_scalar(out=rd[:, c * 128:(c + 1) * 128], in0=ps32(5)[:, c * 128:(c + 1) * 128],
                                                      scalar1=Ecol[:, 4 * j + c:4 * j + c + 1], scalar2=None, op0=ALU.add),
               reads=['ps5', 'Ecol', 'RT'], writes=['rd'])
        op('act', lambda e: e.activation(out=rd, in_=rd, func=AF.Ln), reads=['rd', 'RT'], writes=['rd'])
        op('act', lambda e: e.activation(out=rd, in_=rd, func=AF.Exp, scale=-1.0), reads=['rd', 'RT'], writes=['rd'])
        op('dve', lambda e: e.tensor_tensor(out=o32, in0=ps32(4), in1=rd, op=ALU.mult), reads=['ps4', 'rd', 'RT'], writes=['o32'])
        mv = mixT[:, 4 * j:4 * j + 4, tt * 128:(tt + 1) * 128]
        op('dve', lambda e: e.tensor_tensor(out=mv, in0=o32.rearrange("p (c q) -> p c q", c=4),
                                             in1=G_[:, :, tt * 128:(tt + 1) * 128], op=ALU.mult),
           reads=['o32', 'gs' + sfx, 'RT'], writes=['mixA%d' % tt])
        op('act', lambda e: e.activation(out=sqa.rearrange("p (c q) -> p c q", c=4), in_=mv, func=AF.Square),
           reads=['mixA%d' % tt, 'RT'], writes=['sqa'])

    def attn_stats(j, tt, first):
        for c in range(4):
            op('pe', lambda e, c=c: e.matmul(ps32(6)[:, tt * 128:(tt + 1) * 128], lhsT=onesb[:, :], rhs=sqa[:, c * 128:(c + 1) * 128],
                                              start=(first and c == 0), stop=(j == 3 and c == 3), skip_group_check=True),
               reads=['onesb', 'sqa', 'RT'], writes=['ps6'])

    def attn_unit(j, NT, first_group):
        hp_ = lambda t: not (first_group and t == 0)
        defer(j, lambda: attn_front(j, 0, hp_(0)))
        for tt in range(NT):
            def piece(tt=tt):
                if tt + 1 < NT:
                    attn_front(j, tt + 1, hp_(tt + 1))
                if tt > 0:
                    attn_stats(j, tt - 1, j == 0 and tt - 1 == 0)
                attn_back(j, tt, hp_(tt))
            defer(j, piece)
        defer(j, lambda: attn_stats(j, NT - 1, j == 0 and NT - 1 == 0))

    def attn_finish(NT):
        n = NT * 128
        op('act', lambda e: e.activation(out=rstd_a[:, 0:n], in_=ps32(6)[:, 0:n], func=AF.Ln, scale=1.0 / D, bias=epsc[:, :]),
           reads=['ps6', 'epsc', 'RT'], writes=['rstd_a'])
        op('act', lambda e: e.activation(out=rstd_a[:, 0:n], in_=rstd_a[:, 0:n], func=AF.Exp, scale=-0.5),
           reads=['rstd_a', 'RT'], writes=['rstd_a'])
        for c16 in range(16):
            eng = 'dve'
            op(eng, lambda e, c16=c16: e.scalar_tensor_tensor(out=mixT[:, c16, 0:n], in0=mixT[:, c16, 0:n], scalar=anorm[:, c16:c16 + 1],
                                                              in1=rstd_a[:, 0:n], op0=ALU.mult, op1=ALU.mult),
               reads=['rstd_a', 'anorm', 'RT'] + ['mixA%d' % t for t in range(NT)], writes=['mixA%d' % t for t in range(NT)])

    def conv_group(g, NT, seq_b, last_group):
        n = NT * 128
        for ci in range(6):
            ch = (4 * g + ci) if ci < 4 else (16 + g if ci == 4 else 20 + g)
            op('act', lambda e, ci=ci, ch=ch: e.activation(out=xpre[:, ci, 0:3], in_=hist[:, ch, :], func=AF.Identity),
               reads=['hist%d' % ch, 'RT'], writes=['xpre%d' % ci])
        chof = lambda ci: (4 * g + ci) if ci < 4 else (16 + g if ci == 4 else 20 + g)

        def tap0(ci):
            ch = chof(ci)
            acc = ctmp[:, ci % 2, 0:n]
            op('act', lambda e: e.activation(out=acc, in_=xpre[:, ci, 0:n], func=AF.Identity, scale=convw[:, 0, ch:ch + 1]),
               reads=['xpre%d' % ci, 'convw', 'RT'], writes=['ctmp%d' % (ci % 2)])
        tap0(0)
        tap0(1)
        for ci in range(6):
            ch = (4 * g + ci) if ci < 4 else (16 + g if ci == 4 else 20 + g)
            eng = 'dve'
            acc = ctmp[:, ci % 2, 0:n]
            an = 'ctmp%d' % (ci % 2)
            for k in range(1, 4):
                op(eng, lambda e, ci=ci, ch=ch, k=k, acc=acc: e.scalar_tensor_tensor(
                    out=acc, in0=xpre[:, ci, k:k + n], scalar=convw[:, k, ch:ch + 1], in1=acc, op0=ALU.mult, op1=ALU.add),
                   reads=['xpre%d' % ci, 'convw', an, 'RT'], writes=[an])
            dst = cur['xc'][:, ci, 0:n] if ci < 4 else cur['BCt'][:, ci - 4, 0:n]
            dn = (('xc%d' % ci) if ci < 4 else ('BCt%d' % (ci - 4))) + cur['sfx']
            op('act', lambda e, ch=ch, acc=acc, dst=dst: e.activation(out=dst, in_=acc, func=AF.Silu, bias=convb[:, ch:ch + 1]),
               reads=[an, 'convb', 'RT'], writes=[dn])
            op('act', lambda e, ci=ci, ch=ch: e.activation(out=hist[:, ch, :], in_=xpre[:, ci, n:n + 3], func=AF.Identity),
               reads=['xpre%d' % ci, 'RT'], writes=['hist%d' % ch])
            if ci + 2 < 6:
                tap0(ci + 2)
        if last_group:
            p5 = ps32(5)[0:3, :].rearrange("p (c n) -> p c n", c=4)
            p7 = ps32(7)[0:3, 0:256].rearrange("p (c n) -> p c n", c=2)
            for ci in range(6):
                tgt = p5[:, ci, :] if ci < 4 else p7[:, ci - 4, :]
                bn = 'ps5' if ci < 4 else 'ps7'
                op('pe', lambda e, ci=ci, tgt=tgt: e.transpose(out=tgt, in_=xpre[:, ci, n:n + 3], identity=ident_f),
                   reads=['xpre%d' % ci, 'cf', 'RT'], writes=[bn])
            op('act', lambda e: e.activation(out=cstage[0:3, 0:4, :], in_=p5, func=AF.Identity), reads=['ps5', 'ctmp0', 'ctmp1', 'RT'],
               writes=['ctmp0', 'ctmp1', 'cstage'])
            op('act', lambda e: e.activation(out=cstage[0:3, 4:6, :], in_=p7, func=AF.Identity), reads=['ps7', 'RT'], writes=['cstage2'])
            dma('sp', cp[seq_b, :, 512 * g:512 * g + 512].rearrange("r (c n) -> r c n", c=4), cstage[0:3, 0:4, :], ['cstage', 'ctmp0', 'ctmp1', 'RT'], [], 'cst')
            dma('sp', cp[seq_b, :, 2048 + 128 * g:2048 + 128 * g + 128], cstage[0:3, 4, :], ['cstage2', 'ctmp0', 'ctmp1', 'RT'], [], 'cst')
            dma('sp', cp[seq_b, :, 2560 + 128 * g:2560 + 128 * g + 128], cstage[0:3, 5, :], ['cstage2', 'ctmp0', 'ctmp1', 'RT'], [], 'cst')

    def ssd_tile(g, tt, first_tile, mask_ap=None, sample_hook=None):
        ssd_prep(g, tt, mask_ap)
        ssd_mid(g, tt, first_tile, sample_hook)
        ssd_back(g, tt)
        ssd_back_b(g, tt)

    def ssd_prep(g, tt, mask_ap=None):
        tsl = slice(tt * 128, (tt + 1) * 128)
        mk = maskb[:, 1, :] if mask_ap is None else mask_ap
        xc, BCt, sfx = cur['xc'], cur['BCt'], cur['sfx']
        op('pe', lambda e: e.matmul(ps32(4)[:, 0:128], lhsT=BCt[:, 0, tsl], rhs=BCt[:, 1, tsl], start=True, stop=True),
           reads=['BCt0' + sfx, 'BCt1' + sfx, 'RT'], writes=['ps4'])
        op('dve', lambda e: e.tensor_tensor(out=cbm, in0=ps32(4)[:, 0:128], in1=mk, op=ALU.mult), reads=['ps4', 'maskb', 'RT'], writes=['cbm'])
        Rv = dm[0:32, :, :]
        op('dve', lambda e: e.tensor_tensor(out=Rv, in0=aT_all[:, tt, :].unsqueeze(1).broadcast_to([32, 8, 128]),
                                            in1=ident_f[0:32, 8 * g:8 * g + 8].unsqueeze(2).broadcast_to([32, 8, 128]), op=ALU.mult),
           reads=['aT%d' % tt, 'cf', 'RT'], writes=['dm'])
        for hf in range(2):
            op('pe', lambda e, hf=hf: e.matmul(ps32(2 + hf), lhsT=ones_f[0:32, :], rhs=dm[0:32, 4 * hf:4 * hf + 4, :], start=True, stop=True),
               reads=['cf', 'dm', 'RT'], writes=[PSN[2 + hf]])
        for hf in range(2):
            op('act', lambda e, hf=hf: e.activation(out=EA[:, 4 * hf:4 * hf + 4, :], in_=ps32(2 + hf).rearrange("p (h l) -> p h l", h=4), func=AF.Exp),
               reads=[PSN[2 + hf], 'RT'], writes=['EA'])
        op('dve', lambda e: e.tensor_tensor(out=CTs, in0=EA, in1=BCt[:, 1, tsl].unsqueeze(1).broadcast_to([128, 8, 128]), op=ALU.mult),
           reads=['EA', 'BCt1' + sfx, 'RT'], writes=['CTs'])
        for hf in range(2):
            op('dve', lambda e, hf=hf: e.tensor_tensor(
                out=dm[:, 4 * hf:4 * hf + 4, :], in0=ps32(2 + hf).rearrange("p (h l) -> p h l", h=4),
                in1=a_all[:, tt, 8 * g + 4 * hf:8 * g + 4 * hf + 4].unsqueeze(2).broadcast_to([128, 4, 128]), op=ALU.subtract),
               reads=[PSN[2 + hf], 'a%d' % tt, 'RT'], writes=['dm'])
        op('dve', lambda e: e.tensor_scalar(out=dm, in0=dm, scalar1=0.0, scalar2=None, op0=ALU.min), reads=['dm', 'RT'], writes=['dm'])
        op('act', lambda e: e.activation(out=Eb2, in_=dm, func=AF.Exp), reads=['dm', 'RT'], writes=['Eb2'])
        op('dve', lambda e: e.tensor_tensor(out=MT, in0=Eb2, in1=cbm.unsqueeze(1).broadcast_to([128, 8, 128]), op=ALU.mult),
           reads=['Eb2', 'cbm', 'RT'], writes=['MT'])
        pT = ps16(5).rearrange("p (c q) -> p c q", c=8)
        for c in range(4):
            op('pe', lambda e, c=c: e.transpose(out=pT[:, c, :], in_=xc[:, c, tsl], identity=identb[:, :]),
               reads=['xc%d' % c + sfx, 'identb', 'RT'], writes=['ps5'])
        op('pe', lambda e: e.transpose(out=pT[:, 4, :], in_=BCt[:, 0, tsl], identity=identb[:, :]), reads=['BCt0' + sfx, 'identb', 'RT'], writes=['ps5'])
        for h2 in range(2):
            hs = slice(64 * h2, 64 * h2 + 64)
            dsl = bass.AP(tensor=dt_all[:, :, :].tensor, offset=dt_all[:, tt, 8 * g + h2:8 * g + h2 + 1].offset, ap=[list(dt_all[:, :, :].ap[0]), [2, 4], [0, 64]])
            wsl_ = bass.AP(tensor=w_all[:, :, :].tensor, offset=w_all[:, tt, 8 * g + h2:8 * g + h2 + 1].offset, ap=[list(w_all[:, :, :].ap[0]), [2, 4], [0, 64]])
            op('dve', lambda e, h2=h2, hs=hs, dsl=dsl: e.tensor_tensor(out=xdtp[:, :, h2, hs], in0=pT[:, 0:4, hs], in1=dsl, op=ALU.mult),
               reads=['ps5', 'dt%d' % tt, 'RT'], writes=['xdtp'])
            op('dve', lambda e, h2=h2, hs=hs, wsl_=wsl_: e.tensor_tensor(out=xw.rearrange("p (c q) -> p c q", c=4)[:, :, hs], in0=pT[:, 0:4, hs], in1=wsl_, op=ALU.mult),
               reads=['ps5', 'w%d_' % tt, 'RT'], writes=['xw'])
        op('act', lambda e: e.activation(out=Btok, in_=pT[:, 4, :], func=AF.Identity), reads=['ps5', 'RT'], writes=['Btok'])

    def ssd_mid(g, tt, first_tile, sample_hook=None):
        tsl = slice(tt * 128, (tt + 1) * 128)
        Y = ps32(6).rearrange("p (c l) -> p c l", c=4)
        for c in range(4):
            seq = []
            for h2 in range(2):
                seq.append(('intra', h2))
                if not first_tile:
                    seq.append(('inter', h2))
            for idx, (kind, h2) in enumerate(seq):
                hl = 2 * c + h2
                if kind == 'intra':
                    op('pe', lambda e, c=c, h2=h2, hl=hl, idx=idx, ns=len(seq): e.matmul(
                        Y[:, c, :], lhsT=xdtp[:, c, h2, :], rhs=MT[:, hl, :], start=(idx == 0 and (sample_hook is None or c == 0)), stop=(idx == ns - 1),
                        skip_group_check=(sample_hook is not None)),
                       reads=['xdtp', 'MT', 'RT'], writes=['ps6'])
                else:
                    op('pe', lambda e, c=c, h2=h2, hl=hl, idx=idx, ns=len(seq): e.matmul(
                        Y[:, c, :], lhsT=hpad[:, 8 * g + hl, :], rhs=CTs[:, hl, :], start=(idx == 0), stop=(idx == ns - 1)),
                       reads=['hpad%d' % g, 'CTs', 'RT'], writes=['ps6'])
        if sample_hook is not None:
            sample_hook(Y)
        if sample_hook is None:
            op('pe', lambda e: e.matmul(ps32(7), lhsT=Btok, rhs=xw, start=True, stop=True), reads=['Btok', 'xw', 'RT'], writes=['ps7'])
        hsv = hstate[:, 512 * g:512 * g + 512]
        if sample_hook is not None:
            pass
        elif first_tile:
            op('act', lambda e: e.activation(out=hsv, in_=ps32(7), func=AF.Identity), reads=['ps7'], writes=['hst%d' % g])
        else:
            op('dve', lambda e: e.tensor_tensor(out=hsv.rearrange("p (h q) -> p h q", h=8), in0=hsv.rearrange("p (h q) -> p h q", h=8),
                                                 in1=cd_all[:, tt, 8 * g:8 * g + 8].unsqueeze(2).broadcast_to([128, 8, 64]), op=ALU.mult),
               reads=['hst%d' % g, 'cd%d' % tt], writes=['hst%d' % g])
            op('dve', lambda e: e.tensor_tensor(out=hsv, in0=hsv, in1=ps32(7), op=ALU.add), reads=['hst%d' % g, 'ps7'], writes=['hst%d' % g])
        hv = hstate[:, 512 * g:512 * g + 512].rearrange("p (c e q) -> p c e q", c=4, e=2)
        hpv = hpad[:, 8 * g:8 * g + 8, :].rearrange("p (c e) q -> p c e q", c=4)
        for h2 in range(2):
            if sample_hook is not None:
                break
            hs = slice(64 * h2, 64 * h2 + 64)
            op('act', lambda e, h2=h2, hs=hs: e.activation(out=hpv[:, :, h2, hs], in_=hv[:, :, h2, :], func=AF.Identity),
               reads=['hst%d' % g], writes=['hpad%d' % g])

    def ssd_back(g, tt):
        tsl = slice(tt * 128, (tt + 1) * 128)
        Y = ps32(6).rearrange("p (c l) -> p c l", c=4)
        ydv = yd.rearrange("p (c l) -> p c l", c=4)
        xc, zs, sfx = cur['xc'], cur['zs'], cur['sfx']
        for c in range(4):
            op('act', lambda e, c=c: e.activation(out=ydv[:, c, :], in_=xc[:, c, tsl], func=AF.Identity, scale=Dcol[:, 4 * g + c:4 * g + c + 1]),
               reads=['xc%d' % c + sfx, 'Dcol', 'RT'], writes=['yd'])
        op('dve', lambda e: e.tensor_tensor(out=ydv, in0=ydv, in1=Y, op=ALU.add), reads=['yd', 'ps6', 'RT'], writes=['yd'])
        op('dve', lambda e: e.tensor_tensor(out=ydv, in0=ydv, in1=zs[:, :, tsl], op=ALU.mult), reads=['yd', 'zs' + sfx, 'RT'], writes=['yd'])
        op('act', lambda e: e.activation(out=sqs, in_=yd, func=AF.Square), reads=['yd', 'RT'], writes=['sqs'])

    def ssd_back_b(g, tt):
        tsl = slice(tt * 128, (tt + 1) * 128)
        ydv = yd.rearrange("p (c l) -> p c l", c=4)
        xc, zs, sfx = cur['xc'], cur['zs'], cur['sfx']
        for c in range(4):
            op('pe', lambda e, c=c: e.matmul(ps32(4)[:, 128:256], lhsT=onesb[:, :], rhs=sqs[:, c * 128:(c + 1) * 128], start=(c == 0), stop=(c == 3)),
               reads=['onesb', 'sqs', 'RT'], writes=['ps4'])
        op('act', lambda e: e.activation(out=rg, in_=ps32(4)[:, 128:256], func=AF.Ln, scale=1.0 / 512, bias=epsc[:, :]), reads=['ps4', 'epsc', 'RT'], writes=['rg'])
        op('act', lambda e: e.activation(out=rg, in_=rg, func=AF.Exp, scale=-0.5), reads=['rg', 'RT'], writes=['rg'])
        op('dve', lambda e: e.tensor_tensor(out=ydv, in0=ydv, in1=rg.unsqueeze(1).broadcast_to([128, 4, 128]), op=ALU.mult), reads=['yd', 'rg', 'RT'], writes=['yd'])
        for c in range(4):
            op('act', lambda e, c=c: e.activation(out=mixT[:, 16 + 4 * g + c, tsl], in_=ydv[:, c, :], func=AF.Identity, scale=snorm[:, 4 * g + c:4 * g + c + 1]),
               reads=['yd', 'snorm', 'RT'], writes=['mixS%d' % tt])

    def ssm_out(g, dst):
        pv = ps32(5).rearrange("p (c n) -> p c n", c=4)
        for c in range(4):
            op('pe', lambda e, c=c: e.transpose(out=pv[:, c, :], in_=hstate[:, 512 * g + 128 * c:512 * g + 128 * c + 128], identity=ident_f),
               reads=['hst%d' % g, 'cf'], writes=['ps5'])
        op('act', lambda e: e.activation(out=hstage, in_=pv, func=AF.Identity), reads=['ps5', 'dm', 'RT'], writes=['dm', 'hstage'])
        dma('sp', dst[8 * g:8 * g + 8].rearrange("(c e) p n -> (e p) c n", e=2), hstage, ['hstage', 'dm', 'RT'], [], 'hst_o')

    def post_tile(tt, xsrc, ydst):
        dma('sp', xt[:, :], xsrc[tt * 128:(tt + 1) * 128, :], [], ['xt'], 'xld')
        op('dve', lambda e: e.tensor_reduce(out=stt[:, 4:5], in_=ssq[:, tt, :], axis=AX.X, op=ALU.add), reads=['ssq%d' % tt], writes=['stt'])
        op('act', lambda e: e.activation(out=stt[:, 5:6], in_=stt[:, 4:5], func=AF.Ln, scale=1.0 / D, bias=epsc[:, :]), reads=['stt', 'epsc'], writes=['stt'])
        op('act', lambda e: e.activation(out=stt[:, 6:7], in_=stt[:, 5:6], func=AF.Exp, scale=-0.5), reads=['stt'], writes=['stt'])
        op('dve', lambda e: e.scalar_tensor_tensor(out=outacc[:, tt, :], in0=outacc[:, tt, :], scalar=stt[:, 6:7], in1=gpost_b[:, :], op0=ALU.mult, op1=ALU.mult),
           reads=['oacc%d' % tt, 'stt', 'gpost_b', 'RT'], writes=['oacc%d' % tt])
        op('dve', lambda e: e.tensor_tensor(out=outacc[:, tt, :], in0=outacc[:, tt, :], in1=xt[:, :], op=ALU.add),
           reads=['oacc%d' % tt, 'xt', 'RT'], writes=['oacc%d' % tt])
        dma('sp', ydst[tt * 128:(tt + 1) * 128, :], outacc[:, tt, :], ['oacc%d' % tt, 'RT'], [], 'yst%d' % tt)


    def sample_attn(j):
        for dup in range(2):
            dma('pool', Kc[:, :, dup, :], ck[:, :, 64 * j:64 * j + 64].rearrange("b s d -> s b d"), ['RT'], ['Kc'], 'sk0')
        dma('pool', Vc_pad[:, :, 0, 0:64], cv[:, :, 64 * j:64 * j + 64].rearrange("b s d -> s b d"), ['RT'], ['Vc_pad'], 'sk1')
        dma('pool', Vc_pad[:, :, 1, 64:128], cv[:, :, 64 * j:64 * j + 64].rearrange("b s d -> s b d"), ['RT'], ['Vc_pad'], 'sk1')
        dma('pool', Vn_pad[0:8, :, 0, 0:64], vscr[:, 64 * j:64 * j + 64].rearrange("(b t) d -> t b d", t=8), ['RT', 'vscr'], ['Vn_pad'], 'sk2')
        dma('pool', Vn_pad[0:8, :, 1, 64:128], vscr[:, 64 * j:64 * j + 64].rearrange("(b t) d -> t b d", t=8), ['RT', 'vscr'], ['Vn_pad'], 'sk2')
        for half in range(2):
            bi = 2 + half
            pv = ps16(bi).rearrange("p (b s) -> p b s", b=8)
            for b8 in range(8):
                b = half * 8 + b8
                op('pe', lambda e, b=b, b8=b8, pv=pv: e.transpose(out=pv[:, b8, :], in_=Kc[:, b, :, :].rearrange("p e d -> p (e d)"), identity=identb[:, :]),
                   reads=['Kc', 'identb', 'RT'], writes=[PSN[bi]])
            op('act', lambda e, half=half, pv=pv: e.activation(out=KTc[:, half * 8:half * 8 + 8, :], in_=pv, func=AF.Identity),
               reads=[PSN[bi], 'RT'], writes=['KTc'])
        for b in range(16):
            for h2 in range(2):
                hs = slice(64 * h2, 64 * h2 + 64)
                qv = QTu[hs, 0, :, 8 * b:8 * b + 8]
                op('pe', lambda e, b=b, h2=h2, hs=hs, qv=qv: e.matmul(ps32(2 + h2)[:, 32 * b:32 * b + 32], lhsT=KTc[hs, b, :], rhs=qv, start=True, stop=True),
                   reads=['KTc', 'QTu', 'RT'], writes=[PSN[2 + h2]])
                op('pe', lambda e, b=b, h2=h2, hs=hs, qv=qv: e.matmul(ps32(4 + h2)[0:8, 32 * b:32 * b + 32], lhsT=KT[hs, j, 128 + 8 * b:128 + 8 * b + 8], rhs=qv, start=True, stop=True),
                   reads=['KT%d' % j, 'QTu', 'RT'], writes=[PSN[4 + h2]])
        for h2 in range(2):
            op('act', lambda e, h2=h2: e.activation(out=Eb[:, 2 * h2, :], in_=ps32(2 + h2), func=AF.Exp, scale=0.125),
               reads=[PSN[2 + h2], 'RT'], writes=['Eb%d' % (2 * h2)])
            op('act', lambda e, h2=h2: e.activation(out=Eb[0:8, 2 * h2 + 1, :], in_=ps32(4 + h2)[0:8, :], func=AF.Exp, scale=0.125),
               reads=[PSN[4 + h2], 'RT'], writes=['Eb%d' % (2 * h2 + 1)])
            op('dve', lambda e, h2=h2: e.tensor_tensor(out=Eb[:, 2 * h2, :].rearrange("p (g t) -> p g t", t=8), in0=Eb[:, 2 * h2, :].rearrange("p (g t) -> p g t", t=8),
                                                       in1=maskb[:, 0, 0:8].unsqueeze(1).broadcast_to([128, 64, 8]), op=ALU.mult),
               reads=['Eb%d' % (2 * h2), 'maskb', 'RT'], writes=['Eb%d' % (2 * h2)])
            op('dve', lambda e, h2=h2: e.tensor_tensor(out=Eb[0:8, 2 * h2 + 1, :].rearrange("p (g t) -> p g t", t=8), in0=Eb[0:8, 2 * h2 + 1, :].rearrange("p (g t) -> p g t", t=8),
                                                        in1=maskb[0:8, 1, 0:8].unsqueeze(1).broadcast_to([8, 64, 8]), op=ALU.mult),
               reads=['Eb%d' % (2 * h2 + 1), 'maskb', 'RT'], writes=['Eb%d' % (2 * h2 + 1)])
        for (bank, vp_, vn_, nm) in ((7, Vc_pad, Vn_pad, 'pv'), (2, None, None, 'den')):
            for b in range(16):
                cs = slice(32 * b, 32 * b + 32)
                idx = 0
                for h2 in range(2):
                    for kb in range(2):
                        if nm == 'pv':
                            lh = vp_[:, b, h2, :] if kb == 0 else vn_[0:8, b, h2, :]
                        else:
                            lh = onespad[:, h2, :] if kb == 0 else onespad[0:8, h2, :]
                        rh = Eb[:, 2 * h2, cs] if kb == 0 else Eb[0:8, 2 * h2 + 1, cs]
                        op('pe', lambda e, bank=bank, cs=cs, lh=lh, rh=rh, idx=idx: e.matmul(ps32(bank)[:, cs], lhsT=lh, rhs=rh, start=(idx == 0), stop=(idx == 3)),
                           reads=['Vc_pad', 'Vn_pad', 'onespad', 'Eb%d' % (2 * h2 + kb), 'RT'], writes=[PSN[bank]])
                        idx += 1
        rdv = rd.rearrange("p (b c t) -> p b c t", b=16, c=4)
        dnv = ps32(2).rearrange("p (b c t) -> p b c t", b=16, c=4)
        for c in range(4):
            op('dve', lambda e, c=c: e.tensor_scalar(out=rdv[:, :, c, :], in0=dnv[:, :, c, :], scalar1=Ecol[:, 4 * j + c:4 * j + c + 1], scalar2=None, op0=ALU.add),
               reads=['ps2', 'Ecol', 'RT'], writes=['rd'])
        op('act', lambda e: e.activation(out=rd, in_=rd, func=AF.Ln), reads=['rd', 'RT'], writes=['rd'])
        op('act', lambda e: e.activation(out=rd, in_=rd, func=AF.Exp, scale=-1.0), reads=['rd', 'RT'], writes=['rd'])
        op('dve', lambda e: e.tensor_tensor(out=o32, in0=ps32(7), in1=rd, op=ALU.mult), reads=['ps7', 'rd', 'RT'], writes=['o32'])
        mv = mixT[:, 4 * j:4 * j + 4, 0:128]
        op('dve', lambda e: e.tensor_tensor(out=mv.rearrange("p c (b t) -> p b c t", t=8), in0=o32.rearrange("p (b c t) -> p b c t", b=16, c=4),
                                             in1=gs[:, :, 0:128].rearrange("p c (b t) -> p b c t", t=8), op=ALU.mult),
           reads=['o32', 'gs', 'RT'], writes=['mixA0'])
        op('act', lambda e: e.activation(out=sqa.rearrange("p (c q) -> p c q", c=4), in_=mv, func=AF.Square), reads=['mixA0', 'RT'], writes=['sqa'])
        for c in range(4):
            op('pe', lambda e, c=c: e.matmul(ps32(6)[:, 0:128], lhsT=onesb[:, :], rhs=sqa[:, c * 128:(c + 1) * 128], start=(j == 0 and c == 0), stop=(j == 3 and c == 3), skip_group_check=True),
               reads=['onesb', 'sqa', 'RT'], writes=['ps6'])

    def sample_conv(g):
        n = 128
        scv = sconv.rearrange("b k c -> (b k) c")
        dma('sp', cst_in[0:48, 0:4, :], scv[:, 512 * g:512 * g + 512].rearrange("r (c n) -> r c n", c=4), ['RT'], ['cst_in'], 'sk3')
        dma('sp', cst_in[0:48, 4, :], scv[:, 2048 + 128 * g:2048 + 128 * g + 128], ['RT'], ['cst_in'], 'sk3')
        dma('sp', cst_in[0:48, 5, :], scv[:, 2560 + 128 * g:2560 + 128 * g + 128], ['RT'], ['cst_in'], 'sk3')
        ph = ps32(5)[:, 0:288].rearrange("p (c r) -> p c r", c=6)
        for ci in range(6):
            op('pe', lambda e, ci=ci: e.transpose(out=ph[:, ci, :], in_=cst_in[0:48, ci, :], identity=ident_f[0:48, 0:48]),
               reads=['cst_in', 'cf', 'RT'], writes=['ps5'])
        op('act', lambda e: e.activation(out=xpre_s[:, :, :, 0:3], in_=ph.rearrange("p c (b k) -> p c b k", k=3), func=AF.Identity),
           reads=['ps5', 'RT'], writes=['xpre%d' % ci for ci in range(6)])
        for ci in range(6):
            ch = (4 * g + ci) if ci < 4 else (16 + g if ci == 4 else 20 + g)
            acc = ctmp[:, ci % 2, 0:n].rearrange("p (b t) -> p b t", t=8)
            an = 'ctmp%d' % (ci % 2)
            op('dve', lambda e, ci=ci, ch=ch, acc=acc: e.tensor_scalar(out=acc, in0=xpre_s[:, ci, :, 0:8], scalar1=convw[:, 0, ch:ch + 1], scalar2=None, op0=ALU.mult),
               reads=['xpre%d' % ci, 'convw', 'RT'], writes=[an])
            for k in range(1, 4):
                op('dve', lambda e, ci=ci, ch=ch, k=k, acc=acc: e.scalar_tensor_tensor(
                    out=acc, in0=xpre_s[:, ci, :, k:k + 8], scalar=convw[:, k, ch:ch + 1], in1=acc, op0=ALU.mult, op1=ALU.add),
                   reads=['xpre%d' % ci, 'convw', an, 'RT'], writes=[an])
            dst = xc[:, ci, 0:n] if ci < 4 else BCt[:, ci - 4, 0:n]
            dn = ('xc%d' % ci) if ci < 4 else ('BCt%d' % (ci - 4))
            op('act', lambda e, ch=ch, ci=ci, dst=dst: e.activation(out=dst, in_=ctmp[:, ci % 2, 0:n], func=AF.Silu, bias=convb[:, ch:ch + 1]),
               reads=[an, 'convb', 'RT'], writes=[dn])
        op('act', lambda e: e.activation(out=cst_o.rearrange("p c (b k) -> p c b k", k=3), in_=xpre_s[:, :, :, 8:11], func=AF.Identity),
           reads=['xpre%d' % ci for ci in range(6)] + ['RT'], writes=['cst_o'])
        p5 = ps32(5)[0:48, :].rearrange("p (c n) -> p c n", c=4)
        p7 = ps32(7)[0:48, 0:256].rearrange("p (c n) -> p c n", c=2)
        for ci in range(6):
            tgt = p5[:, ci, :] if ci < 4 else p7[:, ci - 4, :]
            bn = 'ps5' if ci < 4 else 'ps7'
            op('pe', lambda e, ci=ci, tgt=tgt: e.transpose(out=tgt, in_=cst_o[:, ci, :], identity=ident_f), reads=['cst_o', 'cf', 'RT'], writes=[bn])
        op('act', lambda e: e.activation(out=cstage[0:48, 0:4, :], in_=p5, func=AF.Identity), reads=['ps5', 'ctmp0', 'ctmp1', 'RT'], writes=['ctmp0', 'ctmp1', 'cstage'])
        op('act', lambda e: e.activation(out=cstage[0:48, 4:6, :], in_=p7, func=AF.Identity), reads=['ps7', 'RT'], writes=['cstage2'])
        cso = csm.rearrange("b k c -> (b k) c")
        dma('sp', cso[:, 512 * g:512 * g + 512].rearrange("r (c n) -> r c n", c=4), cstage[0:48, 0:4, :], ['cstage', 'ctmp0', 'ctmp1', 'RT'], [], 'so1')
        dma('sp', cso[:, 2048 + 128 * g:2048 + 128 * g + 128], cstage[0:48, 4, :], ['cstage2', 'ctmp0', 'ctmp1', 'RT'], [], 'so1')
        dma('sp', cso[:, 2560 + 128 * g:2560 + 128 * g + 128], cstage[0:48, 5, :], ['cstage2', 'ctmp0', 'ctmp1', 'RT'], [], 'so1')

    def sample_ssd_hook(g):
        def hook(Y):
            op('dve', lambda e: e.tensor_tensor(out=Bm, in0=Btok.unsqueeze(1).broadcast_to([128, 16, 128]),
                                                in1=cf[:, 640:656].unsqueeze(2).broadcast_to([128, 16, 128]), op=ALU.mult),
               reads=['Btok', 'cf', 'RT'], writes=['Bm'])
            aTl = aT_all[:, 0, :].rearrange("p (b t) -> p b t", t=8)[:, :, 7]
            for c in range(4):
                cg = 4 * g + c
                R2 = Rm[0:32, 0:32].rearrange("p (e b) -> p e b", e=2)
                op('dve', lambda e, cg=cg, R2=R2: e.tensor_tensor(out=R2, in0=aTl.unsqueeze(1).broadcast_to([32, 2, 16]),
                                                                  in1=ident_f[0:32, 2 * cg:2 * cg + 2].unsqueeze(2).broadcast_to([32, 2, 16]), op=ALU.mult),
                   reads=['aT0', 'cf', 'RT'], writes=['Rm'])
                for h2 in range(2):
                    op('pe', lambda e, h2=h2, R2=R2: e.matmul(ps32(4)[:, 256:272], lhsT=onespad_f[:, h2, :], rhs=R2[:, h2, :], start=(h2 == 0), stop=(h2 == 1)),
                       reads=['onespad_f', 'Rm', 'RT'], writes=['ps4'])
                op('act', lambda e: e.activation(out=Rm[:, 64:80], in_=ps32(4)[:, 256:272], func=AF.Exp), reads=['ps4', 'RT'], writes=['cdT'])
                for hb in range(2):
                    bs = slice(8 * hb, 8 * hb + 8)
                    src = sssm[bs, 2 * cg:2 * cg + 2].rearrange("b e p n -> (e p) b n")
                    dma('sp', h0f, src, ['RT'], ['h0f'], 'sk4')
                    dma('pool', h0b, src, ['RT'], ['h0b'], 'sk5')
                    pv = ps16(3).rearrange("p (b q) -> p b q", b=8)
                    for b8 in range(8):
                        op('pe', lambda e, b8=b8, pv=pv: e.transpose(out=pv[:, b8, :], in_=h0b[:, b8, :], identity=identb[:, :]),
                           reads=['h0b', 'identb', 'RT'], writes=['ps3'])
                    op('act', lambda e, pv=pv: e.activation(out=hps[:, :, 0, 0:64], in_=pv[:, :, 0:64], func=AF.Identity), reads=['ps3', 'RT'], writes=['hps'])
                    op('dve', lambda e, pv=pv: e.tensor_copy(out=hps[:, :, 1, 64:128], in_=pv[:, :, 64:128]), reads=['ps3', 'RT'], writes=['hps'])
                    for b8 in range(8):
                        b = 8 * hb + b8
                        for h2 in range(2):
                            op('pe', lambda e, c=c, b=b, b8=b8, h2=h2: e.matmul(Y[:, c, 8 * b:8 * b + 8], lhsT=hps[:, b8, h2, :], rhs=CTs[:, 2 * c + h2, 8 * b:8 * b + 8],
                                                                             start=False, stop=(h2 == 1), skip_group_check=True),
                               reads=['hps', 'CTs', 'RT'], writes=['ps6'])
                    for q in range(2):
                        op('pe', lambda e, c=c, hb=hb, q=q: e.matmul(ps32(q), lhsT=xw[:, c * 128:(c + 1) * 128],
                                                                    rhs=Bm[:, 8 * hb + 4 * q:8 * hb + 4 * q + 4, :], start=True, stop=True),
                           reads=['xw', 'Bm', 'RT'], writes=[PSN[q]])
                    op('dve', lambda e, hb=hb: e.tensor_tensor(out=h0f, in0=h0f, in1=Rm[:, 64 + 8 * hb:64 + 8 * hb + 8].unsqueeze(2).broadcast_to([128, 8, 128]), op=ALU.mult),
                       reads=['h0f', 'cdT', 'RT'], writes=['h0f'])
                    for q in range(2):
                        op('dve', lambda e, q=q: e.tensor_tensor(out=h0f[:, 4 * q:4 * q + 4, :], in0=h0f[:, 4 * q:4 * q + 4, :],
                                                                 in1=ps32(q).rearrange("p (b n) -> p b n", b=4), op=ALU.add),
                           reads=['h0f', PSN[q], 'RT'], writes=['h0f'])
                    dma('sp', hsm[bs, 2 * cg:2 * cg + 2].rearrange("b e p n -> (e p) b n"), h0f, ['h0f', 'RT'], [], 'so2')
        return hook

    def add_sample_group():
        NT = 1
        n = 128
        gi = len(grp_x)
        grp_x.append((xsm, NT))

        def ld_kd(slot):
            for j in range(4):
                for dup in range(2):
                    wload((C_K + 64 * j, 64), slot, j * 128 + dup * 64)

        def cp_kd(slot):
            fence('A')
            cur.update(QTu=QTu, gs=gs, zs=zs, xc=xc, BCt=BCt, sfx='')
            op('dve', lambda e: e.memset(Vc_pad, 0.0), reads=['RT'], writes=['Vc_pad'])
            op('dve', lambda e: e.memset(Vn_pad, 0.0), reads=['RT'], writes=['Vn_pad'])
            for b in range(16):
                dma('sp', ksm[b, 0:120, :], ck[b, 8:128, :], [], [], 'so3')
                dma('sp', vsm[b, 0:120, :], cv[b, 8:128, :], [], [], 'so3')
            if gi == 0:
                phase0(xsm, NT)
            for j in range(4):
                inproj_chunk(slot, j, n, lambda pa, bn, j=j: op('act', lambda e: e.activation(out=KT[:, j, 128:128 + n], in_=pa, func=AF.Identity),
                                                              reads=[bn], writes=['KT%d' % j]))
        tasks.append((ld_kd, cp_kd))

        def ld_vdt(slot):
            wload((C_V, 256), slot, 0)
            wload((C_DT, 32), slot, 256)

        def cp_vdt(slot):
            phase1_tile(slot, 0, NT, True, False, 0, tri=cf[:, 384:512], blk=cf[:, 512:640], sample=True)
        tasks.append((ld_vdt, cp_vdt))

        def ld_kt(slot):
            wload((C_K, 256), slot, 0)

        def cp_kt(slot):
            w = wsl[slot]
            bi = pbank[0] % 2
            pbank[0] += 1
            for k in range(16):
                op('pe', lambda e, k=k: e.matmul(ps32(bi)[:, 0:256], lhsT=hT[:, k, 0:128], rhs=w[:, k, 0:256], start=(k == 0), stop=(k == 15)),
                   reads=['w%d' % slot, 'hT'], writes=[PSN[bi]])
            op('dve', lambda e: e.tensor_copy(out=kstage, in_=ps32(bi)[:, 0:256]), reads=[PSN[bi], 'RT'], writes=['kstage'])
            for b in range(16):
                dma('sp', ksm[b, 120:128, :], kstage[8 * b:8 * b + 8, :], ['kstage', 'RT'], [], 'so4')
        tasks.append((ld_kt, cp_kt))

        for j in range(4):
            def ld_q(slot, j=j):
                wload((C_Q + 512 * j, 512), slot)

            def cp_q(slot, j=j):
                for c in range(4):
                    inproj_chunk(slot, c, n, lambda pa, bn, c=c: op('act', lambda e: e.activation(out=QTu[:, 0, c, :], in_=pa, func=AF.Identity),
                                                                    reads=[bn, 'RT'], writes=['QTu']))
            tasks.append((ld_q, cp_q))

            def ld_ga(slot, j=j):
                wload((C_GA + 512 * j, 512), slot)

            def cp_ga(slot, j=j):
                for c in range(4):
                    inproj_chunk(slot, c, n, lambda pa, bn, c=c: op('act', lambda e: e.activation(out=gs[:, c, 0:n], in_=pa, func=AF.Silu),
                                                                    reads=[bn, 'RT'], writes=['gs']))
                sample_attn(j)
                if j == 3:
                    attn_finish(NT)
            tasks.append((ld_ga, cp_ga))

        for g in range(4):
            def ld_z(slot, g=g):
                wload((C_Z + 512 * g, 512), slot)

            def cp_z(slot, g=g):
                if g == 0:
                    fence('S')
                    op('dve', lambda e: e.memset(xdtp, 0.0), reads=['RT'], writes=['xdtp'])
                    op('dve', lambda e: e.memset(hps, 0.0), reads=['RT'], writes=['hps'])
                for c in range(4):
                    inproj_chunk(slot, c, n, lambda pa, bn, c=c: op('act', lambda e: e.activation(out=zs[:, c, 0:n], in_=pa, func=AF.Silu),
                                                                    reads=[bn, 'RT'], writes=['zs']))
            tasks.append((ld_z, cp_z))

            def ld_x(slot, g=g):
                wload((C_X + 512 * g, 512), slot)

            def cp_x(slot, g=g):
                for c in range(4):
                    inproj_chunk(slot, c, n, lambda pa, bn, c=c: op('act', lambda e: e.activation(
                        out=xpre_s[:, c, :, 3:11], in_=pa.rearrange("p (b t) -> p b t", t=8), func=AF.Identity), reads=[bn, 'RT'], writes=['xpre%d' % c]))
            tasks.append((ld_x, cp_x))

            def ld_bc(slot, g=g):
                wload((C_B + 128 * g, 128), slot, 0)
                wload((C_C + 128 * g, 128), slot, 128)

            def cp_bc(slot, g=g):
                for c in range(2):
                    inproj_chunk(slot, c, n, lambda pa, bn, c=c: op('act', lambda e: e.activation(
                        out=xpre_s[:, 4 + c, :, 3:11], in_=pa.rearrange("p (b t) -> p b t", t=8), func=AF.Identity), reads=[bn, 'RT'], writes=['xpre%d' % (4 + c)]))
                sample_conv(g)
                ssd_tile(g, 0, True, mask_ap=triub_blk[:, :], sample_hook=sample_ssd_hook(g))
            tasks.append((ld_bc, cp_bc))

        for cb in range(4):
            for kh in range(2):
                def ld_o(slot, cb=cb, kh=kh):
                    wload_rows(kh * 2048, cb * 512, slot)

                def cp_o(slot, cb=cb, kh=kh):
                    if cb == 0 and kh == 0:
                        fence('O')
                        dma('sp', gpost_b, bcast_dram(norm_post, D), ['RT'], ['gpost_b'], 'c10')
                    w = wsl[slot]
                    for k in range(16):
                        kk = kh * 16 + k
                        mn = ['mixA0'] if kk < 16 else ['mixS0']
                        op('pe', lambda e, k=k, kk=kk: e.matmul(ps32(2), lhsT=mixT[:, kk, 0:128], rhs=w[:, k, :], start=(kk == 0), stop=(kk == 31)),
                           reads=['w%d' % slot] + mn, writes=[PSN[2]])
                    if kh == 1:
                        op('act', lambda e: e.activation(out=outacc[:, 0, cb * 512:(cb + 1) * 512], in_=ps32(2), func=AF.Identity),
                           reads=[PSN[2], 'RT'], writes=['oacc0'])
                        op('act', lambda e: e.activation(out=ojunk, in_=ps32(2), func=AF.Square, accum_out=ssq[:, 0, cb:cb + 1]),
                           reads=[PSN[2], 'RT'], writes=['ojunk', 'ssq0'])
                        if cb == 3:
                            post_tile(0, xsm, ysm)
                tasks.append((ld_o, cp_o))

    def add_prompt_group(b, G):
        NT = 4
        n = 512
        first_group = (G == 0)
        last_group = (G == NG - 1)
        xsrc = xp[b * SEQ + G * 512: b * SEQ + (G + 1) * 512, :]
        ydst = yp[b * SEQ + G * 512: b * SEQ + (G + 1) * 512, :]
        gi = len(grp_x)
        grp_x.append((xsrc, NT))

        def ld_kd(slot):
            w = wsl[slot]
            for j in range(4):
                for dup in range(2):
                    wload((C_K + 64 * j, 64), slot, j * 128 + dup * 64)

        def cp_kd(slot):
            fence('A')
            if first_group:
                op('dve', lambda e: e.memset(hist[:, :, :], 0.0), reads=[], writes=['hist%d' % c for c in range(24)])
            if gi == 0:
                phase0(xsrc, NT)
            for j in range(4):
                inproj_chunk(slot, j, n, lambda pa, bn, j=j: op('act', lambda e: e.activation(out=KT[:, j, 128:128 + n], in_=pa, func=AF.Identity),
                                                              reads=[bn], writes=['KT%d' % j]))
        tasks.append((ld_kd, cp_kd))

        def ld_vdt(slot):
            wload((C_V, 256), slot, 0)
            wload((C_DT, 32), slot, 256)

        def cp_vdt(slot):
            for tt in range(NT):
                phase1_tile(slot, tt, NT, first_group and tt == 0, last_group and tt == NT - 1, b)
        tasks.append((ld_vdt, cp_vdt))
        if last_group:
            def ld_kt(slot):
                wload((C_K, 256), slot, 0)

            def cp_kt(slot):
                w = wsl[slot]
                tt = NT - 1
                bi = pbank[0] % 2
                pbank[0] += 1
                for k in range(16):
                    op('pe', lambda e, k=k: e.matmul(ps32(bi)[:, 0:256], lhsT=hT[:, k, tt * 128:(tt + 1) * 128], rhs=w[:, k, 0:256], start=(k == 0), stop=(k == 15)),
                       reads=['w%d' % slot, 'hT'], writes=[PSN[bi]])
                op('dve', lambda e: e.tensor_copy(out=kstage, in_=ps32(bi)[:, 0:256]), reads=[PSN[bi], 'RT'], writes=['kstage'])
                dma('sp', kp[b], kstage, ['kstage', 'RT'], [], 'kst')
            tasks.append((ld_kt, cp_kt))

        for j in range(4):
            def ld_q(slot, j=j):
                wload((C_Q + 512 * j, 512), slot)

            def cp_q(slot, j=j):
                drain(None, older_than=j - 1)
                cur.update(QTu=[QTu, QTuB][j % 2], gs=[gs, gsB][j % 2], sfx='ab'[j % 2])
                Q_, sfx = cur['QTu'], cur['sfx']
                for c in range(4):
                    inproj_chunk(slot, c, n, lambda pa, bn, c=c: op('act', lambda e: e.activation(
                        out=Q_[:, :, c, :], in_=pa.rearrange("p (t q) -> p t q", t=4), func=AF.Identity), reads=[bn, 'RT'], writes=['QTu' + sfx]))
            tasks.append((ld_q, cp_q))

            def ld_ga(slot, j=j):
                wload((C_GA + 512 * j, 512), slot)

            def cp_ga(slot, j=j):
                G_, sfx = cur['gs'], cur['sfx']
                for c in range(4):
                    inproj_chunk(slot, c, n, lambda pa, bn, c=c: op('act', lambda e: e.activation(out=G_[:, c, :], in_=pa, func=AF.Identity),
                                                                    reads=[bn, 'RT'], writes=['gs' + sfx]))
                op('act', lambda e: e.activation(out=G_[:, :, :], in_=G_[:, :, :], func=AF.Silu), reads=['gs' + sfx, 'RT'], writes=['gs' + sfx])
                attn_unit(j, NT, first_group)
                if j == 3:
                    drain()
                    attn_finish(NT)
                    op('act', lambda e: e.activation(out=KT[:, :, 0:128], in_=KT[:, :, 512:640], func=AF.Identity), reads=['KT%d' % jj for jj in range(4)],
                       writes=['KT%d' % jj for jj in range(4)])
                    op('dve', lambda e: e.tensor_copy(out=Vpad[:, 0, :, :, :], in_=Vpad[:, 4, :, :, :]), reads=['Vpad4'], writes=['Vpad0'])
            tasks.append((ld_ga, cp_ga))

        for g in range(4):
            def ld_z(slot, g=g):
                wload((C_Z + 512 * g, 512), slot)

            def cp_z(slot, g=g):
                if g == 0:
                    fence('S')
                    op('dve', lambda e: e.memset(xdtp, 0.0), reads=['RT'], writes=['xdtp'])
                drain(None, older_than=10 + g - 1)
                cur.update(zs=[zs, zsB][g % 2], xc=[xc, xcB][g % 2], BCt=[BCt, BCtB][g % 2], sfx='ab'[g % 2])
                Z_, sfx = cur['zs'], cur['sfx']
                for c in range(4):
                    inproj_chunk(slot, c, n, lambda pa, bn, c=c: op('act', lambda e: e.activation(out=Z_[:, c, :], in_=pa, func=AF.Identity),
                                                                    reads=[bn, 'RT'], writes=['zs' + sfx]))
                op('act', lambda e: e.activation(out=Z_[:, :, :], in_=Z_[:, :, :], func=AF.Silu), reads=['zs' + sfx, 'RT'], writes=['zs' + sfx])
            tasks.append((ld_z, cp_z))

            def ld_x(slot, g=g):
                wload((C_X + 512 * g, 512), slot)

            def cp_x(slot, g=g):
                for c in range(4):
                    eng = 'act' if c % 2 == 0 else 'dve'
                    if eng == 'act':
                        inproj_chunk(slot, c, n, lambda pa, bn, c=c: op('act', lambda e: e.activation(out=xpre[:, c, 3:3 + n], in_=pa, func=AF.Identity),
                                                                        reads=[bn, 'RT'], writes=['xpre%d' % c]))
                    else:
                        inproj_chunk(slot, c, n, lambda pa, bn, c=c: op('dve', lambda e: e.tensor_copy(out=xpre[:, c, 3:3 + n], in_=pa),
                                                                        reads=[bn, 'RT'], writes=['xpre%d' % c]))
            tasks.append((ld_x, cp_x))

            def ld_bc(slot, g=g):
                wload((C_B + 128 * g, 128), slot, 0)
                wload((C_C + 128 * g, 128), slot, 128)

            def cp_bc(slot, g=g):
                for c in range(2):
                    inproj_chunk(slot, c, n, lambda pa, bn, c=c: op('act', lambda e: e.activation(out=xpre[:, 4 + c, 3:3 + n], in_=pa, func=AF.Identity),
                                                                    reads=[bn, 'RT'], writes=['xpre%d' % (4 + c)]))
                conv_group(g, NT, b, last_group)
                tg = 10 + g
                defer(tg, lambda: ssd_prep(g, 0))
                for tt in range(NT + 1):
                    def piece(tt=tt):
                        if tt >= 1:
                            ssd_back_b(g, tt - 1)
                        if tt < NT:
                            ssd_mid(g, tt, first_group and tt == 0)
                            if tt + 1 < NT:
                                ssd_prep(g, tt + 1)
                            ssd_back(g, tt)
                    defer(tg, piece)
                if last_group:
                    defer(tg, lambda: ssm_out(g, hp[b]))
            tasks.append((ld_bc, cp_bc))

        for cb in range(4):
            for kh in range(2):
                def ld_o(slot, cb=cb, kh=kh):
                    wload_rows(kh * 2048, cb * 512, slot)

                def cp_o(slot, cb=cb, kh=kh):
                    if cb == 0 and kh == 0:
                        fence('O')
                        dma('sp', gpost_b, bcast_dram(norm_post, D), ['RT'], ['gpost_b'], 'c10')
                    w = wsl[slot]
                    for k in range(16):
                        kk = kh * 16 + k
                        for tt in range(NT):
                            mn = ['mixA%d' % tt] if kk < 16 else ['mixS%d' % tt]
                            op('pe', lambda e, k=k, kk=kk, tt=tt: e.matmul(ps32(2 + tt), lhsT=mixT[:, kk, tt * 128:(tt + 1) * 128], rhs=w[:, k, :],
                                                                            start=(kk == 0), stop=(kk == 31)),
                               reads=['w%d' % slot] + mn, writes=[PSN[2 + tt]])
                    if gi + 1 < len(grp_x):
                        nx, nnt = grp_x[gi + 1]
                        ib = cb * 2 + kh
                        if ib < nnt:
                            phase0(nx, nnt, tiles=[ib])
                    if kh == 1:
                        for tt in range(NT):
                            op('act', lambda e, tt=tt: e.activation(out=outacc[:, tt, cb * 512:(cb + 1) * 512], in_=ps32(2 + tt), func=AF.Identity),
                               reads=[PSN[2 + tt], 'RT'], writes=['oacc%d' % tt])
                            op('act', lambda e, tt=tt: e.activation(out=ojunk, in_=ps32(2 + tt), func=AF.Square, accum_out=ssq[:, tt, cb:cb + 1]),
                               reads=[PSN[2 + tt], 'RT'], writes=['ojunk', 'ssq%d' % tt])
                        if cb == 3:
                            for tt in range(NT):
                                post_tile(tt, xsrc, ydst)
                tasks.append((ld_o, cp_o))

    for b in range(NP):
        for G in range(NG):
            add_prompt_group(b, G)
    if SAMPLE:
        add_sample_group()

    import os as _os
    if _os.environ.get("KSTOP"):
        del tasks[int(_os.environ["KSTOP"]):]
    if _os.environ.get("KSKIP"):
        del tasks[:int(_os.environ["KSKIP"])]
    nt = len(tasks)
    for i in range(min(NSLOT, nt)):
        tasks[i][0](i % NSLOT)
    for i in range(nt):
        tasks[i][1](i % NSLOT)
        if i + NSLOT < nt:
            tasks[i + NSLOT][0]((i + NSLOT) % NSLOT)
    outsems = ['yst0', 'yst1', 'yst2', 'yst3', 'vst', 'kst', 'cst', 'hst_o', 'so0', 'so1', 'so2', 'so3', 'so4']
    for s in outsems:
        if any(o['dma'] == s for o in P.ops):
            last = max(i for i, o in enumerate(P.ops) if o['dma'] == s)
            P.ops.append(dict(eng='sp', fn=lambda e: e.nop(), deps={last}, dma=None, sig=False, sem=None, val=0))
    P.emit(nc, st)
    st.close()
    return nc, P


_CACHE = {}


def kernel(x_prompt, x_sample, cache_k, cache_v, state_conv, state_ssm, norm_pre, w_in, attn_sink, attn_norm,
           conv_w, conv_b, dt_bias, a_log, d_skip, ssm_norm, w_out, norm_post):
    f = lambda a: np.ascontiguousarray(np.asarray(a, dtype=np.float32))
    B, S, _ = x_prompt.shape
    NPc = B // NCORES
    DB = x_sample.shape[0] // NCORES
    assert DB == 16 and x_sample.shape[1] == 8
    key = (NPc, S)
    if key not in _CACHE:
        _CACHE[key] = build(NPc, S, SAMPLE=True)[0]
    nc = _CACHE[key]
    cst = host_consts()
    shared = dict(w_in=f(w_in[0]), w_out=f(w_out[0]), norm_pre=f(norm_pre[0]), attn_sink=f(attn_sink[0]), attn_norm=f(attn_norm[0]),
                  conv_w=f(conv_w[0]), conv_b=f(conv_b[0]), dt_bias=f(dt_bias[0]), a_log=f(a_log[0]), d_skip=f(d_skip[0]),
                  ssm_norm=f(ssm_norm[0]), norm_post=f(norm_post[0]), consts=cst)
    in_maps = []
    for c in range(NCORES):
        m = dict(shared)
        m["xp"] = f(x_prompt[c * NPc:(c + 1) * NPc]).reshape(NPc * S, D)
        sl = slice(c * DB, (c + 1) * DB)
        m["xsm"] = f(x_sample[sl]).reshape(DB * 8, D)
        m["ck"] = f(cache_k[0, sl]).reshape(DB, 128, 256)
        m["cv"] = f(cache_v[0, sl]).reshape(DB, 128, 256)
        m["sconv"] = f(state_conv[0, sl])
        m["sssm"] = f(state_ssm[0, sl])
        in_maps.append(m)
    res = run_bass_kernel_spmd(nc, in_maps, core_ids=list(range(NCORES)))
    R = res.results
    cat = lambda k: np.concatenate([np.asarray(r[k]) for r in R], axis=0)
    y_prompt = cat("yp").reshape(B, S, D)
    y_sample = cat("ysm").reshape(NCORES * DB, 8, D)
    k_prompt = cat("kp").reshape(1, B, 128, 4, 64)
    v_prompt = cat("vp").reshape(1, B, 128, 4, 64)
    conv_prompt = cat("cp").reshape(1, B, 3, 3072)
    ssm_prompt = cat("hp").reshape(1, B, 32, 64, 128)
    k_sample = cat("ksm").reshape(1, NCORES * DB, 128, 4, 64)
    v_sample = cat("vsm").reshape(1, NCORES * DB, 128, 4, 64)
    conv_sample = cat("csm").reshape(1, NCORES * DB, 3, 3072)
    ssm_sample = cat("hsm").reshape(1, NCORES * DB, 32, 64, 128)
    return (y_prompt.astype(np.float32), y_sample.astype(np.float32), k_prompt, v_prompt, conv_prompt, ssm_prompt,
            k_sample, v_sample, conv_sample, ssm_sample)
```
